# Optimizing a Trainium2 kernel written in Bass

```python
import jax, jax.numpy as jnp
from jax import lax
import numpy as np

D_MODEL = 1024
BATCH = 32
SEQ = 256
DEPTH = 1
DEC_BATCH = 4
DEC_SEQ = 4096
PAST_LEN = 512

GRID_W = 64
H_M = 4
DH_M = 128
MLSTM_W = H_M * DH_M
CHUNK = 128
M_INIT = -1e30
H_A = 8
NOPE = 64
ROPE_DIM = 32
V_DIM = 64
Q_LORA = 384
KV_LORA = 256
AX_DIM = ROPE_DIM // 2
ROPE_BASE = 10000.0
Q_BLOCK = 128
D_FF = 4 * D_MODEL
N_MOD = 6
EPS = 1e-6
IN_SIZES = (MLSTM_W, MLSTM_W, MLSTM_W, MLSTM_W, 4 * H_M, Q_LORA, KV_LORA, ROPE_DIM, 2 * D_MODEL)
IN_COLS = 4 * MLSTM_W + 4 * H_M + Q_LORA + KV_LORA + ROPE_DIM + 2 * D_MODEL

kernel_name = "hybrid_mlstm_mla_prefix_diffusion_step"


def rmsnorm(x, w):
    xf = x.astype(jnp.float32)
    y = xf * lax.rsqrt(jnp.mean(xf * xf, axis=-1, keepdims=True) + EPS)
    return (y * w.astype(jnp.float32)).astype(x.dtype)


def axial_rope(n_tokens):
    rows = n_tokens // GRID_W
    row = jnp.broadcast_to(jnp.arange(rows, dtype=jnp.float32)[:, None], (rows, GRID_W)).reshape(-1)
    col = jnp.broadcast_to(jnp.arange(GRID_W, dtype=jnp.float32)[None, :], (rows, GRID_W)).reshape(-1)
    inv = ROPE_BASE ** (-jnp.arange(0, AX_DIM, 2, dtype=jnp.float32) / AX_DIM)
    ang = jnp.concatenate([row[:, None] * inv, col[:, None] * inv], axis=-1)
    return jnp.cos(ang), jnp.sin(ang)


def apply_rope(x, cos, sin):
    x1, x2 = x[..., 0::2], x[..., 1::2]
    cos, sin = cos.astype(x.dtype), sin.astype(x.dtype)
    return jnp.stack([x1 * cos - x2 * sin, x1 * sin + x2 * cos], axis=-1).reshape(x.shape)


def blocked_attention(q, k, v):
    B, T, H, Dq = q.shape
    nb = T // Q_BLOCK
    scale = Dq ** -0.5
    kf, vf = k.astype(jnp.float32), v.astype(jnp.float32)
    qb = jnp.moveaxis(q.reshape(B, nb, Q_BLOCK, H, Dq), 1, 0)

    def one_block(qi):
        s = jnp.einsum('bqhd,bkhd->bhqk', qi.astype(jnp.float32), kf) * scale
        p = jax.nn.softmax(s, axis=-1)
        return jnp.einsum('bhqk,bkhd->bqhd', p, vf)

    o = lax.map(one_block, qb)
    return jnp.moveaxis(o, 0, 1).reshape(B, T, H, v.shape[-1]).astype(q.dtype)


def _to_chunks(a, nc):
    B, T, H = a.shape[:3]
    a = a.astype(jnp.float32).reshape((B, nc, CHUNK, H) + a.shape[3:])
    return jnp.moveaxis(a, (1, 3), (0, 2))


def mlstm_chunkwise(q, k, v, log_i, log_f, C0, n0, m0):
    B, T, H, Dh = q.shape
    nc = T // CHUNK
    xs = (_to_chunks(q, nc), _to_chunks(k, nc), _to_chunks(v, nc), _to_chunks(log_i, nc), _to_chunks(log_f, nc))
    causal = jnp.tril(jnp.ones((CHUNK, CHUNK), dtype=bool))

    def step(carry, inp):
        C, n, m = carry
        qc, kc, vc, ic, fc = inp
        b = jnp.cumsum(fc, axis=-1)
        dmat = jnp.where(causal, b[..., :, None] - b[..., None, :] + ic[..., None, :], -jnp.inf)
        inter = b + m[..., None]
        m_t = jnp.maximum(inter, jnp.max(dmat, axis=-1))
        w = jnp.exp(dmat - m_t[..., None])
        s_inter = jnp.exp(inter - m_t)
        qk = jnp.einsum('bhtd,bhsd->bhts', qc, kc) * w
        num = s_inter[..., None] * jnp.einsum('bhtd,bhde->bhte', qc, C) + jnp.einsum('bhts,bhse->bhte', qk, vc)
        den = s_inter * jnp.einsum('bhtd,bhd->bht', qc, n) + jnp.sum(qk, axis=-1)
        h = num / jnp.maximum(jnp.abs(den), jnp.exp(-m_t))[..., None]
        bL = b[..., -1]
        g = bL[..., None] - b + ic
        m_new = jnp.maximum(bL + m, jnp.max(g, axis=-1))
        decay = jnp.exp(bL + m - m_new)
        wk = jnp.exp(g - m_new[..., None])[..., None] * kc
        C_new = decay[..., None, None] * C + jnp.einsum('bhsd,bhse->bhde', wk, vc)
        n_new = decay[..., None] * n + jnp.sum(wk, axis=-2)
        return (C_new, n_new, m_new), h

    carry0 = (C0.astype(jnp.float32), n0.astype(jnp.float32), m0.astype(jnp.float32))
    (C, n, m), h = lax.scan(step, carry0, xs)
    h = jnp.moveaxis(h, (0, 2), (1, 3)).reshape(B, T, H, Dh)
    return h, C, n, m


def mlstm_bidirectional(q, k, v, log_i, log_f, C0, n0, m0):
    rev = lambda a: jnp.flip(a, axis=1)
    h_f, Cf, nf, mf = mlstm_chunkwise(q, k, v, log_i[:, :, 0], log_f[:, :, 0], C0[:, 0], n0[:, 0], m0[:, 0])
    h_b, Cb, nb, mb = mlstm_chunkwise(rev(q), rev(k), rev(v), rev(log_i[:, :, 1]), rev(log_f[:, :, 1]),
                                      C0[:, 1], n0[:, 1], m0[:, 1])
    h = h_f + rev(h_b)
    return h, jnp.stack([Cf, Cb], axis=1), jnp.stack([nf, nb], axis=1), jnp.stack([mf, mb], axis=1)


def token_mixer(h, rope, ctx_ckv, ctx_krope, state0, w_in, gate_b, q_norm, kv_norm, w_uq, w_ukv,
                w_mla_o, head_norm, w_mlstm_o, w_out):
    B, T, _ = h.shape
    idx = np.cumsum(IN_SIZES)[:-1].tolist()
    mq, mk, mv, mo, mg, cq, ckv, krope, merge = jnp.split(h @ w_in, idx, axis=-1)

    ckv = rmsnorm(ckv, kv_norm)
    q = jnp.einsum('btr,rhd->bthd', rmsnorm(cq, q_norm), w_uq)
    q_nope, q_rope = q[..., :NOPE], q[..., NOPE:]
    if rope is not None:
        cos, sin = rope
        q_rope = apply_rope(q_rope, cos[:, None, :], sin[:, None, :])
        keys_ckv = jnp.concatenate([ckv, ctx_ckv.astype(ckv.dtype)], axis=1)
        keys_rope = jnp.concatenate([apply_rope(krope, cos, sin), ctx_krope.astype(krope.dtype)], axis=1)
    else:
        keys_ckv, keys_rope = ckv, krope
    S = keys_ckv.shape[1]
    kv = jnp.einsum('bsr,rhd->bshd', keys_ckv, w_ukv)
    k = jnp.concatenate([kv[..., :NOPE], jnp.broadcast_to(keys_rope[:, :, None, :], (B, S, H_A, ROPE_DIM))], axis=-1)
    attn = blocked_attention(jnp.concatenate([q_nope, q_rope], axis=-1), k, kv[..., NOPE:])
    y_attn = attn.reshape(B, T, H_A * V_DIM) @ w_mla_o

    g = mg.reshape(B, T, 2, 2, H_M).astype(jnp.float32) + gate_b.astype(jnp.float32)
    log_i, log_f = g[:, :, :, 0], jax.nn.log_sigmoid(g[:, :, :, 1])
    heads = lambda a: a.reshape(B, T, H_M, DH_M)
    hm, C, n, m = mlstm_bidirectional(heads(mq) * DH_M ** -0.5, heads(mk), heads(mv), log_i, log_f, *state0)
    hm = rmsnorm(hm, head_norm).astype(h.dtype).reshape(B, T, MLSTM_W)
    y_mlstm = (jax.nn.sigmoid(mo) * hm) @ w_mlstm_o

    g_a, g_b = jnp.split(jax.nn.sigmoid(merge), 2, axis=-1)
    out = (g_a * y_mlstm + g_b * y_attn) @ w_out
    return out, ckv, krope, C, n, m


def trunk_layer(x, mod, rope, ctx_ckv, ctx_krope, state0, pre1, post1, pre2, post2, w_mlp1, w_mlp2, mix_w):
    shift1, scale1, gate1, shift2, scale2, gate2 = jnp.split(mod, N_MOD, axis=-1)
    h = rmsnorm(x, pre1) * (1 + scale1) + shift1
    mix, ckv, krope, C, n, m = token_mixer(h, rope, ctx_ckv, ctx_krope, state0, *mix_w)
    x = x + gate1 * rmsnorm(mix, post1)
    h = rmsnorm(x, pre2) * (1 + scale2) + shift2
    ff = jnp.square(jax.nn.relu(h @ w_mlp1)) @ w_mlp2
    x = x + gate2 * rmsnorm(ff, post2)
    return x, ckv, krope, C, n, m


def setup_inputs(seed: int = 0) -> dict:
    key = jax.random.key(seed)
    ks = jax.random.split(key, 27)
    f32 = jnp.float32

    def nrm(i, shape, scale=1.0):
        return jax.random.normal(ks[i], shape, f32) * scale

    def gain(i, shape):
        return 1.0 + 0.05 * nrm(i, shape)

    L = DEPTH
    gate_base = jnp.array([0.0, 3.0], f32)[None, None, :, None]
    return {
        "x_prompt": nrm(0, (BATCH, SEQ, D_MODEL)),
        "x_sample": nrm(1, (DEC_BATCH, DEC_SEQ, D_MODEL)),
        "cache_mla_ckv": nrm(2, (DEC_BATCH, L, PAST_LEN, KV_LORA)),
        "cache_mla_krope": nrm(3, (DEC_BATCH, L, PAST_LEN, ROPE_DIM)),
        "state_mlstm_C": nrm(4, (DEC_BATCH, L, 2, H_M, DH_M, DH_M), 0.1),
        "state_mlstm_n": nrm(5, (DEC_BATCH, L, 2, H_M, DH_M), 0.1),
        "state_mlstm_m": nrm(6, (DEC_BATCH, L, 2, H_M)),
        "c": nrm(7, (DEC_BATCH, D_MODEL)),
        "c_ctx": nrm(8, (D_MODEL,)),
        "w_ada": nrm(9, (L, D_MODEL, N_MOD * D_MODEL), D_MODEL ** -0.5),
        "b_ada": nrm(10, (L, N_MOD * D_MODEL), 0.02),
        "norm_pre1": gain(11, (L, D_MODEL)),
        "norm_post1": gain(12, (L, D_MODEL)),
        "norm_pre2": gain(13, (L, D_MODEL)),
        "norm_post2": gain(14, (L, D_MODEL)),
        "w_in": nrm(15, (L, D_MODEL, IN_COLS), D_MODEL ** -0.5),
        "mlstm_gate_b": gate_base + 0.1 * nrm(16, (L, 2, 2, H_M)),
        "mla_q_norm": gain(17, (L, Q_LORA)),
        "mla_kv_norm": gain(18, (L, KV_LORA)),
        "w_uq": nrm(19, (L, Q_LORA, H_A, NOPE + ROPE_DIM), Q_LORA ** -0.5),
        "w_ukv": nrm(20, (L, KV_LORA, H_A, NOPE + V_DIM), KV_LORA ** -0.5),
        "w_mla_o": nrm(21, (L, H_A * V_DIM, D_MODEL), (H_A * V_DIM) ** -0.5),
        "mlstm_head_norm": gain(22, (L, H_M, DH_M)),
        "w_mlstm_o": nrm(23, (L, MLSTM_W, D_MODEL), MLSTM_W ** -0.5),
        "w_out": nrm(24, (L, D_MODEL, D_MODEL), D_MODEL ** -0.5),
        "w_mlp1": nrm(25, (L, D_MODEL, D_FF), D_MODEL ** -0.5),
        "w_mlp2": nrm(26, (L, D_FF, D_MODEL), D_FF ** -0.5),
    }


def reference(x_prompt, x_sample, cache_mla_ckv, cache_mla_krope, state_mlstm_C, state_mlstm_n, state_mlstm_m,
              c, c_ctx, w_ada, b_ada, norm_pre1, norm_post1, norm_pre2, norm_post2, w_in, mlstm_gate_b,
              mla_q_norm, mla_kv_norm, w_uq, w_ukv, w_mla_o, mlstm_head_norm, w_mlstm_o, w_out, w_mlp1, w_mlp2):
    f32 = jnp.float32
    Bp = x_prompt.shape[0]
    rope = axial_rope(x_sample.shape[1])
    zero_state = (jnp.zeros((Bp, 2, H_M, DH_M, DH_M), f32), jnp.zeros((Bp, 2, H_M, DH_M), f32),
                  jnp.full((Bp, 2, H_M), M_INIT, f32))
    xp, xs = x_prompt, x_sample
    new_ckv, new_krope, new_C, new_n, new_m = [], [], [], [], []
    for l in range(DEPTH):
        mix_w = (w_in[l], mlstm_gate_b[l], mla_q_norm[l], mla_kv_norm[l], w_uq[l], w_ukv[l], w_mla_o[l],
                 mlstm_head_norm[l], w_mlstm_o[l], w_out[l])
        norms = (norm_pre1[l], norm_post1[l], norm_pre2[l], norm_post2[l])
        mod_ctx = (jax.nn.silu(c_ctx) @ w_ada[l] + b_ada[l])[None, None, :]
        mod_lat = (jax.nn.silu(c) @ w_ada[l] + b_ada[l])[:, None, :]
        xp, ckv, krope, C, n, m = trunk_layer(xp, mod_ctx, None, None, None, zero_state, *norms,
                                              w_mlp1[l], w_mlp2[l], mix_w)
        new_ckv.append(ckv)
        new_krope.append(krope)
        new_C.append(C.astype(x_prompt.dtype))
        new_n.append(n.astype(x_prompt.dtype))
        new_m.append(m.astype(x_prompt.dtype))
        state_l = (state_mlstm_C[:, l], state_mlstm_n[:, l], state_mlstm_m[:, l])
        xs = trunk_layer(xs, mod_lat, rope, cache_mla_ckv[:, l], cache_mla_krope[:, l], state_l, *norms,
                         w_mlp1[l], w_mlp2[l], mix_w)[0]
    return (xp, xs, jnp.stack(new_ckv, axis=1), jnp.stack(new_krope, axis=1), jnp.stack(new_C, axis=1),
            jnp.stack(new_n, axis=1), jnp.stack(new_m, axis=1))
```

```python
import numpy as np
from contextlib import ExitStack
import concourse.bass as bass
import concourse.mybir as mybir
from concourse.bass_utils import run_bass_kernel_spmd

F32 = mybir.dt.float32
BF16 = mybir.dt.bfloat16
AF = mybir.ActivationFunctionType
ALU = mybir.AluOpType
AX = mybir.AxisListType

D = 1024
NTP = 1024
NSO = 2048
NST = 4096
PAST = 512
EPS = 1e-6
IN_COLS = 4784
C_Q, C_K, C_V, C_O, C_G, C_CQ, C_CKV, C_KR, C_MG = 0, 512, 1024, 1536, 2048, 2064, 2448, 2704, 2736
SCQ = 96.0 ** -0.5


class Res:
    __slots__ = ("w", "r", "rd")

    def __init__(self):
        self.w = []
        self.r = {}
        self.rd = []


class Buf:
    def __init__(self, t, excl=False):
        self.t = t
        self.res = Res()
        self.excl = excl

    def __getitem__(self, k):
        return self.t[k]


def tiled(buf, n):
    buf.tiles = [Buf(buf.t) for _ in range(n)]
    return buf


class KB:
    ENG = ("pe", "act", "dve", "pool", "sp")

    def __init__(self, nc, es):
        self.nc = nc
        self.es = es
        self.e = {"pe": nc.tensor, "act": nc.scalar, "dve": nc.vector, "pool": nc.gpsimd, "sp": nc.sync}
        self.sem = {k: es.enter_context(nc.semaphore("s_" + k)) for k in self.ENG}
        self.cnt = {k: 0 for k in self.ENG}
        self.known = {k: {} for k in self.ENG}
        self.dsl = {}
        for q, n in (("sp", 24), ("pool", 12), ("act", 4)):
            self.dsl[q] = [{"sem": es.enter_context(nc.semaphore("d_%s%d" % (q, i))), "tot": 0, "i": i}
                           for i in range(n)]
        self.dptr = {"sp": 0, "pool": 0, "act": 0}
        self.n_inst = 0
        self.snap = {}

    def need(self, e, ev):
        kind, key, n = ev
        kk = (kind, key)
        if self.known[e].get(kk, 0) >= n:
            return None
        if kind == "eng":
            if key == e and e in ("pe", "sp"):
                return None
            sem = self.sem[key]
        else:
            q, i = key
            sem = self.dsl[q][i]["sem"]
        self.known[e][kk] = n
        sn = self.snap.get(ev)
        if sn:
            kn = self.known[e]
            for k2, v2 in sn.items():
                if kn.get(k2, 0) < v2:
                    kn[k2] = v2
        return (sem, n)

    def wait(self, e, ev):
        w = self.need(e, ev)
        if w is not None:
            self.e[e].wait_ge(w[0], w[1])
            self.n_inst += 1

    def waits_attach(self, e, evs):
        ws = []
        for ev in evs:
            w = self.need(e, ev)
            if w is not None:
                ws.append(w)
        best = {}
        for sem, n in ws:
            key = id(sem)
            if key not in best or best[key][1] < n:
                best[key] = (sem, n)
        ws = list(best.values())
        for sem, n in ws[:-1]:
            self.e[e].wait_ge(sem, n)
            self.n_inst += 1
        return ws[-1] if ws else None

    def _deps(self, R, W, e=None):
        deps = []
        for b in R:
            deps += b.res.w
            if b.excl:
                deps += [("eng", k, v) for k, v in b.res.r.items() if k != e]
        for b in W:
            rs = b.res
            deps += rs.w
            deps += [("eng", k, v) for k, v in rs.r.items()]
            deps += rs.rd
        return deps

    def _mark(self, ev, R, W):
        for b in R:
            if ev[0] == "eng":
                b.res.r[ev[1]] = ev[2]
            else:
                b.res.rd.append(ev)
        for b in W:
            b.res.w = [ev]
            b.res.r = {}
            b.res.rd = []

    def op(self, e, fn, R=(), W=(), sig=True):
        last = self.waits_attach(e, self._deps(R, W, e))
        n0 = self.nc.n_instructions()
        ins = fn()
        assert self.nc.n_instructions() - n0 == 1, "multi-instruction op"
        if last is not None:
            ins._wait_ge(last[0], last[1])
        self.n_inst += 1
        if sig:
            self.cnt[e] += 1
            ins.then_inc(self.sem[e], 1)
            ev = ("eng", e, self.cnt[e])
            sn = dict(self.known[e])
            sn[("eng", e)] = self.cnt[e]
            self.snap[ev] = sn
        else:
            ev = ("eng", e, self.cnt[e] + 1)
        self._mark(ev, R, W)
        return ev

    def dma(self, q, out, in_, R=(), W=()):
        sl = self.dsl[q][self.dptr[q]]
        self.dptr[q] = (self.dptr[q] + 1) % len(self.dsl[q])
        deps = self._deps(R, W)
        if sl["tot"] > 0:
            deps.append(("dma", (q, sl["i"]), sl["tot"]))
        last = self.waits_attach(q, deps)
        sl["tot"] += 16
        ins = self.e[q].dma_start(out=out, in_=in_)
        if last is not None:
            ins._wait_ge(last[0], last[1])
        ins.then_inc(sl["sem"], 16)
        self.n_inst += 1
        ev = ("dma", (q, sl["i"]), sl["tot"])
        self.snap[ev] = dict(self.known[q])
        self._mark(ev, R, W)
        return ev

    def barrier(self):
        evs = [("eng", k, self.cnt[k]) for k in ("pe", "act", "dve", "pool") if self.cnt[k] > 0]
        for q in self.dsl:
            for sl in self.dsl[q]:
                if sl["tot"] > 0:
                    evs.append(("dma", (q, sl["i"]), sl["tot"]))
        for e in self.ENG:
            for ev in evs:
                self.wait(e, ev)


class Ring:
    def __init__(self, bufs):
        self.b = bufs
        self.i = 0

    def next(self):
        b = self.b[self.i]
        self.i = (self.i + 1) % len(self.b)
        return b


class Prog:
    def __init__(self, do_prompt=True, do_sample=True, stop=None):
        self.do_prompt = do_prompt
        self.do_sample = do_sample
        self.stop = stop

    def ck(self, name):
        if self.stop == name:
            self.stopped = True
        return getattr(self, "stopped", False)

    def alloc(self, es, name, shape, dt):
        self._an = getattr(self, "_an", 0) + 1
        t = es.enter_context(self.nc.sbuf_tensor("sb%d_%s" % (self._an, name), list(shape), dt))
        return Buf(t)

    def dram_in(self, name, shape, dt=F32):
        return self.nc.dram_tensor(name, list(shape), dt, kind="ExternalInput").ap()

    def dram_out(self, name, shape, dt=F32):
        return self.nc.dram_tensor(name, list(shape), dt, kind="ExternalOutput").ap()

    def V(self, fn, R=(), W=(), e="dve"):
        return self.k.op(e, fn, R, W)

    def psb(self):
        return self.psr.next()

    def evac_engine(self):
        self._ev = 1 - getattr(self, "_ev", 0)
        return "act" if self._ev else "dve"

    def copy(self, e, out, in_, R, W, scale=None):
        nc = self.nc
        if e == "act":
            if scale is None:
                return self.k.op("act", lambda: nc.scalar.copy(out=out, in_=in_), R, W)
            return self.k.op("act", lambda: nc.scalar.activation(out=out, in_=in_, func=AF.Copy, scale=scale), R, W)
        eng = nc.vector if e == "dve" else nc.gpsimd
        if scale is None:
            return self.k.op(e, lambda: eng.tensor_copy(out=out, in_=in_), R, W)
        return self.k.op(e, lambda: eng.tensor_scalar(out=out, in0=in_, scalar1=scale, scalar2=None, op0=ALU.mult), R, W)

    def rstd_from_ss(self, ss, col, n, R):
        nc = self.nc
        self.k.op("act", lambda: nc.scalar.activation(out=ss.t[:, col + 1:col + 2], in_=ss.t[:, col:col + 1],
                                                      func=AF.Ln, scale=1.0 / n, bias=EPS), [ss], [ss])
        self.k.op("act", lambda: nc.scalar.activation(out=ss.t[:, col + 2:col + 3], in_=ss.t[:, col + 1:col + 2],
                                                      func=AF.Exp, scale=-0.5), [ss], [ss])
        return ss.t[:, col + 2:col + 3]

    def wstream(self, reqs, depth=2):
        self._wreqs = list(reqs)
        self._wissued = 0
        self._wdepth = depth
        self._wtiles = {}

    def wget(self, i):
        while self._wissued < len(self._wreqs) and self._wissued <= i + self._wdepth - 1:
            j = self._wissued
            ap, kc, ncols = self._wreqs[j]
            slot = self.wring.next()
            view = slot.t[:, 0:kc * ncols].rearrange("p (k n) -> p k n", k=kc)
            self.k.dma("pool", view, ap.rearrange("(k p) n -> p k n", p=128), [], [slot])
            self._wtiles[j] = (view, slot)
            self._wissued += 1
        return self._wtiles.pop(i) if i >= 0 else None

    def build(self):
        nc = bass.Bass("TRN2", target_bir_lowering=False)
        self.nc = nc
        di = self.dram_in
        self.d = dict(
            xs=di("xs", [NST, D]), xp=di("xp", [NTP, D]),
            cckv=di("cckv", [PAST, 256]), ckr=di("ckr", [PAST, 32]),
            C0=di("C0", [2, 4, 128, 128]), n0=di("n0", [2, 4, 128]), m0=di("m0", [8]),
            cvec=di("cvec", [2, D]), w_ada=di("w_ada", [D, 6 * D]),
            vecs=di("vecs", [68, 128]),
            b_ada=di("b_ada", [6 * D]), post1=di("post1", [D]), post2=di("post2", [D]),
            w_in=di("w_in", [D, IN_COLS]), gate_b=di("gate_b", [16]),
            q_norm=di("q_norm", [384]), kv_norm=di("kv_norm", [256]),
            w_uq=di("w_uq", [384, 768]), w_uq_sw=di("w_uq_sw", [384, 8, 32]),
            w_uk=di("w_uk", [256, 8, 64]), w_uv=di("w_uv", [256, 512]),
            w_mla_o=di("w_mla_o", [512, D]), w_mlstm_o=di("w_mlstm_o", [512, D]),
            w_out=di("w_out", [D, D]), w_mlp1=di("w_mlp1", [D, 4 * D]), w_mlp2=di("w_mlp2", [4 * D, D]),
            ident=di("ident", [128, 128]), trif=di("trif", [128, 128]), trib=di("trib", [128, 128]),
            esel=di("esel", [32, 128]), qtab=di("qtab", [64, NSO]), kcos=di("kcos", [NST, 16]), ksin=di("ksin", [NST, 16]),
        )
        do = self.dram_out
        self.o = dict(
            ys=do("ys", [NSO, D]), yp=do("yp", [NTP, D]), o_ckv=do("o_ckv", [NTP, 256]), o_kr=do("o_kr", [NTP, 32]),
            o_C=do("o_C", [4, 8, 128, 128]), o_n=do("o_n", [4, 8, 128]), o_m=do("o_m", [4, 8]),
        )
        with ExitStack() as es:
            self.k = KB(nc, es)
            self.psum = [Buf(es.enter_context(nc.psum_tensor("pb%d" % i, [128, 512], F32)), excl=True) for i in range(8)]
            self.psr = Ring(self.psum)
            self.persistent(es)
            self.phase0()
            if not self.ck("p0"):
                if self.do_prompt:
                    self.run_group("p")
                if self.do_sample and not self.ck("-"):
                    self.run_group("s")
            self.k.barrier()
        return nc

    def persistent(self, es):
        nc, k, A = self.nc, self.k, self.alloc
        d = self.d
        self.ident_f = A(es, "ident_f", [128, 128], F32)
        self.ident_b = A(es, "ident_b", [128, 128], BF16)
        self.trif = A(es, "trif", [128, 128], F32)
        self.trib = A(es, "trib", [128, 128], F32)
        self.ones_f = A(es, "ones_f", [128, 128], F32)
        self.ones_b = A(es, "ones_b", [128, 128], BF16)
        self.modAB = A(es, "modAB", [128, 4, 8, 2], F32)
        self.hnT = A(es, "hnT", [128, 4], F32)
        self.sT = A(es, "sT", [128, 8, 2], F32)
        self.qn_row = A(es, "qn_row", [128, 384], F32)
        self.kvn_row = A(es, "kvn_row", [128, 256], F32)
        self.gb_row = A(es, "gb_row", [128, 16], F32)
        self.smr = Ring([A(es, "sm%d" % i, [128, 16], F32) for i in range(16)])
        self.Gs = (A(es, "Gs1", [128, D], F32), A(es, "Gs2", [128, D], F32))
        k.dma("sp", self.ident_f.t[:], d["ident"], [], [self.ident_f])
        k.dma("sp", self.trif.t[:], d["trif"], [], [self.trif])
        k.dma("sp", self.trib.t[:], d["trib"], [], [self.trib])
        k.dma("sp", self.qn_row.t[:], d["q_norm"].partition_broadcast(128), [], [self.qn_row])
        k.dma("sp", self.kvn_row.t[:], d["kv_norm"].partition_broadcast(128), [], [self.kvn_row])
        k.dma("sp", self.gb_row.t[:], d["gate_b"].partition_broadcast(128), [], [self.gb_row])
        self.V(lambda: nc.vector.tensor_copy(out=self.ident_b.t[:], in_=self.ident_f.t[:]), [self.ident_f], [self.ident_b])
        self.V(lambda: nc.vector.memset(self.ones_f.t[:], 1.0), [], [self.ones_f])
        self.V(lambda: nc.vector.memset(self.ones_b.t[:], 1.0), [], [self.ones_b])

    def phase0(self):
        nc, k, d = self.nc, self.k, self.d
        with ExitStack() as es:
            A = self.alloc
            self.wring = Ring([A(es, "w0_%d" % i, [128, 8 * 512], BF16) for i in range(4)])
            cv = A(es, "cv", [2, D], F32)
            ce = A(es, "ce", [2, D], F32)
            vt = A(es, "vt", [68, 128], F32)
            vT = A(es, "vT", [128, 68], F32)
            sTb = A(es, "sTb", [128, 8, 2], BF16)
            modT = A(es, "modT", [128, 4, 8, 2], F32)
            k.dma("sp", cv.t[:], d["cvec"], [], [cv])
            k.dma("sp", vt.t[:], d["vecs"], [], [vt])
            k.op("act", lambda: nc.scalar.activation(out=ce.t[:], in_=cv.t[:], func=AF.Exp, scale=-1.0), [cv], [ce])
            self.V(lambda: nc.vector.tensor_scalar(out=ce.t[:], in0=ce.t[:], scalar1=1.0, scalar2=None, op0=ALU.add), [ce], [ce])
            self.V(lambda: nc.vector.reciprocal(out=ce.t[:], in_=ce.t[:]), [ce], [ce])
            self.V(lambda: nc.vector.tensor_tensor(out=cv.t[:], in0=cv.t[:], in1=ce.t[:], op=ALU.mult), [cv, ce], [cv])
            pb = self.psb()
            for kc in range(8):
                k.op("pe", lambda kc=kc: nc.tensor.transpose(out=pb.t[:, kc * 2:kc * 2 + 2], in_=cv.t[:, kc * 128:(kc + 1) * 128],
                                                             identity=self.ident_f.t[0:2, 0:2]), [cv, self.ident_f], [pb])
            self.V(lambda: nc.vector.tensor_copy(out=self.sT.t[:], in_=pb.t[:, 0:16].rearrange("p (k g) -> p k g", g=2)), [pb], [self.sT])
            self.V(lambda: nc.vector.tensor_copy(out=sTb.t[:], in_=self.sT.t[:]), [self.sT], [sTb])
            pb2 = self.psb()
            k.op("pe", lambda: nc.tensor.transpose(out=pb2.t[:, 0:68], in_=vt.t[:], identity=self.ident_f.t[0:68, 0:68]),
                 [vt, self.ident_f], [pb2])
            self.V(lambda: nc.vector.tensor_copy(out=vT.t[:], in_=pb2.t[:, 0:68]), [pb2], [vT])
            self.V(lambda: nc.vector.tensor_copy(out=self.hnT.t[:], in_=vT.t[:, 64:68]), [vT], [self.hnT])
            mods = (0, 1, 3, 4)
            self.wstream([(d["w_ada"][:, mi * D + j * 512: mi * D + (j + 1) * 512], 8, 512) for mi in mods for j in range(2)])
            pm = self.psb()
            wi = 0
            for a, mi in enumerate(mods):
                for j in range(2):
                    W, ws = self.wget(wi)
                    wi += 1
                    for c4 in range(4):
                        fc = j * 4 + c4
                        for kc in range(8):
                            k.op("pe", lambda kc=kc, c4=c4, a=a, fc=fc, W=W: nc.tensor.matmul(
                                pm.t[:, (a * 8 + fc) * 2:(a * 8 + fc) * 2 + 2], lhsT=W[:, kc, c4 * 128:(c4 + 1) * 128],
                                rhs=sTb.t[:, kc, :], start=(kc == 0), stop=(kc == 7)), [ws, sTb], [pm], sig=(kc == 7))
            for a, mi in enumerate(mods):
                self.V(lambda a=a, mi=mi: nc.vector.tensor_tensor(
                    out=modT.t[:, a], in0=pm.t[:, a * 16:(a + 1) * 16].rearrange("p (f g) -> p f g", g=2),
                    in1=vT.t[:, mi * 8:(mi + 1) * 8].unsqueeze(2).broadcast_to([128, 8, 2]), op=ALU.add), [pm, vT], [modT])
            for blk, (a_shift, a_scale, prow) in enumerate(((0, 1, 48), (2, 3, 56))):
                self.V(lambda a_scale=a_scale, prow=prow, blk=blk: nc.vector.scalar_tensor_tensor(
                    out=self.modAB.t[:, 2 * blk], in0=modT.t[:, a_scale], scalar=1.0,
                    in1=vT.t[:, prow:prow + 8].unsqueeze(2).broadcast_to([128, 8, 2]), op0=ALU.add, op1=ALU.mult),
                    [modT, vT], [self.modAB])
                self.V(lambda a_shift=a_shift, blk=blk: nc.vector.tensor_copy(out=self.modAB.t[:, 2 * blk + 1], in_=modT.t[:, a_shift]),
                       [modT], [self.modAB])
            k.barrier()

    def norm_transpose(self, x_ap, x_buf, gi, blk, hT, col0):
        nc, k = self.nc, self.k
        ss = self.smr.next()
        xnb = self.xnb.next()
        k.op("act", lambda: nc.scalar.activation(out=xnb.t[:], in_=x_ap, func=AF.Square, accum_out=ss.t[:, 0:1]),
             [x_buf], [xnb, ss])
        rstd = self.rstd_from_ss(ss, 0, D, [])
        k.op("act", lambda: nc.scalar.activation(out=xnb.t[:], in_=x_ap, func=AF.Copy, scale=rstd), [x_buf, ss], [xnb])
        pb = self.psb()
        pv = pb.t[:].bitcast(BF16).rearrange("p (f t) -> p f t", t=128)
        for fc in range(8):
            k.op("pe", lambda fc=fc: nc.tensor.transpose(out=pv[:, fc, :], in_=xnb.t[:, fc * 128:(fc + 1) * 128],
                                                         identity=self.ident_b.t[:]), [xnb, self.ident_b], [pb], sig=(fc == 7))
        A = self.modAB.t[:, 2 * blk]
        B = self.modAB.t[:, 2 * blk + 1]
        evt = self.evt.next()
        self.V(lambda: nc.vector.tensor_tensor(out=evt.t[:], in0=pv, in1=A[:, :, gi:gi + 1].broadcast_to([128, 8, 128]), op=ALU.mult),
               [pb, self.modAB], [evt])
        self.V(lambda: nc.vector.tensor_tensor(out=hT.t[:, :, col0:col0 + 128], in0=evt.t[:], in1=B[:, :, gi:gi + 1].broadcast_to([128, 8, 128]),
                                               op=ALU.add), [evt, self.modAB], [hT.tiles[col0 // 128]])

    def mm_group(self, out_ap, pb, pairs, R):
        nc, k = self.nc, self.k
        n = len(pairs)
        ev = None
        for i, (l, r) in enumerate(pairs):
            ev = k.op("pe", lambda l=l, r=r, i=i: nc.tensor.matmul(out_ap, lhsT=l, rhs=r, start=(i == 0), stop=(i == n - 1)),
                      R, [pb], sig=(i == n - 1))
        return ev

    def run_group(self, g):
        nc, k, d, o = self.nc, self.k, self.d, self.o
        A = self.alloc
        samp = (g == "s")
        gi = 0 if samp else 1
        n_own = 16 if samp else 8
        nt = 32 if samp else 8
        nkey = 36 if samp else 8
        x_own = d["xs"] if samp else d["xp"]
        with ExitStack() as gs:
            zmT = A(gs, g + "zmT", [128, 4, n_own * 128], BF16)
            zaT = A(gs, g + "zaT", [128, 4, n_own * 128], BF16)
            with ExitStack() as ms:
                cqnT = A(ms, g + "cqnT", [128, 3, n_own * 128], BF16)
                ckvT = A(ms, g + "ckvT", [128, 2, nkey * 128], BF16)
                krT = A(ms, g + "krT", [32, nkey * 128], BF16)
                with ExitStack() as ls:
                    nseq = 1 if samp else 4
                    qT = A(ls, g + "qT", [128, 4, n_own * 128], BF16)
                    kT = A(ls, g + "kT", [128, 4, n_own * 128], BF16)
                    vaug = A(ls, g + "vaug", [128, n_own, 4, 129], BF16)
                    graw = A(ls, g + "graw", [128, nt, 16], F32)
                    gsc = {nm: A(ls, g + nm, [128, nt, 8], F32)
                           for nm in ("lf", "b", "bL", "a", "amax", "ref", "dm", "mall", "u", "dec", "E")}
                    self.gtmp = Ring([A(ls, g + "gtmp%d" % i, [128, 256], F32) for i in range(3)])
                    S = A(ls, g + "S", [128, nseq, 8, 129], F32)
                    Sres = [[Buf(S.t[:, s_, j_, :]) for j_ in range(8)] for s_ in range(nseq)]
                    mcur0 = A(ls, g + "mcur0", [128, nseq, 8], F32)
                    self.V(lambda: nc.vector.memset(vaug.t[:, :, :, 128:129], 1.0), [], [vaug])
                    for nm in gsc:
                        self.V(lambda nm=nm: nc.vector.memset(gsc[nm].t[:], 0.0), [], [gsc[nm]])
                    if samp:
                        for j_ in range(8):
                            k.dma("sp", Sres[0][j_].t[:, 0:128], d["C0"][j_ // 4, j_ % 4], [], [Sres[0][j_]])
                        n8 = self.smr.next()
                        n8v = self.gtmp.next()
                        k.dma("sp", n8v.t[0:8, 0:128], d["n0"].rearrange("r h d -> (r h) d"), [], [n8v])
                        pbn = self.psb()
                        k.op("pe", lambda: nc.tensor.transpose(out=pbn.t[:, 0:8], in_=n8v.t[0:8, 0:128], identity=self.ident_f.t[0:8, 0:8]),
                             [n8v, self.ident_f], [pbn])
                        self.V(lambda: nc.vector.tensor_copy(out=S.t[:, 0, :, 128:129], in_=pbn.t[:, 0:8].unsqueeze(2)), [pbn], Sres[0])
                        k.dma("sp", mcur0.t[:, 0, :], d["m0"].partition_broadcast(128), [], [mcur0])
                    else:
                        self.V(lambda: nc.vector.memset(S.t[:], 0.0), [], [b_ for r_ in Sres for b_ in r_])
                        self.V(lambda: nc.vector.memset(mcur0.t[:], -1e30), [], [mcur0])
                    mcur = None
                    with ExitStack() as ps_:
                        self.wring = Ring([A(ps_, g + "w1_%d" % i, [128, 8 * 512], BF16) for i in range(3)])
                        self.xring = Ring([A(ps_, g + "xt%d" % i, [128, D], F32) for i in range(2)])
                        self.xnb = Ring([A(ps_, g + "xnb%d" % i, [128, D], BF16) for i in range(2)])
                        self.junk = Ring([A(ps_, g + "junk%d" % i, [128, 512], BF16) for i in range(1)])
                        self.evt = Ring([A(ps_, g + "evt%d" % i, [128, 8, 128], F32) for i in range(1 if samp else 2)])
                        if samp:
                            hT = tiled(Buf(zmT.t[:].rearrange("p a b -> p (a b)").rearrange("p (f t) -> p f t", f=8)), 8)
                        else:
                            hT = tiled(A(ps_, g + "hT", [128, 8, 1024], BF16), 8)
                        bufs = dict(qT=qT, kT=kT, vaug=vaug, graw=graw, cqnT=cqnT, ckvT=ckvT, krT=krT, hT=hT)
                        bufs["tokw"] = Ring([A(ps_, g + "tokw%d" % i, [128, 288], F32) for i in range(2)])
                        bufs["tokb"] = Ring([A(ps_, g + "tokb%d" % i, [128, 384], BF16) for i in range(3)])
                        bufs["rot"] = Ring([A(ps_, g + "rot%d" % i, [128, 6, 16], F32) for i in range(2)])
                        bufs["cs"] = Ring([A(ps_, g + "cs%d" % i, [128, 2, 16], F32) for i in range(3)])
                        if samp:
                            zflat = zaT.t[:].rearrange("p a b -> p (a b)")
                            ktok = Buf(zflat[:, 0:4096].rearrange("p (i c) -> p i c", c=512))
                            vo = Buf(zflat[:, 4096:8192].rearrange("p (i c) -> p i c", c=512))
                            wk_r = Ring([A(ps_, g + "wkr%d" % i, [128, 128], BF16) for i in range(3)])
                            bufs["ktok"] = ktok
                            bufs["vo"] = vo
                            mview = mcur0.t[:, 0:1, 4:8]
                            mb = [mcur0]
                            for tb in (3, 2):
                                self.proj_pass(g, gi, x_own, tb, False, bufs)
                                tiles = list(range(tb * 8 + 7, tb * 8 - 1, -1))
                                self.gates_phase(graw, gsc, tb * 8, tb * 8 + 8)
                                mview = self.gates_scan(gsc, tiles, 1, mview, mb, 1)
                                mb = [gsc["mall"]]
                                self.gates_bulk(gsc, tb * 8, tb * 8 + 8)
                                for c in tiles:
                                    for h in range(4):
                                        self.mlstm_state_only(gsc, Sres[0][4 + h], ktok, vo, c, c - tb * 8, h, wk_r)
                            mcur = mview
                        for tb in range(n_own // 8):
                            self.proj_pass(g, gi, x_own, tb, True, bufs)
                        k.barrier()
                        if self.ck(g + "proj"):
                            return
                    with ExitStack() as sw:
                        hacc = A(sw, g + "hacc", [128, n_own, 512], F32)
                        self.gates_phase(graw, gsc, 0, n_own)
                        if samp:
                            self.gates_scan(gsc, list(range(16)), 0, mcur0.t[:, 0:1, 0:4], [mcur0], 1)
                            self.gates_scan(gsc, list(range(15, -1, -1)), 1, mcur, [gsc["mall"]], 1)
                        else:
                            self.gates_scan(gsc, [0, 1], 0, mcur0.t[:, :, 0:4], [mcur0], 4)
                            self.gates_scan(gsc, [1, 0], 1, mcur0.t[:, :, 4:8], [mcur0], 4)
                        self.gates_bulk(gsc, 0, n_own)
                        if self.ck(g + "gates"):
                            k.barrier()
                            return
                        self.mlstm_sweeps(g, gsc, Sres, qT, kT, vaug, hacc, zmT, sw, nseq, n_own)
                        if not samp:
                            for s_ in range(4):
                                for j_ in range(8):
                                    k.dma("sp", o["o_C"][s_, j_], Sres[s_][j_].t[:, 0:128], [Sres[s_][j_]], [])
                            pbn = self.psb()
                            for s_ in range(4):
                                k.op("pe", lambda s_=s_: nc.tensor.transpose(out=pbn.t[0:8, s_ * 128:(s_ + 1) * 128], in_=S.t[:, s_, :, 128],
                                                                             identity=self.ident_f.t[:]), Sres[s_] + [self.ident_f], [pbn])
                            for hf in range(2):
                                n_out = self.gtmp.next()
                                self.V(lambda: nc.vector.tensor_copy(out=n_out.t[0:8, 0:256], in_=pbn.t[0:8, hf * 256:(hf + 1) * 256]), [pbn], [n_out])
                                for s2 in range(2):
                                    k.dma("sp", o["o_n"][hf * 2 + s2], n_out.t[0:8, s2 * 128:(s2 + 1) * 128], [n_out], [])
                            mall = gsc["mall"]
                            for s_ in range(4):
                                k.dma("sp", o["o_m"][s_:s_ + 1, 0:4], mall.t[0:1, s_ * 2 + 1, 0:4], [mall], [])
                                k.dma("sp", o["o_m"][s_:s_ + 1, 4:8], mall.t[0:1, s_ * 2 + 0, 4:8], [mall], [])
                        k.barrier()
                        if self.ck(g + "sweeps"):
                            return
                with ExitStack() as at:
                    self.attention(g, at, cqnT, ckvT, krT, zaT, n_own, nkey)
                    k.barrier()
                    if self.ck(g + "attn"):
                        return
            self.join_mlp(g, gi, x_own, zmT, zaT, n_own)
            k.barrier()

    def proj_pass(self, g, gi, x_dram, tb, own, B):
        nc, k, d, o = self.nc, self.k, self.d, self.o
        samp = (g == "s")
        hT = B["hT"]
        t0 = tb * 8
        w = d["w_in"]
        reqs = []
        if own:
            reqs += [(w[:, C_Q:C_Q + 512], 8, 512)]
        reqs += [(w[:, C_K:C_K + 512], 8, 512), (w[:, C_V:C_V + 512], 8, 512), (w[:, C_G:C_G + 16], 8, 16)]
        if own:
            reqs += [(w[:, C_CQ:C_CQ + 384], 8, 384)]
        reqs += [(w[:, C_CKV:C_CKV + 288], 8, 288)]
        self.wstream(reqs)
        self.wget(-1)
        xts = []

        def issue(i):
            b = self.xring.next()
            r0 = (t0 + i) * 128
            k.dma("sp", b.t[:], x_dram[r0:r0 + 128, :], [], [b])
            xts.append(b)
        issue(0)
        issue(1)
        for i in range(8):
            self.norm_transpose(xts[i].t[:], xts[i], gi, 0, hT, i * 128)
            if i + 2 < 8:
                issue(i + 2)
        wi = 0
        if self.ck("pp1"):
            return

        def featmajor(W, ws, dst, scale):
            for h in range(4):
                for half in range(2):
                    pb = self.psb()
                    self.mm_group(pb.t[:, :], pb, [(W[:, kc, h * 128:(h + 1) * 128], hT.t[:, kc, half * 512:(half + 1) * 512])
                                                   for kc in range(8)], [ws] + hT.tiles[half * 4:half * 4 + 4])
                    c0 = tb * 1024 + half * 512
                    self.copy(self.evac_engine(), dst.t[:, h, c0:c0 + 512], pb.t[:, :], [pb], [dst], scale=scale)

        def tokmajor(W, ws, ncols, fn, defer=False):
            prev = None
            for i in range(8):
                pb = self.psb()
                self.mm_group(pb.t[:, 0:ncols], pb, [(hT.t[:, kc, i * 128:(i + 1) * 128], W[:, kc, :]) for kc in range(8)], [ws, hT.tiles[i]])
                if defer:
                    post = fn(i, pb)
                    if prev is not None:
                        prev()
                    prev = post
                else:
                    fn(i, pb)
            if prev is not None:
                prev()

        if own:
            W, ws = self.wget(wi); wi += 1
            featmajor(W, ws, B["qT"], 128.0 ** -0.5)
            if self.ck("pp2"):
                return
            W, ws = self.wget(wi); wi += 1
            featmajor(W, ws, B["kT"], None)
        else:
            W, ws = self.wget(wi); wi += 1
            tokmajor(W, ws, 512, lambda i, pb: self.copy(self.evac_engine(), B["ktok"].t[:, i, :], pb.t[:, :], [pb], [B["ktok"]]))
        if self.ck("pp3"):
            return
        W, ws = self.wget(wi); wi += 1
        if own:
            tokmajor(W, ws, 512, lambda i, pb: self.copy(self.evac_engine(), B["vaug"].t[:, t0 + i, :, 0:128],
                                                         pb.t[:, :].rearrange("p (h e) -> p h e", h=4), [pb], [B["vaug"]]))
        else:
            tokmajor(W, ws, 512, lambda i, pb: self.copy(self.evac_engine(), B["vo"].t[:, i, :], pb.t[:, :], [pb], [B["vo"]]))
        W, ws = self.wget(wi); wi += 1
        tokmajor(W, ws, 16, lambda i, pb: self.copy("dve", B["graw"].t[:, t0 + i, :], pb.t[:, 0:16], [pb], [B["graw"]]))
        if self.ck("pp5"):
            return
        if own:
            W, ws = self.wget(wi); wi += 1

            def cq_fn(i, pb):
                ss = self.smr.next()
                junk = self.junk.next()
                k.op("act", lambda: nc.scalar.activation(out=junk.t[:, 0:384], in_=pb.t[:, 0:384], func=AF.Square,
                                                         accum_out=ss.t[:, 0:1]), [pb], [junk, ss])
                rstd = self.rstd_from_ss(ss, 0, 384, [])
                tb_ = B["tokb"].next()
                self.V(lambda: nc.vector.scalar_tensor_tensor(out=tb_.t[:, 0:384], in0=pb.t[:, 0:384], scalar=rstd,
                                                              in1=self.qn_row.t[:], op0=ALU.mult, op1=ALU.mult),
                       [pb, ss, self.qn_row], [tb_])

                def post():
                    p2 = self.psb()
                    pv = p2.t[:].bitcast(BF16).rearrange("p (f t) -> p f t", t=128)
                    for kc in range(3):
                        k.op("pe", lambda kc=kc: nc.tensor.transpose(out=pv[:, kc, :], in_=tb_.t[:, kc * 128:(kc + 1) * 128],
                                                                     identity=self.ident_b.t[:]), [tb_, self.ident_b], [p2], sig=(kc == 2))
                    c0 = (t0 + i) * 128
                    self.copy(self.evac_engine(), B["cqnT"].t[:, :, c0:c0 + 128], pv[:, 0:3, :], [p2], [B["cqnT"]])
                return post
            tokmajor(W, ws, 384, cq_fn, defer=True)
        if self.ck("pp6"):
            return
        W, ws = self.wget(wi); wi += 1

        def ckv_fn(i, pb):
            ti = t0 + i
            ss = self.smr.next()
            junk = self.junk.next()
            k.op("act", lambda: nc.scalar.activation(out=junk.t[:, 0:256], in_=pb.t[:, 0:256], func=AF.Square,
                                                     accum_out=ss.t[:, 0:1]), [pb], [junk, ss])
            rstd = self.rstd_from_ss(ss, 0, 256, [])
            tw = B["tokw"].next()
            tb_ = B["tokb"].next()
            self.V(lambda: nc.vector.scalar_tensor_tensor(out=tw.t[:, 0:256], in0=pb.t[:, 0:256], scalar=rstd,
                                                          in1=self.kvn_row.t[:], op0=ALU.mult, op1=ALU.mult),
                   [pb, ss, self.kvn_row], [tw])
            self.V(lambda: nc.vector.tensor_copy(out=tw.t[:, 256:288], in_=pb.t[:, 256:288]), [pb], [tw])
            self.V(lambda: nc.vector.tensor_copy(out=tb_.t[:, 0:256], in_=tw.t[:, 0:256]), [tw], [tb_])
            if not samp:
                k.dma("sp", o["o_ckv"][ti * 128:(ti + 1) * 128, :], tw.t[:, 0:256], [tw], [])
                k.dma("sp", o["o_kr"][ti * 128:(ti + 1) * 128, :], tw.t[:, 256:288], [tw], [])
                self.V(lambda: nc.vector.tensor_copy(out=tb_.t[:, 256:288], in_=tw.t[:, 256:288]), [tw], [tb_])
            else:
                cs = B["cs"].next()
                rot = B["rot"].next()
                k.dma("sp", cs.t[:, 0, :], d["kcos"][ti * 128:(ti + 1) * 128, :], [], [cs])
                k.dma("sp", cs.t[:, 1, :], d["ksin"][ti * 128:(ti + 1) * 128, :], [], [cs])
                kr2 = tw.t[:, 256:288].rearrange("p (i two) -> p i two", two=2)
                x1, x2 = kr2[:, :, 0], kr2[:, :, 1]
                ob = tb_.t[:, 256:288].rearrange("p (i two) -> p i two", two=2)
                cos_, sin_ = cs.t[:, 0, :], cs.t[:, 1, :]
                self.V(lambda: nc.vector.tensor_tensor(out=rot.t[:, 0, :], in0=x1, in1=cos_, op=ALU.mult), [tw, cs], [rot])
                self.V(lambda: nc.vector.tensor_tensor(out=rot.t[:, 1, :], in0=x2, in1=sin_, op=ALU.mult), [tw, cs], [rot])
                self.V(lambda: nc.vector.tensor_tensor(out=rot.t[:, 2, :], in0=x1, in1=sin_, op=ALU.mult), [tw, cs], [rot])
                self.V(lambda: nc.vector.tensor_tensor(out=rot.t[:, 3, :], in0=x2, in1=cos_, op=ALU.mult), [tw, cs], [rot])
                self.V(lambda: nc.vector.tensor_tensor(out=ob[:, :, 0], in0=rot.t[:, 0, :], in1=rot.t[:, 1, :], op=ALU.subtract), [rot], [tb_])
                self.V(lambda: nc.vector.tensor_tensor(out=ob[:, :, 1], in0=rot.t[:, 2, :], in1=rot.t[:, 3, :], op=ALU.add), [rot], [tb_])

            def post():
                p2 = self.psb()
                pv = p2.t[:].bitcast(BF16).rearrange("p (f t) -> p f t", t=128)
                for kc in range(2):
                    k.op("pe", lambda kc=kc: nc.tensor.transpose(out=pv[:, kc, :], in_=tb_.t[:, kc * 128:(kc + 1) * 128],
                                                                 identity=self.ident_b.t[:]), [tb_, self.ident_b], [p2], sig=False)
                k.op("pe", lambda: nc.tensor.transpose(out=pv[0:32, 2, :], in_=tb_.t[:, 256:288], identity=self.ident_b.t[:]),
                     [tb_, self.ident_b], [p2])
                c0 = ti * 128
                eng = self.evac_engine()
                self.copy(eng, B["ckvT"].t[:, :, c0:c0 + 128], pv[:, 0:2, :], [p2], [B["ckvT"]])
                self.copy(eng, B["krT"].t[0:32, c0:c0 + 128], pv[0:32, 2, :], [p2], [B["krT"]])
            return post
        tokmajor(W, ws, 288, ckv_fn, defer=True)

    def gates_phase(self, graw, gsc, lo, hi):
        nc, k = self.nc, self.k
        n = hi - lo
        G = graw.t[:, lo:hi, :]
        self.V(lambda: nc.vector.tensor_tensor(out=G, in0=G, in1=self.gb_row.t[:].unsqueeze(1).broadcast_to([128, n, 16]), op=ALU.add),
               [graw, self.gb_row], [graw])
        G5 = graw.t[:, lo:hi, :].rearrange("p n (r i h) -> p n r i h", r=2, i=2)
        gf, gi_ = G5[:, :, :, 1, :], G5[:, :, :, 0, :]
        t1b, t2b = self.gtmp.next(), self.gtmp.next()
        t1 = t1b.t[:, 0:n * 8].rearrange("p (n r h) -> p n r h", r=2, h=4)
        t2 = t2b.t[:, 0:n * 8].rearrange("p (n r h) -> p n r h", r=2, h=4)
        k.op("act", lambda: nc.scalar.activation(out=t1, in_=gf, func=AF.Abs), [graw], [t1b])
        k.op("act", lambda: nc.scalar.activation(out=t1, in_=t1, func=AF.Exp, scale=-1.0), [t1b], [t1b])
        k.op("act", lambda: nc.scalar.activation(out=t1, in_=t1, func=AF.Ln, scale=1.0, bias=1.0), [t1b], [t1b])
        self.V(lambda: nc.vector.tensor_single_scalar(out=t2, in_=gf, scalar=0.0, op=ALU.min), [graw], [t2b])
        lf, b, bL, a, amax = gsc["lf"], gsc["b"], gsc["bL"], gsc["a"], gsc["amax"]
        lfv = lf.t[:, lo:hi, :].rearrange("p n (r h) -> p n r h", r=2)
        self.V(lambda: nc.vector.tensor_tensor(out=lfv, in0=t2, in1=t1, op=ALU.subtract), [t1b, t2b], [lf])
        pb = self.psb()
        for r, tri in ((0, self.trif), (1, self.trib)):
            k.op("pe", lambda r=r, tri=tri: nc.tensor.matmul(
                pb.t[:, r * n * 4:(r + 1) * n * 4].rearrange("p (n h) -> p n h", h=4), lhsT=tri.t[:],
                rhs=lf.t[:, lo:hi, r * 4:(r + 1) * 4], start=True, stop=True), [tri, lf], [pb])
            self.V(lambda r=r: nc.vector.tensor_copy(out=b.t[:, lo:hi, r * 4:(r + 1) * 4],
                                                     in_=pb.t[:, r * n * 4:(r + 1) * n * 4].rearrange("p (n h) -> p n h", h=4)), [pb], [b])
        pb2 = self.psb()
        k.op("pe", lambda: nc.tensor.matmul(pb2.t[:, 0:n * 8], lhsT=self.ones_f.t[:], rhs=lf.t[:, lo:hi, :].rearrange("p n j -> p (n j)"),
                                            start=True, stop=True), [self.ones_f, lf], [pb2])
        self.V(lambda: nc.vector.tensor_copy(out=bL.t[:, lo:hi, :].rearrange("p n j -> p (n j)"), in_=pb2.t[:, 0:n * 8]), [pb2], [bL])
        av = a.t[:, lo:hi, :].rearrange("p n (r h) -> p n r h", r=2)
        bv = b.t[:, lo:hi, :].rearrange("p n (r h) -> p n r h", r=2)
        self.V(lambda: nc.vector.tensor_tensor(out=av, in0=gi_, in1=bv, op=ALU.subtract), [graw, b], [a])
        for blo in range(lo, hi, 16):
            nb = min(16, hi - blo)
            cols = nb * 8
            p3 = self.psb()
            k.op("pe", lambda: nc.tensor.transpose(out=p3.t[0:cols, 0:128], in_=a.t[:, blo:blo + nb, :].rearrange("p n j -> p (n j)"),
                                                   identity=self.ident_f.t[:]), [a, self.ident_f], [p3])
            sm = self.smr.next()
            self.V(lambda: nc.vector.tensor_reduce(out=sm.t[0:cols, 0:1], in_=p3.t[0:cols, 0:128], axis=AX.X, op=ALU.max), [p3], [sm])
            dg = self.gtmp.next()
            self.V(lambda: nc.vector.tensor_scalar(out=dg.t[0:cols, 0:cols], in0=self.ident_f.t[0:cols, 0:cols], scalar1=sm.t[0:cols, 0:1],
                                                   scalar2=None, op0=ALU.mult), [self.ident_f, sm], [dg])
            p4 = self.psb()
            k.op("pe", lambda: nc.tensor.matmul(p4.t[:, 0:cols], lhsT=self.ones_f.t[0:cols, :], rhs=dg.t[0:cols, 0:cols],
                                                start=True, stop=True), [self.ones_f, dg], [p4])
            self.V(lambda: nc.vector.tensor_copy(out=amax.t[:, blo:blo + nb, :].rearrange("p n j -> p (n j)"), in_=p4.t[:, 0:cols]),
                   [p4], [amax])

    def gates_scan(self, gsc, order, dr, m_ap, m_bufs, nseq):
        nc = self.nc

        def view(nm, c):
            t = gsc[nm].t
            if nseq == 1:
                return t[:, c:c + 1, dr * 4:dr * 4 + 4]
            return t[:, :, dr * 4:dr * 4 + 4].rearrange("p (s c) j -> p s c j", c=2)[:, :, c, :]
        cur, cb = m_ap, list(m_bufs)
        for c in order:
            ref, dm, mall = view("ref", c), view("dm", c), view("mall", c)
            self.V(lambda: nc.vector.tensor_tensor(out=ref, in0=cur, in1=view("amax", c), op=ALU.max), cb + [gsc["amax"]], [gsc["ref"]])
            self.V(lambda: nc.vector.tensor_tensor(out=dm, in0=cur, in1=ref, op=ALU.subtract), cb + [gsc["ref"]], [gsc["dm"]])
            self.V(lambda: nc.vector.tensor_tensor(out=mall, in0=view("bL", c), in1=ref, op=ALU.add), [gsc["bL"], gsc["ref"]], [gsc["mall"]])
            cur, cb = mall, [gsc["mall"]]
        return cur

    def gates_bulk(self, gsc, lo, hi):
        nc, k = self.nc, self.k
        n = hi - lo
        sl = lambda nm: gsc[nm].t[:, lo:hi, :]
        self.V(lambda: nc.vector.tensor_single_scalar(out=sl("dm"), in_=sl("dm"), scalar=-100.0, op=ALU.max), [gsc["dm"]], [gsc["dm"]])
        k.op("act", lambda: nc.scalar.activation(out=sl("dec"), in_=sl("dm"), func=AF.Exp), [gsc["dm"]], [gsc["dec"]])
        t1b, t2b = self.gtmp.next(), self.gtmp.next()
        t1 = t1b.t[:, 0:n * 8].rearrange("p (n j) -> p n j", j=8)
        t2 = t2b.t[:, 0:n * 8].rearrange("p (n j) -> p n j", j=8)
        self.V(lambda: nc.vector.tensor_tensor(out=t1, in0=sl("a"), in1=sl("ref"), op=ALU.subtract), [gsc["a"], gsc["ref"]], [t1b])
        k.op("act", lambda: nc.scalar.activation(out=sl("u"), in_=t1, func=AF.Exp), [t1b], [gsc["u"]])
        self.V(lambda: nc.vector.tensor_tensor(out=t2, in0=sl("b"), in1=sl("ref"), op=ALU.add), [gsc["b"], gsc["ref"]], [t2b])
        self.V(lambda: nc.vector.tensor_single_scalar(out=t2, in_=t2, scalar=-80.0, op=ALU.max), [t2b], [t2b])
        k.op("act", lambda: nc.scalar.activation(out=sl("E"), in_=t2, func=AF.Exp, scale=-1.0), [t2b], [gsc["E"]])

    def mlstm_state_only(self, gsc, Sb_, ktok, vo, c, ci, h, wk_r):
        nc, k = self.nc, self.k
        j = 4 + h
        u = gsc["u"].t[:, c, j:j + 1]
        dec = gsc["dec"].t[:, c, j:j + 1]
        wk = wk_r.next()
        k.op("act", lambda: nc.scalar.activation(out=wk.t[:], in_=ktok.t[:, ci, h * 128:(h + 1) * 128], func=AF.Copy, scale=u),
             [ktok, gsc["u"]], [wk])
        pb = self.psb()
        k.op("pe", lambda: nc.tensor.matmul(pb.t[:, 0:128], lhsT=wk.t[:], rhs=vo.t[:, ci, h * 128:(h + 1) * 128], start=True, stop=True),
             [wk, vo], [pb], sig=False)
        k.op("pe", lambda: nc.tensor.matmul(pb.t[:, 128:129], lhsT=wk.t[:], rhs=self.ones_b.t[:, 0:1], start=True, stop=True),
             [wk, self.ones_b], [pb])
        self.V(lambda: nc.vector.scalar_tensor_tensor(out=Sb_.t[:], in0=Sb_.t[:], scalar=dec, in1=pb.t[:, 0:129], op0=ALU.mult, op1=ALU.add),
               [Sb_, gsc["dec"], pb], [Sb_])

    def mlstm_sweeps(self, g, gsc, Sres, qT, kT, vaug, hacc, zmT, es, nseq, n_own):
        nc, k = self.nc, self.k
        A = self.alloc
        Sbf = Ring([A(es, g + "Sbf%d" % i, [128, 129], BF16) for i in range(8)])
        qkr = Ring([A(es, g + "qk%d" % i, [128, 128], BF16) for i in range(8)])
        wkr = Ring([A(es, g + "wk%d" % i, [128, 128], BF16) for i in range(8)])
        hnr = Ring([A(es, g + "hn%d" % i, [128, 512], BF16) for i in range(2)])
        jk = A(es, g + "jk", [128, 128], BF16)
        npc = n_own // nseq
        visits = {}
        masks = (self.trif, self.trib)

        def stage_a(ch):
            s_, c, dr, h, pb = ch["s"], ch["c"], ch["dr"], ch["h"], ch["pb"]
            j = dr * 4 + h
            Sb_ = Sres[s_][j]
            cols = slice(c * 128, (c + 1) * 128)
            dec = gsc["dec"].t[:, c, j:j + 1]
            sb = Sbf.next()
            ch["sb"] = sb
            k.op("act", lambda: nc.scalar.activation(out=sb.t[:], in_=Sb_.t[:], func=AF.Copy, scale=dec), [Sb_, gsc["dec"]], [sb])
            pA = pb.t[:, 0:128]
            pC = pb.t[:, 260:324].bitcast(BF16)
            k.op("pe", lambda: nc.tensor.matmul(pA, lhsT=kT.t[:, h, cols], rhs=qT.t[:, h, cols], start=True, stop=True),
                 [kT, qT], [pb], sig=False)
            k.op("pe", lambda: nc.tensor.transpose(out=pC, in_=kT.t[:, h, cols], identity=self.ident_b.t[:]), [kT, self.ident_b], [pb])

        def stage_b(ch):
            s_, c, dr, h, pb = ch["s"], ch["c"], ch["dr"], ch["h"], ch["pb"]
            j = dr * 4 + h
            u = gsc["u"].t[:, c, j:j + 1]
            pA = pb.t[:, 0:128]
            pC = pb.t[:, 260:324].bitcast(BF16)
            qk = qkr.next()
            ch["qk"] = qk
            self.V(lambda: nc.vector.scalar_tensor_tensor(out=qk.t[:], in0=pA, scalar=u, in1=masks[dr].t[:], op0=ALU.mult, op1=ALU.mult),
                   [pb, gsc["u"], masks[dr]], [qk])
            wk = wkr.next()
            ch["wk"] = wk
            k.op("act", lambda: nc.scalar.activation(out=wk.t[:], in_=pC, func=AF.Copy, scale=u), [pb, gsc["u"]], [wk])

        def stage_c(ch):
            s_, c, dr, h, pb = ch["s"], ch["c"], ch["dr"], ch["h"], ch["pb"]
            cols = slice(c * 128, (c + 1) * 128)
            pB = pb.t[:, 128:257]
            pD = pb.t[:, 328:457]
            sb, qk, wk = ch["sb"], ch["qk"], ch["wk"]
            k.op("pe", lambda: nc.tensor.matmul(pB, lhsT=qT.t[:, h, cols], rhs=sb.t[:], start=True, stop=False), [qT, sb], [pb], sig=False)
            k.op("pe", lambda: nc.tensor.matmul(pB, lhsT=qk.t[:], rhs=vaug.t[:, c, h, :], start=False, stop=True), [qk, vaug], [pb], sig=False)
            k.op("pe", lambda: nc.tensor.matmul(pD, lhsT=wk.t[:], rhs=vaug.t[:, c, h, :], start=True, stop=True), [wk, vaug], [pb])
            dcol = self.psum[0].t[:, 460 + ch["ci"]:461 + ch["ci"]]
            k.op("pe", lambda: nc.tensor.matmul(dcol, lhsT=qT.t[:, h, cols], rhs=sb.t[:, 128:129], start=True, stop=False),
                 [qT, sb], [self.psum[0]], sig=False)
            k.op("pe", lambda: nc.tensor.matmul(dcol, lhsT=qk.t[:], rhs=vaug.t[:, c, h, 128:129], start=False, stop=True),
                 [qk, vaug], [self.psum[0]])

        def stage_d0(chains):
            cb_, cf_ = chains[0]["c"], chains[1]["c"]
            sm = self.smr.next()
            den8 = self.psum[0].t[:, 460:468]
            self.V(lambda: nc.vector.tensor_scalar(out=sm.t[:, 8:16], in0=den8, scalar1=-1.0, scalar2=None, op0=ALU.mult), [self.psum[0]], [sm])
            self.V(lambda: nc.vector.tensor_tensor(out=sm.t[:, 0:8], in0=den8, in1=sm.t[:, 8:16], op=ALU.max), [self.psum[0], sm], [sm])
            d2 = sm.t[:, 0:8].rearrange("p (h two) -> p h two", two=2)
            self.V(lambda: nc.vector.tensor_tensor(out=d2[:, :, 0], in0=d2[:, :, 0], in1=gsc["E"].t[:, cb_, 4:8], op=ALU.max), [sm, gsc["E"]], [sm])
            self.V(lambda: nc.vector.tensor_tensor(out=d2[:, :, 1], in0=d2[:, :, 1], in1=gsc["E"].t[:, cf_, 0:4], op=ALU.max), [sm, gsc["E"]], [sm])
            self.V(lambda: nc.vector.reciprocal(out=sm.t[:, 0:8], in_=sm.t[:, 0:8]), [sm], [sm])
            for ch in chains:
                ch["rd"] = sm

        def stage_d(ch):
            s_, c, dr, h, pb = ch["s"], ch["c"], ch["dr"], ch["h"], ch["pb"]
            j = dr * 4 + h
            Sb_ = Sres[s_][j]
            E = gsc["E"].t[:, c, j:j + 1]
            pB = pb.t[:, 128:257]
            pD = pb.t[:, 328:457]
            sm = ch["rd"]
            rd = sm.t[:, ch["ci"]:ch["ci"] + 1]
            hv = hacc.t[:, c, h * 128:(h + 1) * 128]
            if visits.get((c, h), 0) == 0:
                self.V(lambda: nc.vector.tensor_scalar(out=hv, in0=pB[:, 0:128], scalar1=rd, scalar2=None, op0=ALU.mult), [pb, sm], [hacc])
            else:
                self.V(lambda: nc.vector.scalar_tensor_tensor(out=hv, in0=pB[:, 0:128], scalar=rd, in1=hv, op0=ALU.mult, op1=ALU.add),
                       [pb, sm, hacc], [hacc])
            visits[(c, h)] = visits.get((c, h), 0) + 1
            dec = gsc["dec"].t[:, c, j:j + 1]
            self.V(lambda: nc.vector.scalar_tensor_tensor(out=Sb_.t[:], in0=Sb_.t[:], scalar=dec, in1=pD, op0=ALU.mult, op1=ALU.add),
                   [Sb_, gsc["dec"], pb], [Sb_])

        def finalize(c):
            ss = self.smr.next()
            for h in range(4):
                k.op("act", lambda h=h: nc.scalar.activation(out=jk.t[:], in_=hacc.t[:, c, h * 128:(h + 1) * 128], func=AF.Square,
                                                             accum_out=ss.t[:, h:h + 1]), [hacc], [jk, ss])
            k.op("act", lambda: nc.scalar.activation(out=ss.t[:, 4:8], in_=ss.t[:, 0:4], func=AF.Ln, scale=1.0 / 128, bias=EPS), [ss], [ss])
            k.op("act", lambda: nc.scalar.activation(out=ss.t[:, 8:12], in_=ss.t[:, 4:8], func=AF.Exp, scale=-0.5), [ss], [ss])
            hn = hnr.next()
            self.V(lambda: nc.vector.tensor_tensor(out=hn.t[:].rearrange("p (h e) -> p h e", h=4),
                                                   in0=hacc.t[:, c, :].rearrange("p (h e) -> p h e", h=4),
                                                   in1=ss.t[:, 8:12].unsqueeze(2).broadcast_to([128, 4, 128]), op=ALU.mult), [hacc, ss], [hn])
            pb = self.psb()
            pv = pb.t[:].bitcast(BF16).rearrange("p (f t) -> p f t", t=128)
            for h in range(4):
                k.op("pe", lambda h=h: nc.tensor.transpose(out=pv[:, h, :], in_=hn.t[:, h * 128:(h + 1) * 128], identity=self.ident_b.t[:]),
                     [hn, self.ident_b], [pb], sig=(h == 3))
            self.V(lambda: nc.vector.tensor_tensor(out=zmT.t[:, :, c * 128:(c + 1) * 128], in0=pv[:, 0:4, :],
                                                   in1=self.hnT.t[:].unsqueeze(2).broadcast_to([128, 4, 128]), op=ALU.mult),
                   [pb, self.hnT], [zmT])

        for jj in range(npc):
            for s_ in range(nseq):
                cb = s_ * npc + (npc - 1 - jj)
                cf = s_ * npc + jj
                chains = []
                for h in range(4):
                    chains.append(dict(s=s_, c=cb, dr=1, h=h, pb=self.psum[2 * h], ci=2 * h))
                    chains.append(dict(s=s_, c=cf, dr=0, h=h, pb=self.psum[2 * h + 1], ci=2 * h + 1))
                for st in (stage_a, stage_b, stage_c):
                    for ch in chains:
                        st(ch)
                stage_d0(chains)
                for ch in chains:
                    stage_d(ch)
                for c in sorted({cb, cf}):
                    if all(visits.get((c, h), 0) == 2 for h in range(4)):
                        finalize(c)

    def attention(self, g, es, cqnT, ckvT, krT, zaT, n_own, nkey):
        nc, k, d = self.nc, self.k, self.d
        A = self.alloc
        samp = (g == "s")
        Tq = n_own * 128
        Sk = nkey * 128
        w_uqc = A(es, g + "w_uqc", [128, 3, 8, 128], BF16)
        w_ukp = A(es, g + "w_ukp", [128, 2, 8, 128], BF16)
        w_uv = A(es, g + "w_uv", [128, 2, 512], BF16)
        esel = A(es, g + "esel", [32, 128], BF16)
        self.V(lambda: nc.vector.memset(w_ukp.t[:], 0.0), [], [w_ukp])
        for kc in range(3):
            rs = slice(kc * 128, (kc + 1) * 128)
            k.dma("pool", w_uqc.t[:, kc, :, 0:96], d["w_uq"][rs, :].rearrange("p (h n) -> p h n", h=8), [], [w_uqc])
            k.dma("pool", w_uqc.t[:, kc, :, 96:128], d["w_uq_sw"][rs], [], [w_uqc])
        for kc in range(2):
            rs = slice(kc * 128, (kc + 1) * 128)
            k.dma("pool", w_ukp.t[:, kc, :, 0:64], d["w_uk"][rs], [], [w_ukp])
        k.dma("pool", w_uv.t[:], d["w_uv"].rearrange("(k p) n -> p k n", p=128), [], [w_uv])
        k.dma("pool", esel.t[:], d["esel"], [], [esel])
        V_all = A(es, g + "Vall", [128, nkey, 8 * 65 + 64], BF16)
        KhT = Ring([A(es, g + "KhT%d" % i, [128, Sk], BF16) for i in range(2)])
        qhT = Ring([A(es, g + "qhT%d" % i, [128, Tq], BF16) for i in range(2)])
        sqb = A(es, g + "sqb", [128, Sk], BF16)
        sqq = A(es, g + "sqq", [128, Tq], BF16)
        PT = Ring([A(es, g + "PT%d" % i, [128, 512], BF16) for i in range(4)])
        tmpO = Ring([A(es, g + "tmpO%d" % i, [128, 512], F32) for i in range(2)])
        stg = Ring([A(es, g + "stg%d" % i, [64, 512], BF16) for i in range(2)])
        mx = A(es, g + "mx", [1, 32], F32)
        negM = Ring([A(es, g + "negM%d" % i, [128, 1], F32) for i in range(2)])
        psS = Ring(self.psum[0:4])
        psO = Ring(self.psum[4:6])
        psX = Ring(self.psum[6:8])
        self.V(lambda: nc.vector.memset(V_all.t[:], 0.0), [], [V_all])
        self.V(lambda: nc.vector.memset(V_all.t[:, :, 0:520].rearrange("p k (h e) -> p k h e", e=65)[:, :, :, 64:65], 1.0), [], [V_all])
        if samp:
            qtab = A(es, g + "qtab", [128, Tq], F32)
            k.dma("sp", qtab.t[64:128, :], d["qtab"], [], [qtab])
            self.V(lambda: nc.vector.tensor_scalar(out=qtab.t[64:128, :], in0=qtab.t[64:128, :], scalar1=SCQ, scalar2=None, op0=ALU.mult),
                   [qtab], [qtab])
            cw = Ring([A(es, g + "cw%d" % i, [128, 288], F32) for i in range(2)])
            cb_ = Ring([A(es, g + "cb%d" % i, [128, 288], BF16) for i in range(2)])
            for i in range(4):
                w_, b_ = cw.next(), cb_.next()
                k.dma("sp", w_.t[:, 0:256], d["cckv"][i * 128:(i + 1) * 128, :], [], [w_])
                k.dma("sp", w_.t[:, 256:288], d["ckr"][i * 128:(i + 1) * 128, :], [], [w_])
                self.V(lambda: nc.vector.tensor_copy(out=b_.t[:], in_=w_.t[:]), [w_], [b_])
                p2 = psX.next()
                pv = p2.t[:].bitcast(BF16).rearrange("p (f t) -> p f t", t=128)
                for kc in range(2):
                    k.op("pe", lambda kc=kc: nc.tensor.transpose(out=pv[:, kc, :], in_=b_.t[:, kc * 128:(kc + 1) * 128],
                                                                 identity=self.ident_b.t[:]), [b_, self.ident_b], [p2], sig=False)
                k.op("pe", lambda: nc.tensor.transpose(out=pv[0:32, 2, :], in_=b_.t[:, 256:288], identity=self.ident_b.t[:]),
                     [b_, self.ident_b], [p2])
                c0 = (32 + i) * 128
                self.copy("dve", ckvT.t[:, :, c0:c0 + 128], pv[:, 0:2, :], [p2], [ckvT])
                self.copy("dve", krT.t[0:32, c0:c0 + 128], pv[0:32, 2, :], [p2], [krT])
        else:
            qcol = A(es, g + "qcol", [128, 1], F32)
            self.V(lambda: nc.vector.memset(qcol.t[:], 0.0), [], [qcol])
            self.V(lambda: nc.vector.memset(qcol.t[64:96, :], SCQ), [], [qcol])
        for kt in range(nkey):
            pb = psX.next()
            self.mm_group(pb.t[:, :], pb, [(ckvT.t[:, kc, kt * 128:(kt + 1) * 128], w_uv.t[:, kc, :]) for kc in range(2)], [ckvT, w_uv])
            self.copy("dve" if kt % 2 else "act", V_all.t[:, kt, 0:520].rearrange("p (h e) -> p h e", e=65)[:, :, 0:64],
                      pb.t[:, :].rearrange("p (h e) -> p h e", h=8), [pb], [V_all])
        if samp:
            units = [(qb * 512, 512, list(range(nkey))) for qb in range(Tq // 512)]
        else:
            units = [(s_ * 256, 256, [2 * s_, 2 * s_ + 1]) for s_ in range(4)]
        heads = {}

        def build(h):
            kh, qh = KhT.next(), qhT.next()
            heads[h] = dict(kh=kh, qh=qh)
            return [lambda c0=c0: build_k(h, c0) for c0 in range(0, Sk, 512)] + [lambda c0=c0: build_q(h, c0) for c0 in range(0, Tq, 512)]

        def build_k(h, c0):
            kh = heads[h]["kh"]
            if True:
                pb = psX.next()
                self.mm_group(pb.t[:, :], pb, [(esel.t[:, :], krT.t[0:32, c0:c0 + 512])] +
                              [(w_ukp.t[:, kc, h, :], ckvT.t[:, kc, c0:c0 + 512]) for kc in range(2)], [esel, krT, w_ukp, ckvT])
                self.copy("dve", kh.t[:, c0:c0 + 512], pb.t[:, :], [pb], [kh])

        def build_q(h, c0):
            qh = heads[h]["qh"]
            if True:
                pb = psX.next()
                self.mm_group(pb.t[:, :], pb, [(w_uqc.t[:, kc, h, :], cqnT.t[:, kc, c0:c0 + 512]) for kc in range(3)], [w_uqc, cqnT])
                self.V(lambda: nc.vector.tensor_scalar(out=qh.t[0:64, c0:c0 + 512], in0=pb.t[0:64, :], scalar1=SCQ, scalar2=None, op0=ALU.mult),
                       [pb], [qh])
                if samp:
                    self.V(lambda: nc.vector.tensor_tensor(out=qh.t[64:128, c0:c0 + 512], in0=pb.t[64:128, :], in1=qtab.t[64:128, c0:c0 + 512],
                                                           op=ALU.mult), [pb, qtab], [qh])
                else:
                    self.V(lambda: nc.vector.tensor_scalar(out=qh.t[64:128, c0:c0 + 512], in0=pb.t[64:128, :], scalar1=qcol.t[64:128, 0:1],
                                                           scalar2=None, op0=ALU.mult), [pb, qcol], [qh])

        def bound(h):
            kh, qh = heads[h]["kh"], heads[h]["qh"]
            nq_blk = Tq // 512
            nk_blk = Sk // 512
            pieces = []

            def sq_piece(src, n, off):
                sq = sqq if src is qh else sqb
                if samp:
                    k.op("pool", lambda: nc.gpsimd.tensor_tensor(out=sq.t[:, 0:n], in0=src.t[:, 0:n], in1=src.t[:, 0:n], op=ALU.mult), [src], [sq])
                else:
                    k.op("act", lambda: nc.scalar.activation(out=sq.t[:, 0:n], in_=src.t[:, 0:n], func=AF.Square), [src], [sq])

            def ones_piece(sq, c0, col):
                pb = psX.next()
                k.op("pe", lambda: nc.tensor.matmul(pb.t[0:1, :], lhsT=self.ones_b.t[:, 0:1], rhs=sq.t[:, c0:c0 + 512],
                                                    start=True, stop=True), [self.ones_b, sq], [pb])
                self.V(lambda: nc.vector.tensor_reduce(out=mx.t[0:1, col:col + 1], in_=pb.t[0:1, :], axis=AX.X, op=ALU.max), [pb], [mx])
            pieces.append(lambda: (sq_piece(qh, Tq, 0), sq_piece(kh, Sk, 0)))
            col = 0
            for sq, n in ((sqq, Tq), (sqb, Sk)):
                for c0 in range(0, n, 512):
                    pieces.append(lambda sq=sq, c0=c0, col=col: ones_piece(sq, c0, col))
                    col += 1
            pieces.append(lambda: bound_final(h, nq_blk, nk_blk))
            return pieces

        def bound_final(h, nq_blk, nk_blk):
            self.V(lambda: nc.vector.tensor_reduce(out=mx.t[0:1, 24:25], in_=mx.t[0:1, 0:nq_blk], axis=AX.X, op=ALU.max), [mx], [mx])
            self.V(lambda: nc.vector.tensor_reduce(out=mx.t[0:1, 25:26], in_=mx.t[0:1, nq_blk:nq_blk + nk_blk], axis=AX.X, op=ALU.max), [mx], [mx])
            self.V(lambda: nc.vector.tensor_tensor(out=mx.t[0:1, 26:27], in0=mx.t[0:1, 24:25], in1=mx.t[0:1, 25:26], op=ALU.mult), [mx], [mx])
            k.op("act", lambda: nc.scalar.activation(out=mx.t[0:1, 27:28], in_=mx.t[0:1, 26:27], func=AF.Ln, bias=1e-30), [mx], [mx])
            k.op("act", lambda: nc.scalar.activation(out=mx.t[0:1, 28:29], in_=mx.t[0:1, 27:28], func=AF.Exp, scale=0.5), [mx], [mx])
            pb = psX.next()
            k.op("pe", lambda: nc.tensor.matmul(pb.t[:, 0:1], lhsT=self.ones_f.t[0:1, :], rhs=mx.t[0:1, 28:29], start=True, stop=True),
                 [self.ones_f, mx], [pb])
            nm = negM.next()
            heads[h]["nm"] = nm
            self.V(lambda: nc.vector.tensor_scalar(out=nm.t[:], in0=pb.t[:, 0:1], scalar1=-1.02, scalar2=None, op0=ALU.mult), [pb], [nm])

        def main_unit(h, unit):
            kh, qh, nm = heads[h]["kh"], heads[h]["qh"], heads[h]["nm"]
            (q0, nq, kts) = unit
            pO = psO.next()
            nk = len(kts)

            def smat(ki):
                pS = psS.next()
                kt = kts[ki]
                k.op("pe", lambda: nc.tensor.matmul(pS.t[:, 0:nq], lhsT=kh.t[:, kt * 128:(kt + 1) * 128], rhs=qh.t[:, q0:q0 + nq],
                                                    start=True, stop=True), [kh, qh], [pS])
                pt = PT.next()
                k.op("act", lambda: nc.scalar.activation(out=pt.t[:, 0:nq], in_=pS.t[:, 0:nq], func=AF.Exp, bias=nm.t[:, 0:1], scale=1.0),
                     [pS, nm], [pt])
                return pt

            def pv(ki, pt):
                kt = kts[ki]
                k.op("pe", lambda: nc.tensor.matmul(pO.t[:, 0:nq], lhsT=V_all.t[:, kt, h * 65:h * 65 + 128], rhs=pt.t[:, 0:nq],
                                                    start=(ki == 0), stop=(ki == nk - 1)), [V_all, pt], [pO], sig=(ki == nk - 1))
            LA = 3
            pts = [smat(i_) for i_ in range(min(LA, nk))]
            for ki in range(nk):
                if ki + LA < nk:
                    pts.append(smat(ki + LA))
                pv(ki, pts[ki])
                if ki == min(5, nk - 1) and pending:
                    pending.pop(0)()
                if ki % 2 == 1 and ki >= 7 and work:
                    work.pop(0)()
            to = tmpO.next()
            if samp:
                self.V(lambda: nc.vector.tensor_copy(out=to.t[0:65, 0:nq], in_=pO.t[0:65, 0:nq]), [pO], [to])
                self.V(lambda: nc.vector.reciprocal(out=to.t[64:65, 0:nq], in_=to.t[64:65, 0:nq]), [to], [to])
            else:
                k.op("act", lambda: nc.scalar.copy(out=to.t[0:65, 0:nq], in_=pO.t[0:65, 0:nq]), [pO], [to])
                k.op("act", lambda: nc.scalar.activation(out=to.t[64:65, 0:nq], in_=to.t[64:65, 0:nq], func=AF.Ln), [to], [to])
                k.op("act", lambda: nc.scalar.activation(out=to.t[64:65, 0:nq], in_=to.t[64:65, 0:nq], func=AF.Exp, scale=-1.0), [to], [to])
            pending.append(lambda: fin(h, q0, nq, to))

        def fin(h, q0, nq, to):
            pR = psX.next()
            k.op("pe", lambda: nc.tensor.matmul(pR.t[0:64, 0:nq], lhsT=self.ones_f.t[64:65, 0:64], rhs=to.t[64:65, 0:nq],
                                                start=True, stop=True), [self.ones_f, to], [pR])
            if h % 2 == 0:
                self.V(lambda: nc.vector.tensor_tensor(out=zaT.t[0:64, h // 2, q0:q0 + nq], in0=to.t[0:64, 0:nq], in1=pR.t[0:64, 0:nq],
                                                       op=ALU.mult), [to, pR], [zaT])
            else:
                sg = stg.next()
                self.V(lambda: nc.vector.tensor_tensor(out=sg.t[0:64, 0:nq], in0=to.t[0:64, 0:nq], in1=pR.t[0:64, 0:nq], op=ALU.mult),
                       [to, pR], [sg])
                k.dma("sp", zaT.t[64:128, h // 2, q0:q0 + nq], sg.t[0:64, 0:nq], [sg], [zaT])

        pending = []
        work = []
        for p_ in build(0) + bound(0):
            p_()
        for h in range(8):
            if h + 1 < 8:
                work += build(h + 1)
                work += bound(h + 1)
            for ui, unit in enumerate(units):
                main_unit(h, unit)
                if not samp:
                    for _ in range(3):
                        if work:
                            work.pop(0)()
            while work:
                work.pop(0)()
        while pending:
            pending.pop(0)()

    def join_mlp(self, g, gi, x_dram, zmT, zaT, n_own):
        nc, k, d, o = self.nc, self.k, self.d, self.o
        A = self.alloc
        samp = (g == "s")
        y_dram = o["ys"] if samp else o["yp"]
        w = d["w_in"]
        with ExitStack() as js:
            if samp:
                G1, G2 = self.Gs
                glist = []
            else:
                G1 = A(js, g + "G1", [128, D], F32)
                G2 = A(js, g + "G2", [128, D], F32)
                glist = [(1, G1, G2), (0, self.Gs[0], self.Gs[1])]
            x1 = tiled(A(js, g + "x1", [128, 8, D], F32), 8)
            self.wring = Ring([A(js, g + "w5_%d" % i, [128, 8 * 512], BF16) for i in range(4)])
            self.xnb = Ring([A(js, g + "xnb5%d" % i, [128, D], BF16) for i in range(2)])
            self.junk = Ring([A(js, g + "junk5%d" % i, [128, 512], BF16) for i in range(1)])
            self.evt = Ring([A(js, g + "evt5%d" % i, [128, 8, 128], F32) for i in range(1)])
            tmpf = Ring([A(js, g + "tmpf%d" % i, [128, 512], F32) for i in range(3 if samp else 5)])
            tmpb = Ring([A(js, g + "tmpb%d" % i, [128, 512], BF16) for i in range(2)])

            def load_x(tb):
                for i in range(8):
                    r0 = (tb * 8 + i) * 128
                    k.dma("sp", x1.t[:, i, :], x_dram[r0:r0 + 128, :], [], [x1.tiles[i]])
            load_x(0)
            if glist:
                reps = {gi_: A(js, g + "rep%d" % gi_, [128, 8, 128], BF16) for (gi_, _, _) in glist}

            def g_begin():
                for (gi_, _, _) in glist:
                    for kc in range(8):
                        self.V(lambda kc=kc, gi_=gi_: nc.vector.tensor_scalar(out=reps[gi_].t[:, kc, :], in0=self.ones_b.t[:],
                                                                               scalar1=self.sT.t[:, kc, gi_:gi_ + 1], scalar2=None, op0=ALU.mult),
                               [self.ones_b, self.sT], [reps[gi_]])
                self.wstream([(d["w_ada"][:, mi * D + j * 512: mi * D + (j + 1) * 512], 8, 512) for mi in (2, 5) for j in range(2)], depth=3)
                self.wget(-1)

            def g_finish():
                wi = 0
                for idx, (mi, post) in enumerate(((2, "post1"), (5, "post2"))):
                    for j in range(2):
                        W, ws = self.wget(wi); wi += 1
                        t1, t2 = tmpf.next(), tmpf.next()
                        k.dma("sp", t1.t[:], d["b_ada"][mi * D + j * 512: mi * D + (j + 1) * 512].partition_broadcast(128), [], [t1])
                        k.dma("sp", t2.t[:], d[post][j * 512:(j + 1) * 512].partition_broadcast(128), [], [t2])
                        for (gi_, G1_, G2_) in glist:
                            G = (G1_, G2_)[idx]
                            pb = self.psb()
                            self.mm_group(pb.t[:, :], pb, [(reps[gi_].t[:, kc, :], W[:, kc, :]) for kc in range(8)], [reps[gi_], ws])
                            t3 = tmpf.next()
                            self.V(lambda: nc.vector.tensor_tensor(out=t3.t[:], in0=pb.t[:, :], in1=t1.t[:], op=ALU.add), [pb, t1], [t3])
                            self.V(lambda: nc.vector.tensor_tensor(out=G.t[:, j * 512:(j + 1) * 512], in0=t3.t[:], in1=t2.t[:], op=ALU.mult), [t3, t2], [G])
            for tb in range(n_own // 8):
                with ExitStack() as s1:
                    hT = tiled(A(s1, g + "hT5", [128, 8, 1024], BF16), 8)
                    Wm = A(s1, g + "Wm", [128, 4, D], BF16)
                    Wa = A(s1, g + "Wa", [128, 4, D], BF16)
                    mT = A(s1, g + "mT", [128, 8, 1024], BF16)
                    tmpj = Ring(tmpf.b + [A(s1, g + "tmpj%d" % i, [128, 512], F32) for i in range(3)])
                    if tb > 0:
                        load_x(tb)
                    k.dma("pool", Wm.t[:], d["w_mlstm_o"].rearrange("(k p) n -> p k n", p=128), [], [Wm])
                    k.dma("pool", Wa.t[:], d["w_mla_o"].rearrange("(k p) n -> p k n", p=128), [], [Wa])
                    reqs = [(w[:, C_O:C_O + 512], 8, 512)]
                    for fg in range(2):
                        reqs += [(w[:, C_MG + fg * 512:C_MG + (fg + 1) * 512], 8, 512),
                                 (w[:, C_MG + D + fg * 512:C_MG + D + (fg + 1) * 512], 8, 512)]
                    reqs += [(d["w_out"][:, 0:512], 8, 512), (d["w_out"][:, 512:1024], 8, 512)]
                    if tb == 0 and glist:
                        g_begin()
                        for i in range(8):
                            self.norm_transpose(x1.t[:, i, :], x1.tiles[i], gi, 0, hT, i * 128)
                        g_finish()
                        self.wstream(reqs, depth=3)
                        self.wget(-1)
                    else:
                        self.wstream(reqs, depth=3)
                        self.wget(-1)
                        for i in range(8):
                            self.norm_transpose(x1.t[:, i, :], x1.tiles[i], gi, 0, hT, i * 128)
                    W, ws = self.wget(0)
                    for h in range(4):
                        for half in range(2):
                            pb = self.psb()
                            self.mm_group(pb.t[:, :], pb, [(W[:, kc, h * 128:(h + 1) * 128], hT.t[:, kc, half * 512:(half + 1) * 512])
                                                           for kc in range(8)], [ws] + hT.tiles[half * 4:half * 4 + 4])
                            e_ = tmpj.next()
                            k.op("act", lambda: nc.scalar.activation(out=e_.t[:], in_=pb.t[:, :], func=AF.Exp, scale=-1.0), [pb], [e_])
                            k.op("act", lambda: nc.scalar.activation(out=e_.t[:], in_=e_.t[:], func=AF.Ln, bias=1.0), [e_], [e_])
                            k.op("act", lambda: nc.scalar.activation(out=e_.t[:], in_=e_.t[:], func=AF.Exp, scale=-1.0), [e_], [e_])
                            c0 = tb * 1024 + half * 512
                            self.V(lambda: nc.vector.tensor_tensor(out=zmT.t[:, h, c0:c0 + 512], in0=zmT.t[:, h, c0:c0 + 512], in1=e_.t[:],
                                                                   op=ALU.mult), [zmT, e_], [zmT])
                    for fg in range(2):
                        Wga, wsa = self.wget(1 + 2 * fg)
                        Wgb, wsb = self.wget(2 + 2 * fg)
                        for c4 in range(4):
                            fc = fg * 4 + c4
                            for half in range(2):
                                c0 = tb * 1024 + half * 512
                                hc = slice(half * 512, (half + 1) * 512)
                                pM, pA, pGa, pGb = self.psb(), self.psb(), self.psb(), self.psb()
                                self.mm_group(pM.t[:, :], pM, [(Wm.t[:, kc, fc * 128:(fc + 1) * 128], zmT.t[:, kc, c0:c0 + 512]) for kc in range(4)], [Wm, zmT])
                                self.mm_group(pA.t[:, :], pA, [(Wa.t[:, kc, fc * 128:(fc + 1) * 128], zaT.t[:, kc, c0:c0 + 512]) for kc in range(4)], [Wa, zaT])
                                self.mm_group(pGa.t[:, :], pGa, [(Wga[:, kc, c4 * 128:(c4 + 1) * 128], hT.t[:, kc, hc]) for kc in range(8)], [wsa] + hT.tiles[half * 4:half * 4 + 4])
                                self.mm_group(pGb.t[:, :], pGb, [(Wgb[:, kc, c4 * 128:(c4 + 1) * 128], hT.t[:, kc, hc]) for kc in range(8)], [wsb] + hT.tiles[half * 4:half * 4 + 4])
                                ea, eb = tmpj.next(), tmpj.next()
                                k.op("act", lambda: nc.scalar.activation(out=ea.t[:], in_=pGa.t[:, :], func=AF.Exp, scale=-1.0), [pGa], [ea])
                                k.op("act", lambda: nc.scalar.activation(out=eb.t[:], in_=pGb.t[:, :], func=AF.Exp, scale=-1.0), [pGb], [eb])
                                k.op("act", lambda: nc.scalar.activation(out=ea.t[:], in_=ea.t[:], func=AF.Ln, bias=1.0), [ea], [ea])
                                k.op("act", lambda: nc.scalar.activation(out=ea.t[:], in_=ea.t[:], func=AF.Exp, scale=-1.0), [ea], [ea])
                                k.op("act", lambda: nc.scalar.activation(out=eb.t[:], in_=eb.t[:], func=AF.Ln, bias=1.0), [eb], [eb])
                                k.op("act", lambda: nc.scalar.activation(out=eb.t[:], in_=eb.t[:], func=AF.Exp, scale=-1.0), [eb], [eb])
                                self.V(lambda: nc.vector.tensor_tensor(out=ea.t[:], in0=pM.t[:, :], in1=ea.t[:], op=ALU.mult), [pM, ea], [ea])
                                self.V(lambda: nc.vector.tensor_tensor(out=eb.t[:], in0=pA.t[:, :], in1=eb.t[:], op=ALU.mult), [pA, eb], [eb])
                                self.V(lambda: nc.vector.tensor_tensor(out=mT.t[:, fc, hc], in0=ea.t[:], in1=eb.t[:], op=ALU.add), [ea, eb], [mT])
                    W0, ws0 = self.wget(5)
                    W1, ws1 = self.wget(6)
                    for i in range(8):
                        p0, p1 = self.psb(), self.psb()
                        self.mm_group(p0.t[:, :], p0, [(mT.t[:, kc, i * 128:(i + 1) * 128], W0[:, kc, :]) for kc in range(8)], [mT, ws0])
                        self.mm_group(p1.t[:, :], p1, [(mT.t[:, kc, i * 128:(i + 1) * 128], W1[:, kc, :]) for kc in range(8)], [mT, ws1])
                        self.post_residual(x1, i, [p0.t[:, :], p1.t[:, :]], [p0, p1], G1, tmpf)
                with ExitStack() as s2:
                    h2T = tiled(A(s2, g + "h2T", [128, 8, 1024], BF16), 8)
                    ff0b = [Buf(None) for _ in range(8)]
                    aT = A(s2, g + "aT", [128, 32, 1024], BF16)
                    self.wstream([(d["w_mlp1"][:, t * 512:(t + 1) * 512], 8, 512) for t in range(8)] +
                                 [(d["w_mlp2"][gq * 1024:(gq + 1) * 1024, j * 512:(j + 1) * 512], 8, 512) for j in range(2) for gq in range(4)], depth=3)
                    self.wget(-1)
                    for i in range(8):
                        self.norm_transpose(x1.t[:, i, :], x1.tiles[i], gi, 1, h2T, i * 128)
                    def mlp1_blk(t, W, ws, half):
                        hc = slice(half * 512, (half + 1) * 512)
                        for c4 in range(4):
                            pb = self.psb()
                            self.mm_group(pb.t[:, :], pb, [(W[:, kc, c4 * 128:(c4 + 1) * 128], h2T.t[:, kc, hc]) for kc in range(8)],
                                          [ws] + h2T.tiles[half * 4:half * 4 + 4])
                            dst = aT.t[:, t * 4 + c4, hc]
                            r_ = tmpb.next()
                            k.op("act", lambda: nc.scalar.activation(out=r_.t[:], in_=pb.t[:, :], func=AF.Relu), [pb], [r_])
                            self.V(lambda: nc.vector.tensor_tensor(out=dst, in0=r_.t[:], in1=r_.t[:], op=ALU.mult), [r_], [aT])
                    w01 = [self.wget(0), self.wget(1)]
                    for half in range(2):
                        for t in range(2):
                            mlp1_blk(t, w01[t][0], w01[t][1], half)
                    for t in range(2, 8):
                        W, ws = self.wget(t)
                        for half in range(2):
                            mlp1_blk(t, W, ws, half)
                    ff0 = h2T.t[:].rearrange("p a b -> p (a b)").bitcast(F32).rearrange("p (i c) -> p i c", c=512)
                    for j in range(2):
                        for gq in range(4):
                            W, ws = self.wget(8 + j * 4 + gq)
                            for i in range(8):
                                pb = self.psum[i]
                                for kc in range(8):
                                    first = (gq == 0 and kc == 0)
                                    last = (gq == 3 and kc == 7)
                                    k.op("pe", lambda kc=kc, first=first, last=last: nc.tensor.matmul(
                                        pb.t[:, :], lhsT=aT.t[:, gq * 8 + kc, i * 128:(i + 1) * 128], rhs=W[:, kc, :], start=first, stop=last),
                                        [aT, ws], [pb], sig=(kc == 7))
                        if j == 0:
                            for i in range(8):
                                self.copy("act" if i % 2 else "dve", ff0[:, i, :], self.psum[i].t[:, :], [self.psum[i]], [ff0b[i]] + h2T.tiles)
                        else:
                            for i in range(8):
                                self.post_residual(x1, i, [ff0[:, i, :], self.psum[i].t[:, :]], [ff0b[i], self.psum[i]], G2, tmpf)
                                r0 = (tb * 8 + i) * 128
                                k.dma("sp", y_dram[r0:r0 + 128, :], x1.t[:, i, :], [x1.tiles[i]], [])
                    k.barrier()

    def post_residual(self, x1, i, halves, hbufs, G, tmpf):
        nc, k = self.nc, self.k
        ss = self.smr.next()
        junk = self.junk.next()
        for j in range(2):
            k.op("act", lambda j=j: nc.scalar.activation(out=junk.t[:, 0:512], in_=halves[j], func=AF.Square,
                                                         accum_out=ss.t[:, 4 + j:5 + j]), [hbufs[j]], [junk, ss])
        self.V(lambda: nc.vector.tensor_tensor(out=ss.t[:, 0:1], in0=ss.t[:, 4:5], in1=ss.t[:, 5:6], op=ALU.add), [ss], [ss])
        rstd = self.rstd_from_ss(ss, 0, D, [])
        for j in range(2):
            t_ = tmpf.next()
            self.V(lambda j=j: nc.vector.scalar_tensor_tensor(out=t_.t[:], in0=halves[j], scalar=rstd, in1=G.t[:, j * 512:(j + 1) * 512],
                                                              op0=ALU.mult, op1=ALU.mult), [hbufs[j], ss, G], [t_])
            self.V(lambda j=j: nc.vector.tensor_tensor(out=x1.t[:, i, j * 512:(j + 1) * 512], in0=x1.t[:, i, j * 512:(j + 1) * 512],
                                                       in1=t_.t[:], op=ALU.add), [x1.tiles[i], t_], [x1.tiles[i]])


_CACHE = {}


def _rope_tables(rev):
    t = np.arange(NST)
    pos = (NST - 1 - t) if rev else t
    row = (pos // 64).astype(np.float32)
    col = (pos % 64).astype(np.float32)
    inv = (10000.0 ** (-np.arange(0, 16, 2, dtype=np.float32) / 16)).astype(np.float32)
    ang = np.concatenate([row[:, None] * inv, col[:, None] * inv], axis=-1).astype(np.float32)
    cos, sin = np.cos(ang).astype(np.float32), np.sin(ang).astype(np.float32)
    qtab = np.zeros((64, NSO), np.float32)
    qtab[0:32:2] = cos[:NSO].T
    qtab[1:32:2] = cos[:NSO].T
    qtab[32:64:2] = -sin[:NSO].T
    qtab[33:64:2] = sin[:NSO].T
    return cos, sin, qtab


def _consts():
    ident = np.eye(128, dtype=np.float32)
    s_, t_ = np.meshgrid(np.arange(128), np.arange(128), indexing="ij")
    trif = (s_ <= t_).astype(np.float32)
    trib = (s_ >= t_).astype(np.float32)
    esel = np.zeros((32, 128), np.float32)
    esel[np.arange(32), 64 + np.arange(32)] = 1.0
    esel[np.arange(32), 96 + np.arange(32)] = 1.0
    return ident, trif, trib, esel


def make_in_maps(inp):
    f = lambda a: np.ascontiguousarray(np.asarray(a, dtype=np.float32))
    ident, trif, trib, esel = _consts()
    w_in = f(inp["w_in"][0])
    w_in_sw = w_in.copy()
    gsw = w_in[:, C_G:C_G + 16].reshape(D, 2, 8)[:, ::-1, :].reshape(D, 16)
    w_in_sw[:, C_G:C_G + 16] = gsw
    gate_b = f(inp["mlstm_gate_b"][0]).reshape(2, 8)
    w_uq = f(inp["w_uq"][0]).reshape(384, 8, 96)
    sw = w_uq[:, :, 64:96].reshape(384, 8, 16, 2)[:, :, :, ::-1].reshape(384, 8, 32)
    w_ukv = f(inp["w_ukv"][0])
    vecs = np.concatenate([f(inp["b_ada"][0]).reshape(48, 128), f(inp["norm_pre1"][0]).reshape(8, 128),
                           f(inp["norm_pre2"][0]).reshape(8, 128), f(inp["mlstm_head_norm"][0]).reshape(4, 128)], axis=0)
    common = dict(
        w_ada=f(inp["w_ada"][0]), vecs=f(vecs), b_ada=f(inp["b_ada"][0]), post1=f(inp["norm_post1"][0]), post2=f(inp["norm_post2"][0]),
        q_norm=f(inp["mla_q_norm"][0]), kv_norm=f(inp["mla_kv_norm"][0]),
        w_uq=f(w_uq.reshape(384, 768)), w_uq_sw=f(sw), w_uk=f(w_ukv[:, :, 0:64]), w_uv=f(w_ukv[:, :, 64:128].reshape(256, 512)),
        w_mla_o=f(inp["w_mla_o"][0]), w_mlstm_o=f(inp["w_mlstm_o"][0]), w_out=f(inp["w_out"][0]),
        w_mlp1=f(inp["w_mlp1"][0]), w_mlp2=f(inp["w_mlp2"][0]), ident=ident, trif=trif, trib=trib, esel=esel,
    )
    tabs = {False: _rope_tables(False), True: _rope_tables(True)}
    maps = []
    for r in range(8):
        b, rev = r // 2, (r % 2 == 1)
        xs = np.asarray(inp["x_sample"][b], np.float32)
        xp = np.asarray(inp["x_prompt"][4 * r:4 * r + 4], np.float32)
        C0 = np.asarray(inp["state_mlstm_C"][b, 0], np.float32)
        n0 = np.asarray(inp["state_mlstm_n"][b, 0], np.float32)
        m0 = np.asarray(inp["state_mlstm_m"][b, 0], np.float32)
        if rev:
            xs, xp, C0, n0, m0 = xs[::-1], xp[:, ::-1], C0[::-1], n0[::-1], m0[::-1]
        cos, sin, qtab = tabs[rev]
        m = dict(common)
        m.update(
            xs=f(xs), xp=f(xp.reshape(NTP, D)), cckv=f(inp["cache_mla_ckv"][b, 0]), ckr=f(inp["cache_mla_krope"][b, 0]),
            C0=f(C0), n0=f(n0), m0=f(m0.reshape(8)), cvec=f(np.stack([np.asarray(inp["c"][b]), np.asarray(inp["c_ctx"])])),
            w_in=(w_in_sw if rev else w_in), gate_b=f((gate_b[::-1] if rev else gate_b).reshape(16)),
            qtab=f(qtab), kcos=f(cos), ksin=f(sin),
        )
        maps.append(m)
    return maps


def assemble(results):
    y_p = np.zeros((32, 256, D), np.float32)
    y_s = np.zeros((4, 4096, D), np.float32)
    ckv = np.zeros((32, 1, 256, 256), np.float32)
    kr = np.zeros((32, 1, 256, 32), np.float32)
    Cn = np.zeros((32, 1, 2, 4, 128, 128), np.float32)
    nn = np.zeros((32, 1, 2, 4, 128), np.float32)
    mm = np.zeros((32, 1, 2, 4), np.float32)
    for r, res in enumerate(results):
        b, rev = r // 2, (r % 2 == 1)
        yp = np.asarray(res["yp"]).reshape(4, 256, D)
        ock = np.asarray(res["o_ckv"]).reshape(4, 256, 256)
        okr = np.asarray(res["o_kr"]).reshape(4, 256, 32)
        oC = np.asarray(res["o_C"]).reshape(4, 2, 4, 128, 128)
        on = np.asarray(res["o_n"]).reshape(4, 2, 4, 128)
        om = np.asarray(res["o_m"]).reshape(4, 2, 4)
        ys = np.asarray(res["ys"])
        if rev:
            yp, ock, okr, oC, on, om, ys = yp[:, ::-1], ock[:, ::-1], okr[:, ::-1], oC[:, ::-1], on[:, ::-1], om[:, ::-1], ys[::-1]
            y_s[b, 2048:] = ys
        else:
            y_s[b, :2048] = ys
        sl = slice(4 * r, 4 * r + 4)
        y_p[sl], ckv[sl, 0], kr[sl, 0], Cn[sl, 0], nn[sl, 0], mm[sl, 0] = yp, ock, okr, oC, on, om
    return (y_p, y_s, ckv, kr, Cn, nn, mm)


def kernel(**inputs):
    if "nc" not in _CACHE:
        _CACHE["nc"] = Prog().build()
    maps = make_in_maps(inputs)
    res = run_bass_kernel_spmd(_CACHE["nc"], maps, core_ids=list(range(8)))
    return assemble(res.results)
```

```python
import numpy as np
from contextlib import ExitStack
import concourse.bass as bass
import concourse.mybir as mybir
from concourse.bass_utils import run_bass_kernel_spmd

F32 = mybir.dt.float32
BF16 = mybir.dt.bfloat16
AF = mybir.ActivationFunctionType
ALU = mybir.AluOpType
AX = mybir.AxisListType

D = 1024
NTP = 1024
NSO = 2048
NST = 4096
PAST = 512
EPS = 1e-6
IN_COLS = 4784
C_Q, C_K, C_V, C_O, C_G, C_CQ, C_CKV, C_KR, C_MG = 0, 512, 1024, 1536, 2048, 2064, 2448, 2704, 2736
SCQ = 96.0 ** -0.5


class Res:
    __slots__ = ("w", "r", "rd")

    def __init__(self):
        self.w = []
        self.r = {}
        self.rd = []


class Buf:
    def __init__(self, t, excl=False):
        self.t = t
        self.res = Res()
        self.excl = excl

    def __getitem__(self, k):
        return self.t[k]


def tiled(buf, n):
    buf.tiles = [Buf(buf.t) for _ in range(n)]
    return buf


class KB:
    ENG = ("pe", "act", "dve", "pool", "sp")

    def __init__(self, nc, es):
        self.nc = nc
        self.es = es
        self.e = {"pe": nc.tensor, "act": nc.scalar, "dve": nc.vector, "pool": nc.gpsimd, "sp": nc.sync}
        self.sem = {k: es.enter_context(nc.semaphore("s_" + k)) for k in self.ENG}
        self.cnt = {k: 0 for k in self.ENG}
        self.known = {k: {} for k in self.ENG}
        self.dsl = {}
        for q, n in (("sp", 24), ("pool", 12), ("act", 4)):
            self.dsl[q] = [{"sem": es.enter_context(nc.semaphore("d_%s%d" % (q, i))), "tot": 0, "i": i}
                           for i in range(n)]
        self.dptr = {"sp": 0, "pool": 0, "act": 0}
        self.n_inst = 0
        self.snap = {}

    def need(self, e, ev):
        kind, key, n = ev
        kk = (kind, key)
        if self.known[e].get(kk, 0) >= n:
            return None
        if kind == "eng":
            if key == e and e in ("pe", "sp"):
                return None
            sem = self.sem[key]
        else:
            q, i = key
            sem = self.dsl[q][i]["sem"]
        self.known[e][kk] = n
        sn = self.snap.get(ev)
        if sn:
            kn = self.known[e]
            for k2, v2 in sn.items():
                if kn.get(k2, 0) < v2:
                    kn[k2] = v2
        return (sem, n)

    def wait(self, e, ev):
        w = self.need(e, ev)
        if w is not None:
            self.e[e].wait_ge(w[0], w[1])
            self.n_inst += 1

    def waits_attach(self, e, evs):
        ws = []
        for ev in evs:
            w = self.need(e, ev)
            if w is not None:
                ws.append(w)
        best = {}
        for sem, n in ws:
            key = id(sem)
            if key not in best or best[key][1] < n:
                best[key] = (sem, n)
        ws = list(best.values())
        for sem, n in ws[:-1]:
            self.e[e].wait_ge(sem, n)
            self.n_inst += 1
        return ws[-1] if ws else None

    def _deps(self, R, W, e=None):
        deps = []
        for b in R:
            deps += b.res.w
            if b.excl:
                deps += [("eng", k, v) for k, v in b.res.r.items() if k != e]
        for b in W:
            rs = b.res
            deps += rs.w
            deps += [("eng", k, v) for k, v in rs.r.items()]
            deps += rs.rd
        return deps

    def _mark(self, ev, R, W):
        for b in R:
            if ev[0] == "eng":
                b.res.r[ev[1]] = ev[2]
            else:
                b.res.rd.append(ev)
        for b in W:
            b.res.w = [ev]
            b.res.r = {}
            b.res.rd = []

    def op(self, e, fn, R=(), W=(), sig=True):
        last = self.waits_attach(e, self._deps(R, W, e))
        n0 = self.nc.n_instructions()
        ins = fn()
        assert self.nc.n_instructions() - n0 == 1, "multi-instruction op"
        if last is not None:
            ins._wait_ge(last[0], last[1])
        self.n_inst += 1
        if sig:
            self.cnt[e] += 1
            ins.then_inc(self.sem[e], 1)
            ev = ("eng", e, self.cnt[e])
            sn = dict(self.known[e])
            sn[("eng", e)] = self.cnt[e]
            self.snap[ev] = sn
        else:
            ev = ("eng", e, self.cnt[e] + 1)
        self._mark(ev, R, W)
        return ev

    def dma(self, q, out, in_, R=(), W=()):
        sl = self.dsl[q][self.dptr[q]]
        self.dptr[q] = (self.dptr[q] + 1) % len(self.dsl[q])
        deps = self._deps(R, W)
        if sl["tot"] > 0:
            deps.append(("dma", (q, sl["i"]), sl["tot"]))
        last = self.waits_attach(q, deps)
        sl["tot"] += 16
        ins = self.e[q].dma_start(out=out, in_=in_)
        if last is not None:
            ins._wait_ge(last[0], last[1])
        ins.then_inc(sl["sem"], 16)
        self.n_inst += 1
        ev = ("dma", (q, sl["i"]), sl["tot"])
        self.snap[ev] = dict(self.known[q])
        self._mark(ev, R, W)
        return ev

    def barrier(self):
        evs = [("eng", k, self.cnt[k]) for k in ("pe", "act", "dve", "pool") if self.cnt[k] > 0]
        for q in self.dsl:
            for sl in self.dsl[q]:
                if sl["tot"] > 0:
                    evs.append(("dma", (q, sl["i"]), sl["tot"]))
        for e in self.ENG:
            for ev in evs:
                self.wait(e, ev)


class Ring:
    def __init__(self, bufs):
        self.b = bufs
        self.i = 0

    def next(self):
        b = self.b[self.i]
        self.i = (self.i + 1) % len(self.b)
        return b


class Prog:
    def __init__(self, do_prompt=True, do_sample=True, stop=None):
        self.do_prompt = do_prompt
        self.do_sample = do_sample
        self.stop = stop

    def ck(self, name):
        if self.stop == name:
            self.stopped = True
        return getattr(self, "stopped", False)

    def alloc(self, es, name, shape, dt):
        self._an = getattr(self, "_an", 0) + 1
        t = es.enter_context(self.nc.sbuf_tensor("sb%d_%s" % (self._an, name), list(shape), dt))
        return Buf(t)

    def dram_in(self, name, shape, dt=F32):
        return self.nc.dram_tensor(name, list(shape), dt, kind="ExternalInput").ap()

    def dram_out(self, name, shape, dt=F32):
        return self.nc.dram_tensor(name, list(shape), dt, kind="ExternalOutput").ap()

    def V(self, fn, R=(), W=(), e="dve"):
        return self.k.op(e, fn, R, W)

    def psb(self):
        return self.psr.next()

    def evac_engine(self):
        self._ev = 1 - getattr(self, "_ev", 0)
        return "act" if self._ev else "dve"

    def copy(self, e, out, in_, R, W, scale=None):
        nc = self.nc
        if e == "act":
            if scale is None:
                return self.k.op("act", lambda: nc.scalar.copy(out=out, in_=in_), R, W)
            return self.k.op("act", lambda: nc.scalar.activation(out=out, in_=in_, func=AF.Copy, scale=scale), R, W)
        eng = nc.vector if e == "dve" else nc.gpsimd
        if scale is None:
            return self.k.op(e, lambda: eng.tensor_copy(out=out, in_=in_), R, W)
        return self.k.op(e, lambda: eng.tensor_scalar(out=out, in0=in_, scalar1=scale, scalar2=None, op0=ALU.mult), R, W)

    def rstd_from_ss(self, ss, col, n, R):
        nc = self.nc
        self.k.op("act", lambda: nc.scalar.activation(out=ss.t[:, col + 1:col + 2], in_=ss.t[:, col:col + 1],
                                                      func=AF.Ln, scale=1.0 / n, bias=EPS), [ss], [ss])
        self.k.op("act", lambda: nc.scalar.activation(out=ss.t[:, col + 2:col + 3], in_=ss.t[:, col + 1:col + 2],
                                                      func=AF.Exp, scale=-0.5), [ss], [ss])
        return ss.t[:, col + 2:col + 3]

    def wstream(self, reqs, depth=2):
        self._wreqs = list(reqs)
        self._wissued = 0
        self._wdepth = depth
        self._wtiles = {}

    def wget(self, i):
        while self._wissued < len(self._wreqs) and self._wissued <= i + self._wdepth - 1:
            j = self._wissued
            ap, kc, ncols = self._wreqs[j]
            slot = self.wring.next()
            view = slot.t[:, 0:kc * ncols].rearrange("p (k n) -> p k n", k=kc)
            self.k.dma("pool", view, ap.rearrange("(k p) n -> p k n", p=128), [], [slot])
            self._wtiles[j] = (view, slot)
            self._wissued += 1
        return self._wtiles.pop(i) if i >= 0 else None

    def build(self):
        nc = bass.Bass("TRN2", target_bir_lowering=False)
        self.nc = nc
        di = self.dram_in
        self.d = dict(
            xs=di("xs", [NST, D]), xp=di("xp", [NTP, D]),
            cckv=di("cckv", [PAST, 256]), ckr=di("ckr", [PAST, 32]),
            C0=di("C0", [2, 4, 128, 128]), n0=di("n0", [2, 4, 128]), m0=di("m0", [8]),
            cvec=di("cvec", [2, D]), w_ada=di("w_ada", [D, 6 * D]),
            vecs=di("vecs", [68, 128]),
            b_ada=di("b_ada", [6 * D]), post1=di("post1", [D]), post2=di("post2", [D]),
            w_in=di("w_in", [D, IN_COLS]), gate_b=di("gate_b", [16]),
            q_norm=di("q_norm", [384]), kv_norm=di("kv_norm", [256]),
            w_uq=di("w_uq", [384, 768]), w_uq_sw=di("w_uq_sw", [384, 8, 32]),
            w_uk=di("w_uk", [256, 8, 64]), w_uv=di("w_uv", [256, 512]),
            w_mla_o=di("w_mla_o", [512, D]), w_mlstm_o=di("w_mlstm_o", [512, D]),
            w_out=di("w_out", [D, D]), w_mlp1=di("w_mlp1", [D, 4 * D]), w_mlp2=di("w_mlp2", [4 * D, D]),
            ident=di("ident", [128, 128]), trif=di("trif", [128, 128]), trib=di("trib", [128, 128]),
            esel=di("esel", [32, 128]), qtab=di("qtab", [64, NSO]), kcos=di("kcos", [NST, 16]), ksin=di("ksin", [NST, 16]),
        )
        do = self.dram_out
        self.o = dict(
            ys=do("ys", [NSO, D]), yp=do("yp", [NTP, D]), o_ckv=do("o_ckv", [NTP, 256]), o_kr=do("o_kr", [NTP, 32]),
            o_C=do("o_C", [4, 8, 128, 128]), o_n=do("o_n", [4, 8, 128]), o_m=do("o_m", [4, 8]),
        )
        with ExitStack() as es:
            self.k = KB(nc, es)
            self.psum = [Buf(es.enter_context(nc.psum_tensor("pb%d" % i, [128, 512], F32)), excl=True) for i in range(8)]
            self.psr = Ring(self.psum)
            self.persistent(es)
            self.phase0()
            if not self.ck("p0"):
                if self.do_prompt:
                    self.run_group("p")
                if self.do_sample and not self.ck("-"):
                    self.run_group("s")
            self.k.barrier()
        return nc

    def persistent(self, es):
        nc, k, A = self.nc, self.k, self.alloc
        d = self.d
        self.ident_f = A(es, "ident_f", [128, 128], F32)
        self.ident_b = A(es, "ident_b", [128, 128], BF16)
        self.trif = A(es, "trif", [128, 128], F32)
        self.trib = A(es, "trib", [128, 128], F32)
        self.ones_f = A(es, "ones_f", [128, 128], F32)
        self.ones_b = A(es, "ones_b", [128, 128], BF16)
        self.modAB = A(es, "modAB", [128, 4, 8, 2], F32)
        self.hnT = A(es, "hnT", [128, 4], F32)
        self.sT = A(es, "sT", [128, 8, 2], F32)
        self.qn_row = A(es, "qn_row", [128, 384], F32)
        self.kvn_row = A(es, "kvn_row", [128, 256], F32)
        self.gb_row = A(es, "gb_row", [128, 16], F32)
        self.smr = Ring([A(es, "sm%d" % i, [128, 16], F32) for i in range(16)])
        self.Gs = (A(es, "Gs1", [128, D], F32), A(es, "Gs2", [128, D], F32))
        k.dma("sp", self.ident_f.t[:], d["ident"], [], [self.ident_f])
        k.dma("sp", self.trif.t[:], d["trif"], [], [self.trif])
        k.dma("sp", self.trib.t[:], d["trib"], [], [self.trib])
        k.dma("sp", self.qn_row.t[:], d["q_norm"].partition_broadcast(128), [], [self.qn_row])
        k.dma("sp", self.kvn_row.t[:], d["kv_norm"].partition_broadcast(128), [], [self.kvn_row])
        k.dma("sp", self.gb_row.t[:], d["gate_b"].partition_broadcast(128), [], [self.gb_row])
        self.V(lambda: nc.vector.tensor_copy(out=self.ident_b.t[:], in_=self.ident_f.t[:]), [self.ident_f], [self.ident_b])
        self.V(lambda: nc.vector.memset(self.ones_f.t[:], 1.0), [], [self.ones_f])
        self.V(lambda: nc.vector.memset(self.ones_b.t[:], 1.0), [], [self.ones_b])

    def phase0(self):
        nc, k, d = self.nc, self.k, self.d
        with ExitStack() as es:
            A = self.alloc
            self.wring = Ring([A(es, "w0_%d" % i, [128, 8 * 512], BF16) for i in range(4)])
            cv = A(es, "cv", [2, D], F32)
            ce = A(es, "ce", [2, D], F32)
            vt = A(es, "vt", [68, 128], F32)
            vT = A(es, "vT", [128, 68], F32)
            sTb = A(es, "sTb", [128, 8, 2], BF16)
            modT = A(es, "modT", [128, 4, 8, 2], F32)
            k.dma("sp", cv.t[:], d["cvec"], [], [cv])
            k.dma("sp", vt.t[:], d["vecs"], [], [vt])
            k.op("act", lambda: nc.scalar.activation(out=ce.t[:], in_=cv.t[:], func=AF.Exp, scale=-1.0), [cv], [ce])
            self.V(lambda: nc.vector.tensor_scalar(out=ce.t[:], in0=ce.t[:], scalar1=1.0, scalar2=None, op0=ALU.add), [ce], [ce])
            self.V(lambda: nc.vector.reciprocal(out=ce.t[:], in_=ce.t[:]), [ce], [ce])
            self.V(lambda: nc.vector.tensor_tensor(out=cv.t[:], in0=cv.t[:], in1=ce.t[:], op=ALU.mult), [cv, ce], [cv])
            pb = self.psb()
            for kc in range(8):
                k.op("pe", lambda kc=kc: nc.tensor.transpose(out=pb.t[:, kc * 2:kc * 2 + 2], in_=cv.t[:, kc * 128:(kc + 1) * 128],
                                                             identity=self.ident_f.t[0:2, 0:2]), [cv, self.ident_f], [pb])
            self.V(lambda: nc.vector.tensor_copy(out=self.sT.t[:], in_=pb.t[:, 0:16].rearrange("p (k g) -> p k g", g=2)), [pb], [self.sT])
            self.V(lambda: nc.vector.tensor_copy(out=sTb.t[:], in_=self.sT.t[:]), [self.sT], [sTb])
            pb2 = self.psb()
            k.op("pe", lambda: nc.tensor.transpose(out=pb2.t[:, 0:68], in_=vt.t[:], identity=self.ident_f.t[0:68, 0:68]),
                 [vt, self.ident_f], [pb2])
            self.V(lambda: nc.vector.tensor_copy(out=vT.t[:], in_=pb2.t[:, 0:68]), [pb2], [vT])
            self.V(lambda: nc.vector.tensor_copy(out=self.hnT.t[:], in_=vT.t[:, 64:68]), [vT], [self.hnT])
            mods = (0, 1, 3, 4)
            self.wstream([(d["w_ada"][:, mi * D + j * 512: mi * D + (j + 1) * 512], 8, 512) for mi in mods for j in range(2)])
            pm = self.psb()
            wi = 0
            for a, mi in enumerate(mods):
                for j in range(2):
                    W, ws = self.wget(wi)
                    wi += 1
                    for c4 in range(4):
                        fc = j * 4 + c4
                        for kc in range(8):
                            k.op("pe", lambda kc=kc, c4=c4, a=a, fc=fc, W=W: nc.tensor.matmul(
                                pm.t[:, (a * 8 + fc) * 2:(a * 8 + fc) * 2 + 2], lhsT=W[:, kc, c4 * 128:(c4 + 1) * 128],
                                rhs=sTb.t[:, kc, :], start=(kc == 0), stop=(kc == 7)), [ws, sTb], [pm], sig=(kc == 7))
            for a, mi in enumerate(mods):
                self.V(lambda a=a, mi=mi: nc.vector.tensor_tensor(
                    out=modT.t[:, a], in0=pm.t[:, a * 16:(a + 1) * 16].rearrange("p (f g) -> p f g", g=2),
                    in1=vT.t[:, mi * 8:(mi + 1) * 8].unsqueeze(2).broadcast_to([128, 8, 2]), op=ALU.add), [pm, vT], [modT])
            for blk, (a_shift, a_scale, prow) in enumerate(((0, 1, 48), (2, 3, 56))):
                self.V(lambda a_scale=a_scale, prow=prow, blk=blk: nc.vector.scalar_tensor_tensor(
                    out=self.modAB.t[:, 2 * blk], in0=modT.t[:, a_scale], scalar=1.0,
                    in1=vT.t[:, prow:prow + 8].unsqueeze(2).broadcast_to([128, 8, 2]), op0=ALU.add, op1=ALU.mult),
                    [modT, vT], [self.modAB])
                self.V(lambda a_shift=a_shift, blk=blk: nc.vector.tensor_copy(out=self.modAB.t[:, 2 * blk + 1], in_=modT.t[:, a_shift]),
                       [modT], [self.modAB])
            k.barrier()

    def norm_transpose(self, x_ap, x_buf, gi, blk, hT, col0):
        nc, k = self.nc, self.k
        ss = self.smr.next()
        xnb = self.xnb.next()
        k.op("act", lambda: nc.scalar.activation(out=xnb.t[:], in_=x_ap, func=AF.Square, accum_out=ss.t[:, 0:1]),
             [x_buf], [xnb, ss])
        rstd = self.rstd_from_ss(ss, 0, D, [])
        k.op("act", lambda: nc.scalar.activation(out=xnb.t[:], in_=x_ap, func=AF.Copy, scale=rstd), [x_buf, ss], [xnb])
        pb = self.psb()
        pv = pb.t[:].bitcast(BF16).rearrange("p (f t) -> p f t", t=128)
        for fc in range(8):
            k.op("pe", lambda fc=fc: nc.tensor.transpose(out=pv[:, fc, :], in_=xnb.t[:, fc * 128:(fc + 1) * 128],
                                                         identity=self.ident_b.t[:]), [xnb, self.ident_b], [pb], sig=(fc == 7))
        A = self.modAB.t[:, 2 * blk]
        B = self.modAB.t[:, 2 * blk + 1]
        evt = self.evt.next()
        self.V(lambda: nc.vector.tensor_tensor(out=evt.t[:], in0=pv, in1=A[:, :, gi:gi + 1].broadcast_to([128, 8, 128]), op=ALU.mult),
               [pb, self.modAB], [evt])
        self.V(lambda: nc.vector.tensor_tensor(out=hT.t[:, :, col0:col0 + 128], in0=evt.t[:], in1=B[:, :, gi:gi + 1].broadcast_to([128, 8, 128]),
                                               op=ALU.add), [evt, self.modAB], [hT.tiles[col0 // 128]])

    def mm_group(self, out_ap, pb, pairs, R):
        nc, k = self.nc, self.k
        n = len(pairs)
        ev = None
        for i, (l, r) in enumerate(pairs):
            ev = k.op("pe", lambda l=l, r=r, i=i: nc.tensor.matmul(out_ap, lhsT=l, rhs=r, start=(i == 0), stop=(i == n - 1)),
                      R, [pb], sig=(i == n - 1))
        return ev

    def run_group(self, g):
        nc, k, d, o = self.nc, self.k, self.d, self.o
        A = self.alloc
        samp = (g == "s")
        gi = 0 if samp else 1
        n_own = 16 if samp else 8
        nt = 32 if samp else 8
        nkey = 36 if samp else 8
        x_own = d["xs"] if samp else d["xp"]
        with ExitStack() as gs:
            zmT = A(gs, g + "zmT", [128, 4, n_own * 128], BF16)
            zaT = A(gs, g + "zaT", [128, 4, n_own * 128], BF16)
            with ExitStack() as ms:
                cqnT = A(ms, g + "cqnT", [128, 3, n_own * 128], BF16)
                ckvT = A(ms, g + "ckvT", [128, 2, nkey * 128], BF16)
                krT = A(ms, g + "krT", [32, nkey * 128], BF16)
                with ExitStack() as ls:
                    nseq = 1 if samp else 4
                    qT = A(ls, g + "qT", [128, 4, n_own * 128], BF16)
                    kT = A(ls, g + "kT", [128, 4, n_own * 128], BF16)
                    vaug = A(ls, g + "vaug", [128, n_own, 4, 129], BF16)
                    graw = A(ls, g + "graw", [128, nt, 16], F32)
                    gsc = {nm: A(ls, g + nm, [128, nt, 8], F32)
                           for nm in ("lf", "b", "bL", "a", "amax", "ref", "dm", "mall", "u", "dec", "E")}
                    self.gtmp = Ring([A(ls, g + "gtmp%d" % i, [128, 256], F32) for i in range(3)])
                    S = A(ls, g + "S", [128, nseq, 8, 129], F32)
                    Sres = [[Buf(S.t[:, s_, j_, :]) for j_ in range(8)] for s_ in range(nseq)]
                    mcur0 = A(ls, g + "mcur0", [128, nseq, 8], F32)
                    self.V(lambda: nc.vector.memset(vaug.t[:, :, :, 128:129], 1.0), [], [vaug])
                    for nm in gsc:
                        self.V(lambda nm=nm: nc.vector.memset(gsc[nm].t[:], 0.0), [], [gsc[nm]])
                    if samp:
                        for j_ in range(8):
                            k.dma("sp", Sres[0][j_].t[:, 0:128], d["C0"][j_ // 4, j_ % 4], [], [Sres[0][j_]])
                        n8 = self.smr.next()
                        n8v = self.gtmp.next()
                        k.dma("sp", n8v.t[0:8, 0:128], d["n0"].rearrange("r h d -> (r h) d"), [], [n8v])
                        pbn = self.psb()
                        k.op("pe", lambda: nc.tensor.transpose(out=pbn.t[:, 0:8], in_=n8v.t[0:8, 0:128], identity=self.ident_f.t[0:8, 0:8]),
                             [n8v, self.ident_f], [pbn])
                        self.V(lambda: nc.vector.tensor_copy(out=S.t[:, 0, :, 128:129], in_=pbn.t[:, 0:8].unsqueeze(2)), [pbn], Sres[0])
                        k.dma("sp", mcur0.t[:, 0, :], d["m0"].partition_broadcast(128), [], [mcur0])
                    else:
                        self.V(lambda: nc.vector.memset(S.t[:], 0.0), [], [b_ for r_ in Sres for b_ in r_])
                        self.V(lambda: nc.vector.memset(mcur0.t[:], -1e30), [], [mcur0])
                    mcur = None
                    with ExitStack() as ps_:
                        self.wring = Ring([A(ps_, g + "w1_%d" % i, [128, 8 * 512], BF16) for i in range(3)])
                        self.xring = Ring([A(ps_, g + "xt%d" % i, [128, D], F32) for i in range(2)])
                        self.xnb = Ring([A(ps_, g + "xnb%d" % i, [128, D], BF16) for i in range(2)])
                        self.junk = Ring([A(ps_, g + "junk%d" % i, [128, 512], BF16) for i in range(1)])
                        self.evt = Ring([A(ps_, g + "evt%d" % i, [128, 8, 128], F32) for i in range(1 if samp else 2)])
                        if samp:
                            hT = tiled(Buf(zmT.t[:].rearrange("p a b -> p (a b)").rearrange("p (f t) -> p f t", f=8)), 8)
                        else:
                            hT = tiled(A(ps_, g + "hT", [128, 8, 1024], BF16), 8)
                        bufs = dict(qT=qT, kT=kT, vaug=vaug, graw=graw, cqnT=cqnT, ckvT=ckvT, krT=krT, hT=hT)
                        bufs["tokw"] = Ring([A(ps_, g + "tokw%d" % i, [128, 288], F32) for i in range(2)])
                        bufs["tokb"] = Ring([A(ps_, g + "tokb%d" % i, [128, 384], BF16) for i in range(3)])
                        bufs["rot"] = Ring([A(ps_, g + "rot%d" % i, [128, 6, 16], F32) for i in range(2)])
                        bufs["cs"] = Ring([A(ps_, g + "cs%d" % i, [128, 2, 16], F32) for i in range(3)])
                        if samp:
                            zflat = zaT.t[:].rearrange("p a b -> p (a b)")
                            ktok = Buf(zflat[:, 0:4096].rearrange("p (i c) -> p i c", c=512))
                            vo = Buf(zflat[:, 4096:8192].rearrange("p (i c) -> p i c", c=512))
                            wk_r = Ring([A(ps_, g + "wkr%d" % i, [128, 128], BF16) for i in range(3)])
                            bufs["ktok"] = ktok
                            bufs["vo"] = vo
                            mview = mcur0.t[:, 0:1, 4:8]
                            mb = [mcur0]
                            for tb in (3, 2):
                                self.proj_pass(g, gi, x_own, tb, False, bufs)
                                tiles = list(range(tb * 8 + 7, tb * 8 - 1, -1))
                                self.gates_phase(graw, gsc, tb * 8, tb * 8 + 8)
                                mview = self.gates_scan(gsc, tiles, 1, mview, mb, 1)
                                mb = [gsc["mall"]]
                                self.gates_bulk(gsc, tb * 8, tb * 8 + 8)
                                for c in tiles:
                                    for h in range(4):
                                        self.mlstm_state_only(gsc, Sres[0][4 + h], ktok, vo, c, c - tb * 8, h, wk_r)
                            mcur = mview
                        for tb in range(n_own // 8):
                            self.proj_pass(g, gi, x_own, tb, True, bufs)
                        k.barrier()
                        if self.ck(g + "proj"):
                            return
                    with ExitStack() as sw:
                        hacc = A(sw, g + "hacc", [128, n_own, 512], F32)
                        self.gates_phase(graw, gsc, 0, n_own)
                        if samp:
                            self.gates_scan(gsc, list(range(16)), 0, mcur0.t[:, 0:1, 0:4], [mcur0], 1)
                            self.gates_scan(gsc, list(range(15, -1, -1)), 1, mcur, [gsc["mall"]], 1)
                        else:
                            self.gates_scan(gsc, [0, 1], 0, mcur0.t[:, :, 0:4], [mcur0], 4)
                            self.gates_scan(gsc, [1, 0], 1, mcur0.t[:, :, 4:8], [mcur0], 4)
                        self.gates_bulk(gsc, 0, n_own)
                        if self.ck(g + "gates"):
                            k.barrier()
                            return
                        self.mlstm_sweeps(g, gsc, Sres, qT, kT, vaug, hacc, zmT, sw, nseq, n_own)
                        if not samp:
                            for s_ in range(4):
                                for j_ in range(8):
                                    k.dma("sp", o["o_C"][s_, j_], Sres[s_][j_].t[:, 0:128], [Sres[s_][j_]], [])
                            pbn = self.psb()
                            for s_ in range(4):
                                k.op("pe", lambda s_=s_: nc.tensor.transpose(out=pbn.t[0:8, s_ * 128:(s_ + 1) * 128], in_=S.t[:, s_, :, 128],
                                                                             identity=self.ident_f.t[:]), Sres[s_] + [self.ident_f], [pbn])
                            for hf in range(2):
                                n_out = self.gtmp.next()
                                self.V(lambda: nc.vector.tensor_copy(out=n_out.t[0:8, 0:256], in_=pbn.t[0:8, hf * 256:(hf + 1) * 256]), [pbn], [n_out])
                                for s2 in range(2):
                                    k.dma("sp", o["o_n"][hf * 2 + s2], n_out.t[0:8, s2 * 128:(s2 + 1) * 128], [n_out], [])
                            mall = gsc["mall"]
                            for s_ in range(4):
                                k.dma("sp", o["o_m"][s_:s_ + 1, 0:4], mall.t[0:1, s_ * 2 + 1, 0:4], [mall], [])
                                k.dma("sp", o["o_m"][s_:s_ + 1, 4:8], mall.t[0:1, s_ * 2 + 0, 4:8], [mall], [])
                        k.barrier()
                        if self.ck(g + "sweeps"):
                            return
                with ExitStack() as at:
                    self.attention(g, at, cqnT, ckvT, krT, zaT, n_own, nkey)
                    k.barrier()
                    if self.ck(g + "attn"):
                        return
            self.join_mlp(g, gi, x_own, zmT, zaT, n_own)
            k.barrier()

    def proj_pass(self, g, gi, x_dram, tb, own, B):
        nc, k, d, o = self.nc, self.k, self.d, self.o
        samp = (g == "s")
        hT = B["hT"]
        t0 = tb * 8
        w = d["w_in"]
        reqs = []
        if own:
            reqs += [(w[:, C_Q:C_Q + 512], 8, 512)]
        reqs += [(w[:, C_K:C_K + 512], 8, 512), (w[:, C_V:C_V + 512], 8, 512), (w[:, C_G:C_G + 16], 8, 16)]
        if own:
            reqs += [(w[:, C_CQ:C_CQ + 384], 8, 384)]
        reqs += [(w[:, C_CKV:C_CKV + 288], 8, 288)]
        self.wstream(reqs)
        self.wget(-1)
        xts = []

        def issue(i):
            b = self.xring.next()
            r0 = (t0 + i) * 128
            k.dma("sp", b.t[:], x_dram[r0:r0 + 128, :], [], [b])
            xts.append(b)
        issue(0)
        issue(1)
        for i in range(8):
            self.norm_transpose(xts[i].t[:], xts[i], gi, 0, hT, i * 128)
            if i + 2 < 8:
                issue(i + 2)
        wi = 0
        if self.ck("pp1"):
            return

        def featmajor(W, ws, dst, scale):
            for h in range(4):
                for half in range(2):
                    pb = self.psb()
                    self.mm_group(pb.t[:, :], pb, [(W[:, kc, h * 128:(h + 1) * 128], hT.t[:, kc, half * 512:(half + 1) * 512])
                                                   for kc in range(8)], [ws] + hT.tiles[half * 4:half * 4 + 4])
                    c0 = tb * 1024 + half * 512
                    self.copy(self.evac_engine(), dst.t[:, h, c0:c0 + 512], pb.t[:, :], [pb], [dst], scale=scale)

        def tokmajor(W, ws, ncols, fn, defer=False):
            prev = None
            for i in range(8):
                pb = self.psb()
                self.mm_group(pb.t[:, 0:ncols], pb, [(hT.t[:, kc, i * 128:(i + 1) * 128], W[:, kc, :]) for kc in range(8)], [ws, hT.tiles[i]])
                if defer:
                    post = fn(i, pb)
                    if prev is not None:
                        prev()
                    prev = post
                else:
                    fn(i, pb)
            if prev is not None:
                prev()

        if own:
            W, ws = self.wget(wi); wi += 1
            featmajor(W, ws, B["qT"], 128.0 ** -0.5)
            if self.ck("pp2"):
                return
            W, ws = self.wget(wi); wi += 1
            featmajor(W, ws, B["kT"], None)
        else:
            W, ws = self.wget(wi); wi += 1
            tokmajor(W, ws, 512, lambda i, pb: self.copy(self.evac_engine(), B["ktok"].t[:, i, :], pb.t[:, :], [pb], [B["ktok"]]))
        if self.ck("pp3"):
            return
        W, ws = self.wget(wi); wi += 1
        if own:
            tokmajor(W, ws, 512, lambda i, pb: self.copy(self.evac_engine(), B["vaug"].t[:, t0 + i, :, 0:128],
                                                         pb.t[:, :].rearrange("p (h e) -> p h e", h=4), [pb], [B["vaug"]]))
        else:
            tokmajor(W, ws, 512, lambda i, pb: self.copy(self.evac_engine(), B["vo"].t[:, i, :], pb.t[:, :], [pb], [B["vo"]]))
        W, ws = self.wget(wi); wi += 1
        tokmajor(W, ws, 16, lambda i, pb: self.copy("dve", B["graw"].t[:, t0 + i, :], pb.t[:, 0:16], [pb], [B["graw"]]))
        if self.ck("pp5"):
            return
        if own:
            W, ws = self.wget(wi); wi += 1

            def cq_fn(i, pb):
                ss = self.smr.next()
                junk = self.junk.next()
                k.op("act", lambda: nc.scalar.activation(out=junk.t[:, 0:384], in_=pb.t[:, 0:384], func=AF.Square,
                                                         accum_out=ss.t[:, 0:1]), [pb], [junk, ss])
                rstd = self.rstd_from_ss(ss, 0, 384, [])
                tb_ = B["tokb"].next()
                self.V(lambda: nc.vector.scalar_tensor_tensor(out=tb_.t[:, 0:384], in0=pb.t[:, 0:384], scalar=rstd,
                                                              in1=self.qn_row.t[:], op0=ALU.mult, op1=ALU.mult),
                       [pb, ss, self.qn_row], [tb_])

                def post():
                    p2 = self.psb()
                    pv = p2.t[:].bitcast(BF16).rearrange("p (f t) -> p f t", t=128)
                    for kc in range(3):
                        k.op("pe", lambda kc=kc: nc.tensor.transpose(out=pv[:, kc, :], in_=tb_.t[:, kc * 128:(kc + 1) * 128],
                                                                     identity=self.ident_b.t[:]), [tb_, self.ident_b], [p2], sig=(kc == 2))
                    c0 = (t0 + i) * 128
                    self.copy(self.evac_engine(), B["cqnT"].t[:, :, c0:c0 + 128], pv[:, 0:3, :], [p2], [B["cqnT"]])
                return post
            tokmajor(W, ws, 384, cq_fn, defer=True)
        if self.ck("pp6"):
            return
        W, ws = self.wget(wi); wi += 1

        def ckv_fn(i, pb):
            ti = t0 + i
            ss = self.smr.next()
            junk = self.junk.next()
            k.op("act", lambda: nc.scalar.activation(out=junk.t[:, 0:256], in_=pb.t[:, 0:256], func=AF.Square,
                                                     accum_out=ss.t[:, 0:1]), [pb], [junk, ss])
            rstd = self.rstd_from_ss(ss, 0, 256, [])
            tw = B["tokw"].next()
            tb_ = B["tokb"].next()
            self.V(lambda: nc.vector.scalar_tensor_tensor(out=tw.t[:, 0:256], in0=pb.t[:, 0:256], scalar=rstd,
                                                          in1=self.kvn_row.t[:], op0=ALU.mult, op1=ALU.mult),
                   [pb, ss, self.kvn_row], [tw])
            self.V(lambda: nc.vector.tensor_copy(out=tw.t[:, 256:288], in_=pb.t[:, 256:288]), [pb], [tw])
            self.V(lambda: nc.vector.tensor_copy(out=tb_.t[:, 0:256], in_=tw.t[:, 0:256]), [tw], [tb_])
            if not samp:
                k.dma("sp", o["o_ckv"][ti * 128:(ti + 1) * 128, :], tw.t[:, 0:256], [tw], [])
                k.dma("sp", o["o_kr"][ti * 128:(ti + 1) * 128, :], tw.t[:, 256:288], [tw], [])
                self.V(lambda: nc.vector.tensor_copy(out=tb_.t[:, 256:288], in_=tw.t[:, 256:288]), [tw], [tb_])
            else:
                cs = B["cs"].next()
                rot = B["rot"].next()
                k.dma("sp", cs.t[:, 0, :], d["kcos"][ti * 128:(ti + 1) * 128, :], [], [cs])
                k.dma("sp", cs.t[:, 1, :], d["ksin"][ti * 128:(ti + 1) * 128, :], [], [cs])
                kr2 = tw.t[:, 256:288].rearrange("p (i two) -> p i two", two=2)
                x1, x2 = kr2[:, :, 0], kr2[:, :, 1]
                ob = tb_.t[:, 256:288].rearrange("p (i two) -> p i two", two=2)
                cos_, sin_ = cs.t[:, 0, :], cs.t[:, 1, :]
                self.V(lambda: nc.vector.tensor_tensor(out=rot.t[:, 0, :], in0=x1, in1=cos_, op=ALU.mult), [tw, cs], [rot])
                self.V(lambda: nc.vector.tensor_tensor(out=rot.t[:, 1, :], in0=x2, in1=sin_, op=ALU.mult), [tw, cs], [rot])
                self.V(lambda: nc.vector.tensor_tensor(out=rot.t[:, 2, :], in0=x1, in1=sin_, op=ALU.mult), [tw, cs], [rot])
                self.V(lambda: nc.vector.tensor_tensor(out=rot.t[:, 3, :], in0=x2, in1=cos_, op=ALU.mult), [tw, cs], [rot])
                self.V(lambda: nc.vector.tensor_tensor(out=ob[:, :, 0], in0=rot.t[:, 0, :], in1=rot.t[:, 1, :], op=ALU.subtract), [rot], [tb_])
                self.V(lambda: nc.vector.tensor_tensor(out=ob[:, :, 1], in0=rot.t[:, 2, :], in1=rot.t[:, 3, :], op=ALU.add), [rot], [tb_])

            def post():
                p2 = self.psb()
                pv = p2.t[:].bitcast(BF16).rearrange("p (f t) -> p f t", t=128)
                for kc in range(2):
                    k.op("pe", lambda kc=kc: nc.tensor.transpose(out=pv[:, kc, :], in_=tb_.t[:, kc * 128:(kc + 1) * 128],
                                                                 identity=self.ident_b.t[:]), [tb_, self.ident_b], [p2], sig=False)
                k.op("pe", lambda: nc.tensor.transpose(out=pv[0:32, 2, :], in_=tb_.t[:, 256:288], identity=self.ident_b.t[:]),
                     [tb_, self.ident_b], [p2])
                c0 = ti * 128
                eng = self.evac_engine()
                self.copy(eng, B["ckvT"].t[:, :, c0:c0 + 128], pv[:, 0:2, :], [p2], [B["ckvT"]])
                self.copy(eng, B["krT"].t[0:32, c0:c0 + 128], pv[0:32, 2, :], [p2], [B["krT"]])
            return post
        tokmajor(W, ws, 288, ckv_fn, defer=True)

    def gates_phase(self, graw, gsc, lo, hi):
        nc, k = self.nc, self.k
        n = hi - lo
        G = graw.t[:, lo:hi, :]
        self.V(lambda: nc.vector.tensor_tensor(out=G, in0=G, in1=self.gb_row.t[:].unsqueeze(1).broadcast_to([128, n, 16]), op=ALU.add),
               [graw, self.gb_row], [graw])
        G5 = graw.t[:, lo:hi, :].rearrange("p n (r i h) -> p n r i h", r=2, i=2)
        gf, gi_ = G5[:, :, :, 1, :], G5[:, :, :, 0, :]
        t1b, t2b = self.gtmp.next(), self.gtmp.next()
        t1 = t1b.t[:, 0:n * 8].rearrange("p (n r h) -> p n r h", r=2, h=4)
        t2 = t2b.t[:, 0:n * 8].rearrange("p (n r h) -> p n r h", r=2, h=4)
        k.op("act", lambda: nc.scalar.activation(out=t1, in_=gf, func=AF.Abs), [graw], [t1b])
        k.op("act", lambda: nc.scalar.activation(out=t1, in_=t1, func=AF.Exp, scale=-1.0), [t1b], [t1b])
        k.op("act", lambda: nc.scalar.activation(out=t1, in_=t1, func=AF.Ln, scale=1.0, bias=1.0), [t1b], [t1b])
        self.V(lambda: nc.vector.tensor_single_scalar(out=t2, in_=gf, scalar=0.0, op=ALU.min), [graw], [t2b])
        lf, b, bL, a, amax = gsc["lf"], gsc["b"], gsc["bL"], gsc["a"], gsc["amax"]
        lfv = lf.t[:, lo:hi, :].rearrange("p n (r h) -> p n r h", r=2)
        self.V(lambda: nc.vector.tensor_tensor(out=lfv, in0=t2, in1=t1, op=ALU.subtract), [t1b, t2b], [lf])
        pb = self.psb()
        for r, tri in ((0, self.trif), (1, self.trib)):
            k.op("pe", lambda r=r, tri=tri: nc.tensor.matmul(
                pb.t[:, r * n * 4:(r + 1) * n * 4].rearrange("p (n h) -> p n h", h=4), lhsT=tri.t[:],
                rhs=lf.t[:, lo:hi, r * 4:(r + 1) * 4], start=True, stop=True), [tri, lf], [pb])
            self.V(lambda r=r: nc.vector.tensor_copy(out=b.t[:, lo:hi, r * 4:(r + 1) * 4],
                                                     in_=pb.t[:, r * n * 4:(r + 1) * n * 4].rearrange("p (n h) -> p n h", h=4)), [pb], [b])
        pb2 = self.psb()
        k.op("pe", lambda: nc.tensor.matmul(pb2.t[:, 0:n * 8], lhsT=self.ones_f.t[:], rhs=lf.t[:, lo:hi, :].rearrange("p n j -> p (n j)"),
                                            start=True, stop=True), [self.ones_f, lf], [pb2])
        self.V(lambda: nc.vector.tensor_copy(out=bL.t[:, lo:hi, :].rearrange("p n j -> p (n j)"), in_=pb2.t[:, 0:n * 8]), [pb2], [bL])
        av = a.t[:, lo:hi, :].rearrange("p n (r h) -> p n r h", r=2)
        bv = b.t[:, lo:hi, :].rearrange("p n (r h) -> p n r h", r=2)
        self.V(lambda: nc.vector.tensor_tensor(out=av, in0=gi_, in1=bv, op=ALU.subtract), [graw, b], [a])
        for blo in range(lo, hi, 16):
            nb = min(16, hi - blo)
            cols = nb * 8
            p3 = self.psb()
            k.op("pe", lambda: nc.tensor.transpose(out=p3.t[0:cols, 0:128], in_=a.t[:, blo:blo + nb, :].rearrange("p n j -> p (n j)"),
                                                   identity=self.ident_f.t[:]), [a, self.ident_f], [p3])
            sm = self.smr.next()
            self.V(lambda: nc.vector.tensor_reduce(out=sm.t[0:cols, 0:1], in_=p3.t[0:cols, 0:128], axis=AX.X, op=ALU.max), [p3], [sm])
            dg = self.gtmp.next()
            self.V(lambda: nc.vector.tensor_scalar(out=dg.t[0:cols, 0:cols], in0=self.ident_f.t[0:cols, 0:cols], scalar1=sm.t[0:cols, 0:1],
                                                   scalar2=None, op0=ALU.mult), [self.ident_f, sm], [dg])
            p4 = self.psb()
            k.op("pe", lambda: nc.tensor.matmul(p4.t[:, 0:cols], lhsT=self.ones_f.t[0:cols, :], rhs=dg.t[0:cols, 0:cols],
                                                start=True, stop=True), [self.ones_f, dg], [p4])
            self.V(lambda: nc.vector.tensor_copy(out=amax.t[:, blo:blo + nb, :].rearrange("p n j -> p (n j)"), in_=p4.t[:, 0:cols]),
                   [p4], [amax])

    def gates_scan(self, gsc, order, dr, m_ap, m_bufs, nseq):
        nc = self.nc

        def view(nm, c):
            t = gsc[nm].t
            if nseq == 1:
                return t[:, c:c + 1, dr * 4:dr * 4 + 4]
            return t[:, :, dr * 4:dr * 4 + 4].rearrange("p (s c) j -> p s c j", c=2)[:, :, c, :]
        cur, cb = m_ap, list(m_bufs)
        for c in order:
            ref, dm, mall = view("ref", c), view("dm", c), view("mall", c)
            self.V(lambda: nc.vector.tensor_tensor(out=ref, in0=cur, in1=view("amax", c), op=ALU.max), cb + [gsc["amax"]], [gsc["ref"]])
            self.V(lambda: nc.vector.tensor_tensor(out=dm, in0=cur, in1=ref, op=ALU.subtract), cb + [gsc["ref"]], [gsc["dm"]])
            self.V(lambda: nc.vector.tensor_tensor(out=mall, in0=view("bL", c), in1=ref, op=ALU.add), [gsc["bL"], gsc["ref"]], [gsc["mall"]])
            cur, cb = mall, [gsc["mall"]]
        return cur

    def gates_bulk(self, gsc, lo, hi):
        nc, k = self.nc, self.k
        n = hi - lo
        sl = lambda nm: gsc[nm].t[:, lo:hi, :]
        self.V(lambda: nc.vector.tensor_single_scalar(out=sl("dm"), in_=sl("dm"), scalar=-100.0, op=ALU.max), [gsc["dm"]], [gsc["dm"]])
        k.op("act", lambda: nc.scalar.activation(out=sl("dec"), in_=sl("dm"), func=AF.Exp), [gsc["dm"]], [gsc["dec"]])
        t1b, t2b = self.gtmp.next(), self.gtmp.next()
        t1 = t1b.t[:, 0:n * 8].rearrange("p (n j) -> p n j", j=8)
        t2 = t2b.t[:, 0:n * 8].rearrange("p (n j) -> p n j", j=8)
        self.V(lambda: nc.vector.tensor_tensor(out=t1, in0=sl("a"), in1=sl("ref"), op=ALU.subtract), [gsc["a"], gsc["ref"]], [t1b])
        k.op("act", lambda: nc.scalar.activation(out=sl("u"), in_=t1, func=AF.Exp), [t1b], [gsc["u"]])
        self.V(lambda: nc.vector.tensor_tensor(out=t2, in0=sl("b"), in1=sl("ref"), op=ALU.add), [gsc["b"], gsc["ref"]], [t2b])
        self.V(lambda: nc.vector.tensor_single_scalar(out=t2, in_=t2, scalar=-80.0, op=ALU.max), [t2b], [t2b])
        k.op("act", lambda: nc.scalar.activation(out=sl("E"), in_=t2, func=AF.Exp, scale=-1.0), [t2b], [gsc["E"]])

    def mlstm_state_only(self, gsc, Sb_, ktok, vo, c, ci, h, wk_r):
        nc, k = self.nc, self.k
        j = 4 + h
        u = gsc["u"].t[:, c, j:j + 1]
        dec = gsc["dec"].t[:, c, j:j + 1]
        wk = wk_r.next()
        k.op("act", lambda: nc.scalar.activation(out=wk.t[:], in_=ktok.t[:, ci, h * 128:(h + 1) * 128], func=AF.Copy, scale=u),
             [ktok, gsc["u"]], [wk])
        pb = self.psb()
        k.op("pe", lambda: nc.tensor.matmul(pb.t[:, 0:128], lhsT=wk.t[:], rhs=vo.t[:, ci, h * 128:(h + 1) * 128], start=True, stop=True),
             [wk, vo], [pb], sig=False)
        k.op("pe", lambda: nc.tensor.matmul(pb.t[:, 128:129], lhsT=wk.t[:], rhs=self.ones_b.t[:, 0:1], start=True, stop=True),
             [wk, self.ones_b], [pb])
        self.V(lambda: nc.vector.scalar_tensor_tensor(out=Sb_.t[:], in0=Sb_.t[:], scalar=dec, in1=pb.t[:, 0:129], op0=ALU.mult, op1=ALU.add),
               [Sb_, gsc["dec"], pb], [Sb_])

    def mlstm_sweeps(self, g, gsc, Sres, qT, kT, vaug, hacc, zmT, es, nseq, n_own):
        nc, k = self.nc, self.k
        A = self.alloc
        Sbf = Ring([A(es, g + "Sbf%d" % i, [128, 129], BF16) for i in range(8)])
        qkr = Ring([A(es, g + "qk%d" % i, [128, 128], BF16) for i in range(8)])
        wkr = Ring([A(es, g + "wk%d" % i, [128, 128], BF16) for i in range(8)])
        hnr = Ring([A(es, g + "hn%d" % i, [128, 512], BF16) for i in range(2)])
        jk = A(es, g + "jk", [128, 128], BF16)
        npc = n_own // nseq
        visits = {}
        masks = (self.trif, self.trib)

        def stage_a(ch):
            s_, c, dr, h, pb = ch["s"], ch["c"], ch["dr"], ch["h"], ch["pb"]
            j = dr * 4 + h
            Sb_ = Sres[s_][j]
            cols = slice(c * 128, (c + 1) * 128)
            dec = gsc["dec"].t[:, c, j:j + 1]
            sb = Sbf.next()
            ch["sb"] = sb
            k.op("act", lambda: nc.scalar.activation(out=sb.t[:], in_=Sb_.t[:], func=AF.Copy, scale=dec), [Sb_, gsc["dec"]], [sb])
            pA = pb.t[:, 0:128]
            pC = pb.t[:, 260:324].bitcast(BF16)
            k.op("pe", lambda: nc.tensor.matmul(pA, lhsT=kT.t[:, h, cols], rhs=qT.t[:, h, cols], start=True, stop=True),
                 [kT, qT], [pb], sig=False)
            k.op("pe", lambda: nc.tensor.transpose(out=pC, in_=kT.t[:, h, cols], identity=self.ident_b.t[:]), [kT, self.ident_b], [pb])

        def stage_b(ch):
            s_, c, dr, h, pb = ch["s"], ch["c"], ch["dr"], ch["h"], ch["pb"]
            j = dr * 4 + h
            u = gsc["u"].t[:, c, j:j + 1]
            pA = pb.t[:, 0:128]
            pC = pb.t[:, 260:324].bitcast(BF16)
            qk = qkr.next()
            ch["qk"] = qk
            self.V(lambda: nc.vector.scalar_tensor_tensor(out=qk.t[:], in0=pA, scalar=u, in1=masks[dr].t[:], op0=ALU.mult, op1=ALU.mult),
                   [pb, gsc["u"], masks[dr]], [qk])
            wk = wkr.next()
            ch["wk"] = wk
            k.op("act", lambda: nc.scalar.activation(out=wk.t[:], in_=pC, func=AF.Copy, scale=u), [pb, gsc["u"]], [wk])

        def stage_c(ch):
            s_, c, dr, h, pb = ch["s"], ch["c"], ch["dr"], ch["h"], ch["pb"]
            cols = slice(c * 128, (c + 1) * 128)
            pB = pb.t[:, 128:257]
            pD = pb.t[:, 328:457]
            sb, qk, wk = ch["sb"], ch["qk"], ch["wk"]
            k.op("pe", lambda: nc.tensor.matmul(pB, lhsT=qT.t[:, h, cols], rhs=sb.t[:], start=True, stop=False), [qT, sb], [pb], sig=False)
            k.op("pe", lambda: nc.tensor.matmul(pB, lhsT=qk.t[:], rhs=vaug.t[:, c, h, :], start=False, stop=True), [qk, vaug], [pb], sig=False)
            k.op("pe", lambda: nc.tensor.matmul(pD, lhsT=wk.t[:], rhs=vaug.t[:, c, h, :], start=True, stop=True), [wk, vaug], [pb])
            dcol = self.psum[0].t[:, 460 + ch["ci"]:461 + ch["ci"]]
            k.op("pe", lambda: nc.tensor.matmul(dcol, lhsT=qT.t[:, h, cols], rhs=sb.t[:, 128:129], start=True, stop=False),
                 [qT, sb], [self.psum[0]], sig=False)
            k.op("pe", lambda: nc.tensor.matmul(dcol, lhsT=qk.t[:], rhs=vaug.t[:, c, h, 128:129], start=False, stop=True),
                 [qk, vaug], [self.psum[0]])

        def stage_d0(chains):
            cb_, cf_ = chains[0]["c"], chains[1]["c"]
            sm = self.smr.next()
            den8 = self.psum[0].t[:, 460:468]
            self.V(lambda: nc.vector.tensor_scalar(out=sm.t[:, 8:16], in0=den8, scalar1=-1.0, scalar2=None, op0=ALU.mult), [self.psum[0]], [sm])
            self.V(lambda: nc.vector.tensor_tensor(out=sm.t[:, 0:8], in0=den8, in1=sm.t[:, 8:16], op=ALU.max), [self.psum[0], sm], [sm])
            d2 = sm.t[:, 0:8].rearrange("p (h two) -> p h two", two=2)
            self.V(lambda: nc.vector.tensor_tensor(out=d2[:, :, 0], in0=d2[:, :, 0], in1=gsc["E"].t[:, cb_, 4:8], op=ALU.max), [sm, gsc["E"]], [sm])
            self.V(lambda: nc.vector.tensor_tensor(out=d2[:, :, 1], in0=d2[:, :, 1], in1=gsc["E"].t[:, cf_, 0:4], op=ALU.max), [sm, gsc["E"]], [sm])
            self.V(lambda: nc.vector.reciprocal(out=sm.t[:, 0:8], in_=sm.t[:, 0:8]), [sm], [sm])
            for ch in chains:
                ch["rd"] = sm

        def stage_d(ch):
            s_, c, dr, h, pb = ch["s"], ch["c"], ch["dr"], ch["h"], ch["pb"]
            j = dr * 4 + h
            Sb_ = Sres[s_][j]
            E = gsc["E"].t[:, c, j:j + 1]
            pB = pb.t[:, 128:257]
            pD = pb.t[:, 328:457]
            sm = ch["rd"]
            rd = sm.t[:, ch["ci"]:ch["ci"] + 1]
            hv = hacc.t[:, c, h * 128:(h + 1) * 128]
            if visits.get((c, h), 0) == 0:
                self.V(lambda: nc.vector.tensor_scalar(out=hv, in0=pB[:, 0:128], scalar1=rd, scalar2=None, op0=ALU.mult), [pb, sm], [hacc])
            else:
                self.V(lambda: nc.vector.scalar_tensor_tensor(out=hv, in0=pB[:, 0:128], scalar=rd, in1=hv, op0=ALU.mult, op1=ALU.add),
                       [pb, sm, hacc], [hacc])
            visits[(c, h)] = visits.get((c, h), 0) + 1
            dec = gsc["dec"].t[:, c, j:j + 1]
            self.V(lambda: nc.vector.scalar_tensor_tensor(out=Sb_.t[:], in0=Sb_.t[:], scalar=dec, in1=pD, op0=ALU.mult, op1=ALU.add),
                   [Sb_, gsc["dec"], pb], [Sb_])

        def finalize(c):
            ss = self.smr.next()
            for h in range(4):
                k.op("act", lambda h=h: nc.scalar.activation(out=jk.t[:], in_=hacc.t[:, c, h * 128:(h + 1) * 128], func=AF.Square,
                                                             accum_out=ss.t[:, h:h + 1]), [hacc], [jk, ss])
            k.op("act", lambda: nc.scalar.activation(out=ss.t[:, 4:8], in_=ss.t[:, 0:4], func=AF.Ln, scale=1.0 / 128, bias=EPS), [ss], [ss])
            k.op("act", lambda: nc.scalar.activation(out=ss.t[:, 8:12], in_=ss.t[:, 4:8], func=AF.Exp, scale=-0.5), [ss], [ss])
            hn = hnr.next()
            self.V(lambda: nc.vector.tensor_tensor(out=hn.t[:].rearrange("p (h e) -> p h e", h=4),
                                                   in0=hacc.t[:, c, :].rearrange("p (h e) -> p h e", h=4),
                                                   in1=ss.t[:, 8:12].unsqueeze(2).broadcast_to([128, 4, 128]), op=ALU.mult), [hacc, ss], [hn])
            pb = self.psb()
            pv = pb.t[:].bitcast(BF16).rearrange("p (f t) -> p f t", t=128)
            for h in range(4):
                k.op("pe", lambda h=h: nc.tensor.transpose(out=pv[:, h, :], in_=hn.t[:, h * 128:(h + 1) * 128], identity=self.ident_b.t[:]),
                     [hn, self.ident_b], [pb], sig=(h == 3))
            self.V(lambda: nc.vector.tensor_tensor(out=zmT.t[:, :, c * 128:(c + 1) * 128], in0=pv[:, 0:4, :],
                                                   in1=self.hnT.t[:].unsqueeze(2).broadcast_to([128, 4, 128]), op=ALU.mult),
                   [pb, self.hnT], [zmT])

        for jj in range(npc):
            for s_ in range(nseq):
                cb = s_ * npc + (npc - 1 - jj)
                cf = s_ * npc + jj
                chains = []
                for h in range(4):
                    chains.append(dict(s=s_, c=cb, dr=1, h=h, pb=self.psum[2 * h], ci=2 * h))
                    chains.append(dict(s=s_, c=cf, dr=0, h=h, pb=self.psum[2 * h + 1], ci=2 * h + 1))
                for st in (stage_a, stage_b, stage_c):
                    for ch in chains:
                        st(ch)
                stage_d0(chains)
                for ch in chains:
                    stage_d(ch)
                for c in sorted({cb, cf}):
                    if all(visits.get((c, h), 0) == 2 for h in range(4)):
                        finalize(c)

    def attention(self, g, es, cqnT, ckvT, krT, zaT, n_own, nkey):
        nc, k, d = self.nc, self.k, self.d
        A = self.alloc
        samp = (g == "s")
        Tq = n_own * 128
        Sk = nkey * 128
        w_uqc = A(es, g + "w_uqc", [128, 3, 8, 128], BF16)
        w_ukp = A(es, g + "w_ukp", [128, 2, 8, 128], BF16)
        w_uv = A(es, g + "w_uv", [128, 2, 512], BF16)
        esel = A(es, g + "esel", [32, 128], BF16)
        self.V(lambda: nc.vector.memset(w_ukp.t[:], 0.0), [], [w_ukp])
        for kc in range(3):
            rs = slice(kc * 128, (kc + 1) * 128)
            k.dma("pool", w_uqc.t[:, kc, :, 0:96], d["w_uq"][rs, :].rearrange("p (h n) -> p h n", h=8), [], [w_uqc])
            k.dma("pool", w_uqc.t[:, kc, :, 96:128], d["w_uq_sw"][rs], [], [w_uqc])
        for kc in range(2):
            rs = slice(kc * 128, (kc + 1) * 128)
            k.dma("pool", w_ukp.t[:, kc, :, 0:64], d["w_uk"][rs], [], [w_ukp])
        k.dma("pool", w_uv.t[:], d["w_uv"].rearrange("(k p) n -> p k n", p=128), [], [w_uv])
        k.dma("pool", esel.t[:], d["esel"], [], [esel])
        V_all = A(es, g + "Vall", [128, nkey, 8 * 65 + 64], BF16)
        KhT = Ring([A(es, g + "KhT%d" % i, [128, Sk], BF16) for i in range(2)])
        qhT = Ring([A(es, g + "qhT%d" % i, [128, Tq], BF16) for i in range(2)])
        sqb = A(es, g + "sqb", [128, Sk], BF16)
        sqq = A(es, g + "sqq", [128, Tq], BF16)
        PT = Ring([A(es, g + "PT%d" % i, [128, 512], BF16) for i in range(4)])
        tmpO = Ring([A(es, g + "tmpO%d" % i, [128, 512], F32) for i in range(2)])
        stg = Ring([A(es, g + "stg%d" % i, [64, 512], BF16) for i in range(2)])
        mx = A(es, g + "mx", [1, 32], F32)
        negM = Ring([A(es, g + "negM%d" % i, [128, 1], F32) for i in range(2)])
        psS = Ring(self.psum[0:4])
        psO = Ring(self.psum[4:6])
        psX = Ring(self.psum[6:8])
        self.V(lambda: nc.vector.memset(V_all.t[:], 0.0), [], [V_all])
        self.V(lambda: nc.vector.memset(V_all.t[:, :, 0:520].rearrange("p k (h e) -> p k h e", e=65)[:, :, :, 64:65], 1.0), [], [V_all])
        if samp:
            qtab = A(es, g + "qtab", [128, Tq], F32)
            k.dma("sp", qtab.t[64:128, :], d["qtab"], [], [qtab])
            self.V(lambda: nc.vector.tensor_scalar(out=qtab.t[64:128, :], in0=qtab.t[64:128, :], scalar1=SCQ, scalar2=None, op0=ALU.mult),
                   [qtab], [qtab])
            cw = Ring([A(es, g + "cw%d" % i, [128, 288], F32) for i in range(2)])
            cb_ = Ring([A(es, g + "cb%d" % i, [128, 288], BF16) for i in range(2)])
            for i in range(4):
                w_, b_ = cw.next(), cb_.next()
                k.dma("sp", w_.t[:, 0:256], d["cckv"][i * 128:(i + 1) * 128, :], [], [w_])
                k.dma("sp", w_.t[:, 256:288], d["ckr"][i * 128:(i + 1) * 128, :], [], [w_])
                self.V(lambda: nc.vector.tensor_copy(out=b_.t[:], in_=w_.t[:]), [w_], [b_])
                p2 = psX.next()
                pv = p2.t[:].bitcast(BF16).rearrange("p (f t) -> p f t", t=128)
                for kc in range(2):
                    k.op("pe", lambda kc=kc: nc.tensor.transpose(out=pv[:, kc, :], in_=b_.t[:, kc * 128:(kc + 1) * 128],
                                                                 identity=self.ident_b.t[:]), [b_, self.ident_b], [p2], sig=False)
                k.op("pe", lambda: nc.tensor.transpose(out=pv[0:32, 2, :], in_=b_.t[:, 256:288], identity=self.ident_b.t[:]),
                     [b_, self.ident_b], [p2])
                c0 = (32 + i) * 128
                self.copy("dve", ckvT.t[:, :, c0:c0 + 128], pv[:, 0:2, :], [p2], [ckvT])
                self.copy("dve", krT.t[0:32, c0:c0 + 128], pv[0:32, 2, :], [p2], [krT])
        else:
            qcol = A(es, g + "qcol", [128, 1], F32)
            self.V(lambda: nc.vector.memset(qcol.t[:], 0.0), [], [qcol])
            self.V(lambda: nc.vector.memset(qcol.t[64:96, :], SCQ), [], [qcol])
        for kt in range(nkey):
            pb = psX.next()
            self.mm_group(pb.t[:, :], pb, [(ckvT.t[:, kc, kt * 128:(kt + 1) * 128], w_uv.t[:, kc, :]) for kc in range(2)], [ckvT, w_uv])
            self.copy("dve" if kt % 2 else "act", V_all.t[:, kt, 0:520].rearrange("p (h e) -> p h e", e=65)[:, :, 0:64],
                      pb.t[:, :].rearrange("p (h e) -> p h e", h=8), [pb], [V_all])
        if samp:
            units = [(qb * 512, 512, list(range(nkey))) for qb in range(Tq // 512)]
        else:
            units = [(s_ * 256, 256, [2 * s_, 2 * s_ + 1]) for s_ in range(4)]
        heads = {}

        def build(h):
            kh, qh = KhT.next(), qhT.next()
            heads[h] = dict(kh=kh, qh=qh)
            return [lambda c0=c0: build_k(h, c0) for c0 in range(0, Sk, 512)] + [lambda c0=c0: build_q(h, c0) for c0 in range(0, Tq, 512)]

        def build_k(h, c0):
            kh = heads[h]["kh"]
            if True:
                pb = psX.next()
                self.mm_group(pb.t[:, :], pb, [(esel.t[:, :], krT.t[0:32, c0:c0 + 512])] +
                              [(w_ukp.t[:, kc, h, :], ckvT.t[:, kc, c0:c0 + 512]) for kc in range(2)], [esel, krT, w_ukp, ckvT])
                self.copy("dve", kh.t[:, c0:c0 + 512], pb.t[:, :], [pb], [kh])

        def build_q(h, c0):
            qh = heads[h]["qh"]
            if True:
                pb = psX.next()
                self.mm_group(pb.t[:, :], pb, [(w_uqc.t[:, kc, h, :], cqnT.t[:, kc, c0:c0 + 512]) for kc in range(3)], [w_uqc, cqnT])
                self.V(lambda: nc.vector.tensor_scalar(out=qh.t[0:64, c0:c0 + 512], in0=pb.t[0:64, :], scalar1=SCQ, scalar2=None, op0=ALU.mult),
                       [pb], [qh])
                if samp:
                    self.V(lambda: nc.vector.tensor_tensor(out=qh.t[64:128, c0:c0 + 512], in0=pb.t[64:128, :], in1=qtab.t[64:128, c0:c0 + 512],
                                                           op=ALU.mult), [pb, qtab], [qh])
                else:
                    self.V(lambda: nc.vector.tensor_scalar(out=qh.t[64:128, c0:c0 + 512], in0=pb.t[64:128, :], scalar1=qcol.t[64:128, 0:1],
                                                           scalar2=None, op0=ALU.mult), [pb, qcol], [qh])

        def bound(h):
            kh, qh = heads[h]["kh"], heads[h]["qh"]
            nq_blk = Tq // 512
            nk_blk = Sk // 512
            pieces = []

            def sq_piece(src, n, off):
                sq = sqq if src is qh else sqb
                if samp:
                    k.op("pool", lambda: nc.gpsimd.tensor_tensor(out=sq.t[:, 0:n], in0=src.t[:, 0:n], in1=src.t[:, 0:n], op=ALU.mult), [src], [sq])
                else:
                    k.op("act", lambda: nc.scalar.activation(out=sq.t[:, 0:n], in_=src.t[:, 0:n], func=AF.Square), [src], [sq])

            def ones_piece(sq, c0, col):
                pb = psX.next()
                k.op("pe", lambda: nc.tensor.matmul(pb.t[0:1, :], lhsT=self.ones_b.t[:, 0:1], rhs=sq.t[:, c0:c0 + 512],
                                                    start=True, stop=True), [self.ones_b, sq], [pb])
                self.V(lambda: nc.vector.tensor_reduce(out=mx.t[0:1, col:col + 1], in_=pb.t[0:1, :], axis=AX.X, op=ALU.max), [pb], [mx])
            pieces.append(lambda: (sq_piece(qh, Tq, 0), sq_piece(kh, Sk, 0)))
            col = 0
            for sq, n in ((sqq, Tq), (sqb, Sk)):
                for c0 in range(0, n, 512):
                    pieces.append(lambda sq=sq, c0=c0, col=col: ones_piece(sq, c0, col))
                    col += 1
            pieces.append(lambda: bound_final(h, nq_blk, nk_blk))
            return pieces

        def bound_final(h, nq_blk, nk_blk):
            self.V(lambda: nc.vector.tensor_reduce(out=mx.t[0:1, 24:25], in_=mx.t[0:1, 0:nq_blk], axis=AX.X, op=ALU.max), [mx], [mx])
            self.V(lambda: nc.vector.tensor_reduce(out=mx.t[0:1, 25:26], in_=mx.t[0:1, nq_blk:nq_blk + nk_blk], axis=AX.X, op=ALU.max), [mx], [mx])
            self.V(lambda: nc.vector.tensor_tensor(out=mx.t[0:1, 26:27], in0=mx.t[0:1, 24:25], in1=mx.t[0:1, 25:26], op=ALU.mult), [mx], [mx])
            k.op("act", lambda: nc.scalar.activation(out=mx.t[0:1, 27:28], in_=mx.t[0:1, 26:27], func=AF.Ln, bias=1e-30), [mx], [mx])
            k.op("act", lambda: nc.scalar.activation(out=mx.t[0:1, 28:29], in_=mx.t[0:1, 27:28], func=AF.Exp, scale=0.5), [mx], [mx])
            pb = psX.next()
            k.op("pe", lambda: nc.tensor.matmul(pb.t[:, 0:1], lhsT=self.ones_f.t[0:1, :], rhs=mx.t[0:1, 28:29], start=True, stop=True),
                 [self.ones_f, mx], [pb])
            nm = negM.next()
            heads[h]["nm"] = nm
            self.V(lambda: nc.vector.tensor_scalar(out=nm.t[:], in0=pb.t[:, 0:1], scalar1=-1.02, scalar2=None, op0=ALU.mult), [pb], [nm])

        def main_unit(h, unit):
            kh, qh, nm = heads[h]["kh"], heads[h]["qh"], heads[h]["nm"]
            (q0, nq, kts) = unit
            pO = psO.next()
            nk = len(kts)

            def smat(ki):
                pS = psS.next()
                kt = kts[ki]
                k.op("pe", lambda: nc.tensor.matmul(pS.t[:, 0:nq], lhsT=kh.t[:, kt * 128:(kt + 1) * 128], rhs=qh.t[:, q0:q0 + nq],
                                                    start=True, stop=True), [kh, qh], [pS])
                pt = PT.next()
                k.op("act", lambda: nc.scalar.activation(out=pt.t[:, 0:nq], in_=pS.t[:, 0:nq], func=AF.Exp, bias=nm.t[:, 0:1], scale=1.0),
                     [pS, nm], [pt])
                return pt

            def pv(ki, pt):
                kt = kts[ki]
                k.op("pe", lambda: nc.tensor.matmul(pO.t[:, 0:nq], lhsT=V_all.t[:, kt, h * 65:h * 65 + 128], rhs=pt.t[:, 0:nq],
                                                    start=(ki == 0), stop=(ki == nk - 1)), [V_all, pt], [pO], sig=(ki == nk - 1))
            LA = 3
            pts = [smat(i_) for i_ in range(min(LA, nk))]
            for ki in range(nk):
                if ki + LA < nk:
                    pts.append(smat(ki + LA))
                pv(ki, pts[ki])
                if ki == min(5, nk - 1) and pending:
                    pending.pop(0)()
                if ki % 3 == 1 and ki >= 7 and work:
                    work.pop(0)()
            to = tmpO.next()
            if samp:
                self.V(lambda: nc.vector.tensor_copy(out=to.t[0:65, 0:nq], in_=pO.t[0:65, 0:nq]), [pO], [to])
                self.V(lambda: nc.vector.reciprocal(out=to.t[64:65, 0:nq], in_=to.t[64:65, 0:nq]), [to], [to])
            else:
                k.op("act", lambda: nc.scalar.copy(out=to.t[0:65, 0:nq], in_=pO.t[0:65, 0:nq]), [pO], [to])
                k.op("act", lambda: nc.scalar.activation(out=to.t[64:65, 0:nq], in_=to.t[64:65, 0:nq], func=AF.Ln), [to], [to])
                k.op("act", lambda: nc.scalar.activation(out=to.t[64:65, 0:nq], in_=to.t[64:65, 0:nq], func=AF.Exp, scale=-1.0), [to], [to])
            pending.append(lambda: fin(h, q0, nq, to))

        def fin(h, q0, nq, to):
            pR = psX.next()
            k.op("pe", lambda: nc.tensor.matmul(pR.t[0:64, 0:nq], lhsT=self.ones_f.t[64:65, 0:64], rhs=to.t[64:65, 0:nq],
                                                start=True, stop=True), [self.ones_f, to], [pR])
            if h % 2 == 0:
                self.V(lambda: nc.vector.tensor_tensor(out=zaT.t[0:64, h // 2, q0:q0 + nq], in0=to.t[0:64, 0:nq], in1=pR.t[0:64, 0:nq],
                                                       op=ALU.mult), [to, pR], [zaT])
            else:
                sg = stg.next()
                self.V(lambda: nc.vector.tensor_tensor(out=sg.t[0:64, 0:nq], in0=to.t[0:64, 0:nq], in1=pR.t[0:64, 0:nq], op=ALU.mult),
                       [to, pR], [sg])
                k.dma("sp", zaT.t[64:128, h // 2, q0:q0 + nq], sg.t[0:64, 0:nq], [sg], [zaT])

        pending = []
        work = []
        for p_ in build(0) + bound(0):
            p_()
        for h in range(8):
            if h + 1 < 8:
                work += build(h + 1)
                work += bound(h + 1)
            for ui, unit in enumerate(units):
                main_unit(h, unit)
                if not samp:
                    for _ in range(3):
                        if work:
                            work.pop(0)()
            while work:
                work.pop(0)()
        while pending:
            pending.pop(0)()

    def join_mlp(self, g, gi, x_dram, zmT, zaT, n_own):
        nc, k, d, o = self.nc, self.k, self.d, self.o
        A = self.alloc
        samp = (g == "s")
        y_dram = o["ys"] if samp else o["yp"]
        w = d["w_in"]
        with ExitStack() as js:
            if samp:
                G1, G2 = self.Gs
                glist = []
            else:
                G1 = A(js, g + "G1", [128, D], F32)
                G2 = A(js, g + "G2", [128, D], F32)
                glist = [(1, G1, G2), (0, self.Gs[0], self.Gs[1])]
            x1 = tiled(A(js, g + "x1", [128, 8, D], F32), 8)
            self.wring = Ring([A(js, g + "w5_%d" % i, [128, 8 * 512], BF16) for i in range(4)])
            self.xnb = Ring([A(js, g + "xnb5%d" % i, [128, D], BF16) for i in range(2)])
            self.junk = Ring([A(js, g + "junk5%d" % i, [128, 512], BF16) for i in range(1)])
            self.evt = Ring([A(js, g + "evt5%d" % i, [128, 8, 128], F32) for i in range(1)])
            tmpf = Ring([A(js, g + "tmpf%d" % i, [128, 512], F32) for i in range(3 if samp else 5)])
            tmpb = Ring([A(js, g + "tmpb%d" % i, [128, 512], BF16) for i in range(2)])

            def load_x(tb):
                for i in range(8):
                    r0 = (tb * 8 + i) * 128
                    k.dma("sp", x1.t[:, i, :], x_dram[r0:r0 + 128, :], [], [x1.tiles[i]])
            load_x(0)
            if glist:
                reps = {gi_: A(js, g + "rep%d" % gi_, [128, 8, 128], BF16) for (gi_, _, _) in glist}

            def g_begin():
                for (gi_, _, _) in glist:
                    for kc in range(8):
                        self.V(lambda kc=kc, gi_=gi_: nc.vector.tensor_scalar(out=reps[gi_].t[:, kc, :], in0=self.ones_b.t[:],
                                                                               scalar1=self.sT.t[:, kc, gi_:gi_ + 1], scalar2=None, op0=ALU.mult),
                               [self.ones_b, self.sT], [reps[gi_]])
                self.wstream([(d["w_ada"][:, mi * D + j * 512: mi * D + (j + 1) * 512], 8, 512) for mi in (2, 5) for j in range(2)])
                self.wget(-1)

            def g_finish():
                wi = 0
                for idx, (mi, post) in enumerate(((2, "post1"), (5, "post2"))):
                    for j in range(2):
                        W, ws = self.wget(wi); wi += 1
                        t1, t2 = tmpf.next(), tmpf.next()
                        k.dma("sp", t1.t[:], d["b_ada"][mi * D + j * 512: mi * D + (j + 1) * 512].partition_broadcast(128), [], [t1])
                        k.dma("sp", t2.t[:], d[post][j * 512:(j + 1) * 512].partition_broadcast(128), [], [t2])
                        for (gi_, G1_, G2_) in glist:
                            G = (G1_, G2_)[idx]
                            pb = self.psb()
                            self.mm_group(pb.t[:, :], pb, [(reps[gi_].t[:, kc, :], W[:, kc, :]) for kc in range(8)], [reps[gi_], ws])
                            t3 = tmpf.next()
                            self.V(lambda: nc.vector.tensor_tensor(out=t3.t[:], in0=pb.t[:, :], in1=t1.t[:], op=ALU.add), [pb, t1], [t3])
                            self.V(lambda: nc.vector.tensor_tensor(out=G.t[:, j * 512:(j + 1) * 512], in0=t3.t[:], in1=t2.t[:], op=ALU.mult), [t3, t2], [G])
            for tb in range(n_own // 8):
                with ExitStack() as s1:
                    hT = tiled(A(s1, g + "hT5", [128, 8, 1024], BF16), 8)
                    Wm = A(s1, g + "Wm", [128, 4, D], BF16)
                    Wa = A(s1, g + "Wa", [128, 4, D], BF16)
                    mT = A(s1, g + "mT", [128, 8, 1024], BF16)
                    tmpj = Ring(tmpf.b + [A(s1, g + "tmpj%d" % i, [128, 512], F32) for i in range(3)])
                    if tb > 0:
                        load_x(tb)
                    k.dma("pool", Wm.t[:], d["w_mlstm_o"].rearrange("(k p) n -> p k n", p=128), [], [Wm])
                    k.dma("pool", Wa.t[:], d["w_mla_o"].rearrange("(k p) n -> p k n", p=128), [], [Wa])
                    reqs = [(w[:, C_O:C_O + 512], 8, 512)]
                    for fg in range(2):
                        reqs += [(w[:, C_MG + fg * 512:C_MG + (fg + 1) * 512], 8, 512),
                                 (w[:, C_MG + D + fg * 512:C_MG + D + (fg + 1) * 512], 8, 512)]
                    reqs += [(d["w_out"][:, 0:512], 8, 512), (d["w_out"][:, 512:1024], 8, 512)]
                    if tb == 0 and glist:
                        g_begin()
                        for i in range(8):
                            self.norm_transpose(x1.t[:, i, :], x1.tiles[i], gi, 0, hT, i * 128)
                        g_finish()
                        self.wstream(reqs)
                        self.wget(-1)
                    else:
                        self.wstream(reqs)
                        self.wget(-1)
                        for i in range(8):
                            self.norm_transpose(x1.t[:, i, :], x1.tiles[i], gi, 0, hT, i * 128)
                    W, ws = self.wget(0)
                    for h in range(4):
                        for half in range(2):
                            pb = self.psb()
                            self.mm_group(pb.t[:, :], pb, [(W[:, kc, h * 128:(h + 1) * 128], hT.t[:, kc, half * 512:(half + 1) * 512])
                                                           for kc in range(8)], [ws] + hT.tiles[half * 4:half * 4 + 4])
                            e_ = tmpj.next()
                            k.op("act", lambda: nc.scalar.activation(out=e_.t[:], in_=pb.t[:, :], func=AF.Exp, scale=-1.0), [pb], [e_])
                            k.op("act", lambda: nc.scalar.activation(out=e_.t[:], in_=e_.t[:], func=AF.Ln, bias=1.0), [e_], [e_])
                            k.op("act", lambda: nc.scalar.activation(out=e_.t[:], in_=e_.t[:], func=AF.Exp, scale=-1.0), [e_], [e_])
                            c0 = tb * 1024 + half * 512
                            self.V(lambda: nc.vector.tensor_tensor(out=zmT.t[:, h, c0:c0 + 512], in0=zmT.t[:, h, c0:c0 + 512], in1=e_.t[:],
                                                                   op=ALU.mult), [zmT, e_], [zmT])
                    for fg in range(2):
                        Wga, wsa = self.wget(1 + 2 * fg)
                        Wgb, wsb = self.wget(2 + 2 * fg)
                        for c4 in range(4):
                            fc = fg * 4 + c4
                            for half in range(2):
                                c0 = tb * 1024 + half * 512
                                hc = slice(half * 512, (half + 1) * 512)
                                pM, pA, pGa, pGb = self.psb(), self.psb(), self.psb(), self.psb()
                                self.mm_group(pM.t[:, :], pM, [(Wm.t[:, kc, fc * 128:(fc + 1) * 128], zmT.t[:, kc, c0:c0 + 512]) for kc in range(4)], [Wm, zmT])
                                self.mm_group(pA.t[:, :], pA, [(Wa.t[:, kc, fc * 128:(fc + 1) * 128], zaT.t[:, kc, c0:c0 + 512]) for kc in range(4)], [Wa, zaT])
                                self.mm_group(pGa.t[:, :], pGa, [(Wga[:, kc, c4 * 128:(c4 + 1) * 128], hT.t[:, kc, hc]) for kc in range(8)], [wsa] + hT.tiles[half * 4:half * 4 + 4])
                                self.mm_group(pGb.t[:, :], pGb, [(Wgb[:, kc, c4 * 128:(c4 + 1) * 128], hT.t[:, kc, hc]) for kc in range(8)], [wsb] + hT.tiles[half * 4:half * 4 + 4])
                                ea, eb = tmpj.next(), tmpj.next()
                                k.op("act", lambda: nc.scalar.activation(out=ea.t[:], in_=pGa.t[:, :], func=AF.Exp, scale=-1.0), [pGa], [ea])
                                k.op("act", lambda: nc.scalar.activation(out=eb.t[:], in_=pGb.t[:, :], func=AF.Exp, scale=-1.0), [pGb], [eb])
                                k.op("act", lambda: nc.scalar.activation(out=ea.t[:], in_=ea.t[:], func=AF.Ln, bias=1.0), [ea], [ea])
                                k.op("act", lambda: nc.scalar.activation(out=ea.t[:], in_=ea.t[:], func=AF.Exp, scale=-1.0), [ea], [ea])
                                k.op("act", lambda: nc.scalar.activation(out=eb.t[:], in_=eb.t[:], func=AF.Ln, bias=1.0), [eb], [eb])
                                k.op("act", lambda: nc.scalar.activation(out=eb.t[:], in_=eb.t[:], func=AF.Exp, scale=-1.0), [eb], [eb])
                                self.V(lambda: nc.vector.tensor_tensor(out=ea.t[:], in0=pM.t[:, :], in1=ea.t[:], op=ALU.mult), [pM, ea], [ea])
                                self.V(lambda: nc.vector.tensor_tensor(out=eb.t[:], in0=pA.t[:, :], in1=eb.t[:], op=ALU.mult), [pA, eb], [eb])
                                self.V(lambda: nc.vector.tensor_tensor(out=mT.t[:, fc, hc], in0=ea.t[:], in1=eb.t[:], op=ALU.add), [ea, eb], [mT])
                    W0, ws0 = self.wget(5)
                    W1, ws1 = self.wget(6)
                    for i in range(8):
                        p0, p1 = self.psb(), self.psb()
                        self.mm_group(p0.t[:, :], p0, [(mT.t[:, kc, i * 128:(i + 1) * 128], W0[:, kc, :]) for kc in range(8)], [mT, ws0])
                        self.mm_group(p1.t[:, :], p1, [(mT.t[:, kc, i * 128:(i + 1) * 128], W1[:, kc, :]) for kc in range(8)], [mT, ws1])
                        self.post_residual(x1, i, [p0.t[:, :], p1.t[:, :]], [p0, p1], G1, tmpf)
                with ExitStack() as s2:
                    h2T = tiled(A(s2, g + "h2T", [128, 8, 1024], BF16), 8)
                    ff0b = [Buf(None) for _ in range(8)]
                    aT = A(s2, g + "aT", [128, 32, 1024], BF16)
                    self.wstream([(d["w_mlp1"][:, t * 512:(t + 1) * 512], 8, 512) for t in range(8)] +
                                 [(d["w_mlp2"][gq * 1024:(gq + 1) * 1024, j * 512:(j + 1) * 512], 8, 512) for j in range(2) for gq in range(4)])
                    self.wget(-1)
                    for i in range(8):
                        self.norm_transpose(x1.t[:, i, :], x1.tiles[i], gi, 1, h2T, i * 128)
                    def mlp1_blk(t, W, ws, half):
                        hc = slice(half * 512, (half + 1) * 512)
                        for c4 in range(4):
                            pb = self.psb()
                            self.mm_group(pb.t[:, :], pb, [(W[:, kc, c4 * 128:(c4 + 1) * 128], h2T.t[:, kc, hc]) for kc in range(8)],
                                          [ws] + h2T.tiles[half * 4:half * 4 + 4])
                            dst = aT.t[:, t * 4 + c4, hc]
                            r_ = tmpb.next()
                            k.op("act", lambda: nc.scalar.activation(out=r_.t[:], in_=pb.t[:, :], func=AF.Relu), [pb], [r_])
                            self.V(lambda: nc.vector.tensor_tensor(out=dst, in0=r_.t[:], in1=r_.t[:], op=ALU.mult), [r_], [aT])
                    w01 = [self.wget(0), self.wget(1)]
                    for half in range(2):
                        for t in range(2):
                            mlp1_blk(t, w01[t][0], w01[t][1], half)
                    for t in range(2, 8):
                        W, ws = self.wget(t)
                        for half in range(2):
                            mlp1_blk(t, W, ws, half)
                    ff0 = h2T.t[:].rearrange("p a b -> p (a b)").bitcast(F32).rearrange("p (i c) -> p i c", c=512)
                    for j in range(2):
                        for gq in range(4):
                            W, ws = self.wget(8 + j * 4 + gq)
                            for i in range(8):
                                pb = self.psum[i]
                                for kc in range(8):
                                    first = (gq == 0 and kc == 0)
                                    last = (gq == 3 and kc == 7)
                                    k.op("pe", lambda kc=kc, first=first, last=last: nc.tensor.matmul(
                                        pb.t[:, :], lhsT=aT.t[:, gq * 8 + kc, i * 128:(i + 1) * 128], rhs=W[:, kc, :], start=first, stop=last),
                                        [aT, ws], [pb], sig=(kc == 7))
                        if j == 0:
                            for i in range(8):
                                self.copy("act" if i % 2 else "dve", ff0[:, i, :], self.psum[i].t[:, :], [self.psum[i]], [ff0b[i]] + h2T.tiles)
                        else:
                            for i in range(8):
                                self.post_residual(x1, i, [ff0[:, i, :], self.psum[i].t[:, :]], [ff0b[i], self.psum[i]], G2, tmpf)
                                r0 = (tb * 8 + i) * 128
                                k.dma("sp", y_dram[r0:r0 + 128, :], x1.t[:, i, :], [x1.tiles[i]], [])
                    k.barrier()

    def post_residual(self, x1, i, halves, hbufs, G, tmpf):
        nc, k = self.nc, self.k
        ss = self.smr.next()
        junk = self.junk.next()
        for j in range(2):
            k.op("act", lambda j=j: nc.scalar.activation(out=junk.t[:, 0:512], in_=halves[j], func=AF.Square,
                                                         accum_out=ss.t[:, 4 + j:5 + j]), [hbufs[j]], [junk, ss])
        self.V(lambda: nc.vector.tensor_tensor(out=ss.t[:, 0:1], in0=ss.t[:, 4:5], in1=ss.t[:, 5:6], op=ALU.add), [ss], [ss])
        rstd = self.rstd_from_ss(ss, 0, D, [])
        for j in range(2):
            t_ = tmpf.next()
            self.V(lambda j=j: nc.vector.scalar_tensor_tensor(out=t_.t[:], in0=halves[j], scalar=rstd, in1=G.t[:, j * 512:(j + 1) * 512],
                                                              op0=ALU.mult, op1=ALU.mult), [hbufs[j], ss, G], [t_])
            self.V(lambda j=j: nc.vector.tensor_tensor(out=x1.t[:, i, j * 512:(j + 1) * 512], in0=x1.t[:, i, j * 512:(j + 1) * 512],
                                                       in1=t_.t[:], op=ALU.add), [x1.tiles[i], t_], [x1.tiles[i]])


_CACHE = {}


def _rope_tables(rev):
    t = np.arange(NST)
    pos = (NST - 1 - t) if rev else t
    row = (pos // 64).astype(np.float32)
    col = (pos % 64).astype(np.float32)
    inv = (10000.0 ** (-np.arange(0, 16, 2, dtype=np.float32) / 16)).astype(np.float32)
    ang = np.concatenate([row[:, None] * inv, col[:, None] * inv], axis=-1).astype(np.float32)
    cos, sin = np.cos(ang).astype(np.float32), np.sin(ang).astype(np.float32)
    qtab = np.zeros((64, NSO), np.float32)
    qtab[0:32:2] = cos[:NSO].T
    qtab[1:32:2] = cos[:NSO].T
    qtab[32:64:2] = -sin[:NSO].T
    qtab[33:64:2] = sin[:NSO].T
    return cos, sin, qtab


def _consts():
    ident = np.eye(128, dtype=np.float32)
    s_, t_ = np.meshgrid(np.arange(128), np.arange(128), indexing="ij")
    trif = (s_ <= t_).astype(np.float32)
    trib = (s_ >= t_).astype(np.float32)
    esel = np.zeros((32, 128), np.float32)
    esel[np.arange(32), 64 + np.arange(32)] = 1.0
    esel[np.arange(32), 96 + np.arange(32)] = 1.0
    return ident, trif, trib, esel


def make_in_maps(inp):
    f = lambda a: np.ascontiguousarray(np.asarray(a, dtype=np.float32))
    ident, trif, trib, esel = _consts()
    w_in = f(inp["w_in"][0])
    w_in_sw = w_in.copy()
    gsw = w_in[:, C_G:C_G + 16].reshape(D, 2, 8)[:, ::-1, :].reshape(D, 16)
    w_in_sw[:, C_G:C_G + 16] = gsw
    gate_b = f(inp["mlstm_gate_b"][0]).reshape(2, 8)
    w_uq = f(inp["w_uq"][0]).reshape(384, 8, 96)
    sw = w_uq[:, :, 64:96].reshape(384, 8, 16, 2)[:, :, :, ::-1].reshape(384, 8, 32)
    w_ukv = f(inp["w_ukv"][0])
    vecs = np.concatenate([f(inp["b_ada"][0]).reshape(48, 128), f(inp["norm_pre1"][0]).reshape(8, 128),
                           f(inp["norm_pre2"][0]).reshape(8, 128), f(inp["mlstm_head_norm"][0]).reshape(4, 128)], axis=0)
    common = dict(
        w_ada=f(inp["w_ada"][0]), vecs=f(vecs), b_ada=f(inp["b_ada"][0]), post1=f(inp["norm_post1"][0]), post2=f(inp["norm_post2"][0]),
        q_norm=f(inp["mla_q_norm"][0]), kv_norm=f(inp["mla_kv_norm"][0]),
        w_uq=f(w_uq.reshape(384, 768)), w_uq_sw=f(sw), w_uk=f(w_ukv[:, :, 0:64]), w_uv=f(w_ukv[:, :, 64:128].reshape(256, 512)),
        w_mla_o=f(inp["w_mla_o"][0]), w_mlstm_o=f(inp["w_mlstm_o"][0]), w_out=f(inp["w_out"][0]),
        w_mlp1=f(inp["w_mlp1"][0]), w_mlp2=f(inp["w_mlp2"][0]), ident=ident, trif=trif, trib=trib, esel=esel,
    )
    tabs = {False: _rope_tables(False), True: _rope_tables(True)}
    maps = []
    for r in range(8):
        b, rev = r // 2, (r % 2 == 1)
        xs = np.asarray(inp["x_sample"][b], np.float32)
        xp = np.asarray(inp["x_prompt"][4 * r:4 * r + 4], np.float32)
        C0 = np.asarray(inp["state_mlstm_C"][b, 0], np.float32)
        n0 = np.asarray(inp["state_mlstm_n"][b, 0], np.float32)
        m0 = np.asarray(inp["state_mlstm_m"][b, 0], np.float32)
        if rev:
            xs, xp, C0, n0, m0 = xs[::-1], xp[:, ::-1], C0[::-1], n0[::-1], m0[::-1]
        cos, sin, qtab = tabs[rev]
        m = dict(common)
        m.update(
            xs=f(xs), xp=f(xp.reshape(NTP, D)), cckv=f(inp["cache_mla_ckv"][b, 0]), ckr=f(inp["cache_mla_krope"][b, 0]),
            C0=f(C0), n0=f(n0), m0=f(m0.reshape(8)), cvec=f(np.stack([np.asarray(inp["c"][b]), np.asarray(inp["c_ctx"])])),
            w_in=(w_in_sw if rev else w_in), gate_b=f((gate_b[::-1] if rev else gate_b).reshape(16)),
            qtab=f(qtab), kcos=f(cos), ksin=f(sin),
        )
        maps.append(m)
    return maps


def assemble(results):
    y_p = np.zeros((32, 256, D), np.float32)
    y_s = np.zeros((4, 4096, D), np.float32)
    ckv = np.zeros((32, 1, 256, 256), np.float32)
    kr = np.zeros((32, 1, 256, 32), np.float32)
    Cn = np.zeros((32, 1, 2, 4, 128, 128), np.float32)
    nn = np.zeros((32, 1, 2, 4, 128), np.float32)
    mm = np.zeros((32, 1, 2, 4), np.float32)
    for r, res in enumerate(results):
        b, rev = r // 2, (r % 2 == 1)
        yp = np.asarray(res["yp"]).reshape(4, 256, D)
        ock = np.asarray(res["o_ckv"]).reshape(4, 256, 256)
        okr = np.asarray(res["o_kr"]).reshape(4, 256, 32)
        oC = np.asarray(res["o_C"]).reshape(4, 2, 4, 128, 128)
        on = np.asarray(res["o_n"]).reshape(4, 2, 4, 128)
        om = np.asarray(res["o_m"]).reshape(4, 2, 4)
        ys = np.asarray(res["ys"])
        if rev:
            yp, ock, okr, oC, on, om, ys = yp[:, ::-1], ock[:, ::-1], okr[:, ::-1], oC[:, ::-1], on[:, ::-1], om[:, ::-1], ys[::-1]
            y_s[b, 2048:] = ys
        else:
            y_s[b, :2048] = ys
        sl = slice(4 * r, 4 * r + 4)
        y_p[sl], ckv[sl, 0], kr[sl, 0], Cn[sl, 0], nn[sl, 0], mm[sl, 0] = yp, ock, okr, oC, on, om
    return (y_p, y_s, ckv, kr, Cn, nn, mm)


def kernel(**inputs):
    if "nc" not in _CACHE:
        _CACHE["nc"] = Prog().build()
    maps = make_in_maps(inputs)
    res = run_bass_kernel_spmd(_CACHE["nc"], maps, core_ids=list(range(8)))
    return assemble(res.results)
```

```python
import numpy as np
from contextlib import ExitStack
import concourse.bass as bass
import concourse.mybir as mybir
from concourse.bass_utils import run_bass_kernel_spmd

F32 = mybir.dt.float32
BF16 = mybir.dt.bfloat16
AF = mybir.ActivationFunctionType
ALU = mybir.AluOpType
AX = mybir.AxisListType

D = 1024
NTP = 1024
NSO = 2048
NST = 4096
PAST = 512
EPS = 1e-6
IN_COLS = 4784
C_Q, C_K, C_V, C_O, C_G, C_CQ, C_CKV, C_KR, C_MG = 0, 512, 1024, 1536, 2048, 2064, 2448, 2704, 2736
SCQ = 96.0 ** -0.5


class Res:
    __slots__ = ("w", "r", "rd")

    def __init__(self):
        self.w = []
        self.r = {}
        self.rd = []


class Buf:
    def __init__(self, t, excl=False):
        self.t = t
        self.res = Res()
        self.excl = excl

    def __getitem__(self, k):
        return self.t[k]


def tiled(buf, n):
    buf.tiles = [Buf(buf.t) for _ in range(n)]
    return buf


class KB:
    ENG = ("pe", "act", "dve", "pool", "sp")

    def __init__(self, nc, es):
        self.nc = nc
        self.es = es
        self.e = {"pe": nc.tensor, "act": nc.scalar, "dve": nc.vector, "pool": nc.gpsimd, "sp": nc.sync}
        self.sem = {k: es.enter_context(nc.semaphore("s_" + k)) for k in self.ENG}
        self.cnt = {k: 0 for k in self.ENG}
        self.known = {k: {} for k in self.ENG}
        self.dsl = {}
        for q, n in (("sp", 24), ("pool", 12), ("act", 4)):
            self.dsl[q] = [{"sem": es.enter_context(nc.semaphore("d_%s%d" % (q, i))), "tot": 0, "i": i}
                           for i in range(n)]
        self.dptr = {"sp": 0, "pool": 0, "act": 0}
        self.n_inst = 0
        self.snap = {}

    def need(self, e, ev):
        kind, key, n = ev
        kk = (kind, key)
        if self.known[e].get(kk, 0) >= n:
            return None
        if kind == "eng":
            if key == e and e in ("pe", "sp"):
                return None
            sem = self.sem[key]
        else:
            q, i = key
            sem = self.dsl[q][i]["sem"]
        self.known[e][kk] = n
        sn = self.snap.get(ev)
        if sn:
            kn = self.known[e]
            for k2, v2 in sn.items():
                if kn.get(k2, 0) < v2:
                    kn[k2] = v2
        return (sem, n)

    def wait(self, e, ev):
        w = self.need(e, ev)
        if w is not None:
            self.e[e].wait_ge(w[0], w[1])
            self.n_inst += 1

    def waits_attach(self, e, evs):
        ws = []
        for ev in evs:
            w = self.need(e, ev)
            if w is not None:
                ws.append(w)
        best = {}
        for sem, n in ws:
            key = id(sem)
            if key not in best or best[key][1] < n:
                best[key] = (sem, n)
        ws = list(best.values())
        for sem, n in ws[:-1]:
            self.e[e].wait_ge(sem, n)
            self.n_inst += 1
        return ws[-1] if ws else None

    def _deps(self, R, W, e=None):
        deps = []
        for b in R:
            deps += b.res.w
            if b.excl:
                deps += [("eng", k, v) for k, v in b.res.r.items() if k != e]
        for b in W:
            rs = b.res
            deps += rs.w
            deps += [("eng", k, v) for k, v in rs.r.items()]
            deps += rs.rd
        return deps

    def _mark(self, ev, R, W):
        for b in R:
            if ev[0] == "eng":
                b.res.r[ev[1]] = ev[2]
            else:
                b.res.rd.append(ev)
        for b in W:
            b.res.w = [ev]
            b.res.r = {}
            b.res.rd = []

    def op(self, e, fn, R=(), W=(), sig=True):
        last = self.waits_attach(e, self._deps(R, W, e))
        n0 = self.nc.n_instructions()
        ins = fn()
        assert self.nc.n_instructions() - n0 == 1, "multi-instruction op"
        if last is not None:
            ins._wait_ge(last[0], last[1])
        self.n_inst += 1
        if sig:
            self.cnt[e] += 1
            ins.then_inc(self.sem[e], 1)
            ev = ("eng", e, self.cnt[e])
            sn = dict(self.known[e])
            sn[("eng", e)] = self.cnt[e]
            self.snap[ev] = sn
        else:
            ev = ("eng", e, self.cnt[e] + 1)
        self._mark(ev, R, W)
        return ev

    def dma(self, q, out, in_, R=(), W=()):
        sl = self.dsl[q][self.dptr[q]]
        self.dptr[q] = (self.dptr[q] + 1) % len(self.dsl[q])
        deps = self._deps(R, W)
        if sl["tot"] > 0:
            deps.append(("dma", (q, sl["i"]), sl["tot"]))
        last = self.waits_attach(q, deps)
        sl["tot"] += 16
        ins = self.e[q].dma_start(out=out, in_=in_)
        if last is not None:
            ins._wait_ge(last[0], last[1])
        ins.then_inc(sl["sem"], 16)
        self.n_inst += 1
        ev = ("dma", (q, sl["i"]), sl["tot"])
        self.snap[ev] = dict(self.known[q])
        self._mark(ev, R, W)
        return ev

    def barrier(self):
        evs = [("eng", k, self.cnt[k]) for k in ("pe", "act", "dve", "pool") if self.cnt[k] > 0]
        for q in self.dsl:
            for sl in self.dsl[q]:
                if sl["tot"] > 0:
                    evs.append(("dma", (q, sl["i"]), sl["tot"]))
        for e in self.ENG:
            for ev in evs:
                self.wait(e, ev)


class Ring:
    def __init__(self, bufs):
        self.b = bufs
        self.i = 0

    def next(self):
        b = self.b[self.i]
        self.i = (self.i + 1) % len(self.b)
        return b


class Prog:
    def __init__(self, do_prompt=True, do_sample=True, stop=None):
        self.do_prompt = do_prompt
        self.do_sample = do_sample
        self.stop = stop

    def ck(self, name):
        if self.stop == name:
            self.stopped = True
        return getattr(self, "stopped", False)

    def alloc(self, es, name, shape, dt):
        self._an = getattr(self, "_an", 0) + 1
        t = es.enter_context(self.nc.sbuf_tensor("sb%d_%s" % (self._an, name), list(shape), dt))
        return Buf(t)

    def dram_in(self, name, shape, dt=F32):
        return self.nc.dram_tensor(name, list(shape), dt, kind="ExternalInput").ap()

    def dram_out(self, name, shape, dt=F32):
        return self.nc.dram_tensor(name, list(shape), dt, kind="ExternalOutput").ap()

    def V(self, fn, R=(), W=(), e="dve"):
        return self.k.op(e, fn, R, W)

    def psb(self):
        return self.psr.next()

    def evac_engine(self):
        self._ev = 1 - getattr(self, "_ev", 0)
        return "act" if self._ev else "dve"

    def copy(self, e, out, in_, R, W, scale=None):
        nc = self.nc
        if e == "act":
            if scale is None:
                return self.k.op("act", lambda: nc.scalar.copy(out=out, in_=in_), R, W)
            return self.k.op("act", lambda: nc.scalar.activation(out=out, in_=in_, func=AF.Copy, scale=scale), R, W)
        eng = nc.vector if e == "dve" else nc.gpsimd
        if scale is None:
            return self.k.op(e, lambda: eng.tensor_copy(out=out, in_=in_), R, W)
        return self.k.op(e, lambda: eng.tensor_scalar(out=out, in0=in_, scalar1=scale, scalar2=None, op0=ALU.mult), R, W)

    def rstd_from_ss(self, ss, col, n, R):
        nc = self.nc
        self.k.op("act", lambda: nc.scalar.activation(out=ss.t[:, col + 1:col + 2], in_=ss.t[:, col:col + 1],
                                                      func=AF.Ln, scale=1.0 / n, bias=EPS), [ss], [ss])
        self.k.op("act", lambda: nc.scalar.activation(out=ss.t[:, col + 2:col + 3], in_=ss.t[:, col + 1:col + 2],
                                                      func=AF.Exp, scale=-0.5), [ss], [ss])
        return ss.t[:, col + 2:col + 3]

    def wstream(self, reqs, depth=2):
        self._wreqs = list(reqs)
        self._wissued = 0
        self._wdepth = depth
        self._wtiles = {}

    def wget(self, i):
        while self._wissued < len(self._wreqs) and self._wissued <= i + self._wdepth - 1:
            j = self._wissued
            ap, kc, ncols = self._wreqs[j]
            slot = self.wring.next()
            view = slot.t[:, 0:kc * ncols].rearrange("p (k n) -> p k n", k=kc)
            self.k.dma("pool", view, ap.rearrange("(k p) n -> p k n", p=128), [], [slot])
            self._wtiles[j] = (view, slot)
            self._wissued += 1
        return self._wtiles.pop(i) if i >= 0 else None

    def build(self):
        nc = bass.Bass("TRN2", target_bir_lowering=False)
        self.nc = nc
        di = self.dram_in
        self.d = dict(
            xs=di("xs", [NST, D]), xp=di("xp", [NTP, D]),
            cckv=di("cckv", [PAST, 256]), ckr=di("ckr", [PAST, 32]),
            C0=di("C0", [2, 4, 128, 128]), n0=di("n0", [2, 4, 128]), m0=di("m0", [8]),
            cvec=di("cvec", [2, D]), w_ada=di("w_ada", [D, 6 * D]),
            vecs=di("vecs", [68, 128]),
            b_ada=di("b_ada", [6 * D]), post1=di("post1", [D]), post2=di("post2", [D]),
            w_in=di("w_in", [D, IN_COLS]), gate_b=di("gate_b", [16]),
            q_norm=di("q_norm", [384]), kv_norm=di("kv_norm", [256]),
            w_uq=di("w_uq", [384, 768]), w_uq_sw=di("w_uq_sw", [384, 8, 32]),
            w_uk=di("w_uk", [256, 8, 64]), w_uv=di("w_uv", [256, 512]),
            w_mla_o=di("w_mla_o", [512, D]), w_mlstm_o=di("w_mlstm_o", [512, D]),
            w_out=di("w_out", [D, D]), w_mlp1=di("w_mlp1", [D, 4 * D]), w_mlp2=di("w_mlp2", [4 * D, D]),
            ident=di("ident", [128, 128]), trif=di("trif", [128, 128]), trib=di("trib", [128, 128]),
            esel=di("esel", [32, 128]), qtab=di("qtab", [64, NSO]), kcos=di("kcos", [NST, 16]), ksin=di("ksin", [NST, 16]),
        )
        do = self.dram_out
        self.o = dict(
            ys=do("ys", [NSO, D]), yp=do("yp", [NTP, D]), o_ckv=do("o_ckv", [NTP, 256]), o_kr=do("o_kr", [NTP, 32]),
            o_C=do("o_C", [4, 8, 128, 128]), o_n=do("o_n", [4, 8, 128]), o_m=do("o_m", [4, 8]),
        )
        with ExitStack() as es:
            self.k = KB(nc, es)
            self.psum = [Buf(es.enter_context(nc.psum_tensor("pb%d" % i, [128, 512], F32)), excl=True) for i in range(8)]
            self.psr = Ring(self.psum)
            self.persistent(es)
            self.phase0()
            if not self.ck("p0"):
                if self.do_prompt:
                    self.run_group("p")
                if self.do_sample and not self.ck("-"):
                    self.run_group("s")
            self.k.barrier()
        return nc

    def persistent(self, es):
        nc, k, A = self.nc, self.k, self.alloc
        d = self.d
        self.ident_f = A(es, "ident_f", [128, 128], F32)
        self.ident_b = A(es, "ident_b", [128, 128], BF16)
        self.trif = A(es, "trif", [128, 128], F32)
        self.trib = A(es, "trib", [128, 128], F32)
        self.ones_f = A(es, "ones_f", [128, 128], F32)
        self.ones_b = A(es, "ones_b", [128, 128], BF16)
        self.modAB = A(es, "modAB", [128, 4, 8, 2], F32)
        self.hnT = A(es, "hnT", [128, 4], F32)
        self.sT = A(es, "sT", [128, 8, 2], F32)
        self.qn_row = A(es, "qn_row", [128, 384], F32)
        self.kvn_row = A(es, "kvn_row", [128, 256], F32)
        self.gb_row = A(es, "gb_row", [128, 16], F32)
        self.smr = Ring([A(es, "sm%d" % i, [128, 16], F32) for i in range(16)])
        self.Gs = (A(es, "Gs1", [128, D], F32), A(es, "Gs2", [128, D], F32))
        k.dma("sp", self.ident_f.t[:], d["ident"], [], [self.ident_f])
        k.dma("sp", self.trif.t[:], d["trif"], [], [self.trif])
        k.dma("sp", self.trib.t[:], d["trib"], [], [self.trib])
        k.dma("sp", self.qn_row.t[:], d["q_norm"].partition_broadcast(128), [], [self.qn_row])
        k.dma("sp", self.kvn_row.t[:], d["kv_norm"].partition_broadcast(128), [], [self.kvn_row])
        k.dma("sp", self.gb_row.t[:], d["gate_b"].partition_broadcast(128), [], [self.gb_row])
        self.V(lambda: nc.vector.tensor_copy(out=self.ident_b.t[:], in_=self.ident_f.t[:]), [self.ident_f], [self.ident_b])
        self.V(lambda: nc.vector.memset(self.ones_f.t[:], 1.0), [], [self.ones_f])
        self.V(lambda: nc.vector.memset(self.ones_b.t[:], 1.0), [], [self.ones_b])

    def phase0(self):
        nc, k, d = self.nc, self.k, self.d
        with ExitStack() as es:
            A = self.alloc
            self.wring = Ring([A(es, "w0_%d" % i, [128, 8 * 512], BF16) for i in range(4)])
            cv = A(es, "cv", [2, D], F32)
            ce = A(es, "ce", [2, D], F32)
            vt = A(es, "vt", [68, 128], F32)
            vT = A(es, "vT", [128, 68], F32)
            sTb = A(es, "sTb", [128, 8, 2], BF16)
            modT = A(es, "modT", [128, 4, 8, 2], F32)
            k.dma("sp", cv.t[:], d["cvec"], [], [cv])
            k.dma("sp", vt.t[:], d["vecs"], [], [vt])
            k.op("act", lambda: nc.scalar.activation(out=ce.t[:], in_=cv.t[:], func=AF.Exp, scale=-1.0), [cv], [ce])
            self.V(lambda: nc.vector.tensor_scalar(out=ce.t[:], in0=ce.t[:], scalar1=1.0, scalar2=None, op0=ALU.add), [ce], [ce])
            self.V(lambda: nc.vector.reciprocal(out=ce.t[:], in_=ce.t[:]), [ce], [ce])
            self.V(lambda: nc.vector.tensor_tensor(out=cv.t[:], in0=cv.t[:], in1=ce.t[:], op=ALU.mult), [cv, ce], [cv])
            pb = self.psb()
            for kc in range(8):
                k.op("pe", lambda kc=kc: nc.tensor.transpose(out=pb.t[:, kc * 2:kc * 2 + 2], in_=cv.t[:, kc * 128:(kc + 1) * 128],
                                                             identity=self.ident_f.t[0:2, 0:2]), [cv, self.ident_f], [pb])
            self.V(lambda: nc.vector.tensor_copy(out=self.sT.t[:], in_=pb.t[:, 0:16].rearrange("p (k g) -> p k g", g=2)), [pb], [self.sT])
            self.V(lambda: nc.vector.tensor_copy(out=sTb.t[:], in_=self.sT.t[:]), [self.sT], [sTb])
            pb2 = self.psb()
            k.op("pe", lambda: nc.tensor.transpose(out=pb2.t[:, 0:68], in_=vt.t[:], identity=self.ident_f.t[0:68, 0:68]),
                 [vt, self.ident_f], [pb2])
            self.V(lambda: nc.vector.tensor_copy(out=vT.t[:], in_=pb2.t[:, 0:68]), [pb2], [vT])
            self.V(lambda: nc.vector.tensor_copy(out=self.hnT.t[:], in_=vT.t[:, 64:68]), [vT], [self.hnT])
            mods = (0, 1, 3, 4)
            self.wstream([(d["w_ada"][:, mi * D + j * 512: mi * D + (j + 1) * 512], 8, 512) for mi in mods for j in range(2)])
            pm = self.psb()
            wi = 0
            for a, mi in enumerate(mods):
                for j in range(2):
                    W, ws = self.wget(wi)
                    wi += 1
                    for c4 in range(4):
                        fc = j * 4 + c4
                        for kc in range(8):
                            k.op("pe", lambda kc=kc, c4=c4, a=a, fc=fc, W=W: nc.tensor.matmul(
                                pm.t[:, (a * 8 + fc) * 2:(a * 8 + fc) * 2 + 2], lhsT=W[:, kc, c4 * 128:(c4 + 1) * 128],
                                rhs=sTb.t[:, kc, :], start=(kc == 0), stop=(kc == 7)), [ws, sTb], [pm], sig=(kc == 7))
            for a, mi in enumerate(mods):
                self.V(lambda a=a, mi=mi: nc.vector.tensor_tensor(
                    out=modT.t[:, a], in0=pm.t[:, a * 16:(a + 1) * 16].rearrange("p (f g) -> p f g", g=2),
                    in1=vT.t[:, mi * 8:(mi + 1) * 8].unsqueeze(2).broadcast_to([128, 8, 2]), op=ALU.add), [pm, vT], [modT])
            for blk, (a_shift, a_scale, prow) in enumerate(((0, 1, 48), (2, 3, 56))):
                self.V(lambda a_scale=a_scale, prow=prow, blk=blk: nc.vector.scalar_tensor_tensor(
                    out=self.modAB.t[:, 2 * blk], in0=modT.t[:, a_scale], scalar=1.0,
                    in1=vT.t[:, prow:prow + 8].unsqueeze(2).broadcast_to([128, 8, 2]), op0=ALU.add, op1=ALU.mult),
                    [modT, vT], [self.modAB])
                self.V(lambda a_shift=a_shift, blk=blk: nc.vector.tensor_copy(out=self.modAB.t[:, 2 * blk + 1], in_=modT.t[:, a_shift]),
                       [modT], [self.modAB])
            k.barrier()

    def norm_transpose(self, x_ap, x_buf, gi, blk, hT, col0):
        nc, k = self.nc, self.k
        ss = self.smr.next()
        xnb = self.xnb.next()
        k.op("act", lambda: nc.scalar.activation(out=xnb.t[:], in_=x_ap, func=AF.Square, accum_out=ss.t[:, 0:1]),
             [x_buf], [xnb, ss])
        rstd = self.rstd_from_ss(ss, 0, D, [])
        k.op("act", lambda: nc.scalar.activation(out=xnb.t[:], in_=x_ap, func=AF.Copy, scale=rstd), [x_buf, ss], [xnb])
        pb = self.psb()
        pv = pb.t[:].bitcast(BF16).rearrange("p (f t) -> p f t", t=128)
        for fc in range(8):
            k.op("pe", lambda fc=fc: nc.tensor.transpose(out=pv[:, fc, :], in_=xnb.t[:, fc * 128:(fc + 1) * 128],
                                                         identity=self.ident_b.t[:]), [xnb, self.ident_b], [pb], sig=(fc == 7))
        A = self.modAB.t[:, 2 * blk]
        B = self.modAB.t[:, 2 * blk + 1]
        evt = self.evt.next()
        self.V(lambda: nc.vector.tensor_tensor(out=evt.t[:], in0=pv, in1=A[:, :, gi:gi + 1].broadcast_to([128, 8, 128]), op=ALU.mult),
               [pb, self.modAB], [evt])
        self.V(lambda: nc.vector.tensor_tensor(out=hT.t[:, :, col0:col0 + 128], in0=evt.t[:], in1=B[:, :, gi:gi + 1].broadcast_to([128, 8, 128]),
                                               op=ALU.add), [evt, self.modAB], [hT.tiles[col0 // 128]])

    def mm_group(self, out_ap, pb, pairs, R):
        nc, k = self.nc, self.k
        n = len(pairs)
        ev = None
        for i, (l, r) in enumerate(pairs):
            ev = k.op("pe", lambda l=l, r=r, i=i: nc.tensor.matmul(out_ap, lhsT=l, rhs=r, start=(i == 0), stop=(i == n - 1)),
                      R, [pb], sig=(i == n - 1))
        return ev

    def run_group(self, g):
        nc, k, d, o = self.nc, self.k, self.d, self.o
        A = self.alloc
        samp = (g == "s")
        gi = 0 if samp else 1
        n_own = 16 if samp else 8
        nt = 32 if samp else 8
        nkey = 36 if samp else 8
        x_own = d["xs"] if samp else d["xp"]
        with ExitStack() as gs:
            zmT = A(gs, g + "zmT", [128, 4, n_own * 128], BF16)
            zaT = A(gs, g + "zaT", [128, 4, n_own * 128], BF16)
            with ExitStack() as ms:
                cqnT = A(ms, g + "cqnT", [128, 3, n_own * 128], BF16)
                ckvT = A(ms, g + "ckvT", [128, 2, nkey * 128], BF16)
                krT = A(ms, g + "krT", [32, nkey * 128], BF16)
                with ExitStack() as ls:
                    nseq = 1 if samp else 4
                    qT = A(ls, g + "qT", [128, 4, n_own * 128], BF16)
                    kT = A(ls, g + "kT", [128, 4, n_own * 128], BF16)
                    vaug = A(ls, g + "vaug", [128, n_own, 4, 129], BF16)
                    graw = A(ls, g + "graw", [128, nt, 16], F32)
                    gsc = {nm: A(ls, g + nm, [128, nt, 8], F32)
                           for nm in ("lf", "b", "bL", "a", "amax", "ref", "dm", "mall", "u", "dec", "E")}
                    self.gtmp = Ring([A(ls, g + "gtmp%d" % i, [128, 256], F32) for i in range(3)])
                    S = A(ls, g + "S", [128, nseq, 8, 129], F32)
                    Sres = [[Buf(S.t[:, s_, j_, :]) for j_ in range(8)] for s_ in range(nseq)]
                    mcur0 = A(ls, g + "mcur0", [128, nseq, 8], F32)
                    self.V(lambda: nc.vector.memset(vaug.t[:, :, :, 128:129], 1.0), [], [vaug])
                    for nm in gsc:
                        self.V(lambda nm=nm: nc.vector.memset(gsc[nm].t[:], 0.0), [], [gsc[nm]])
                    if samp:
                        for j_ in range(8):
                            k.dma("sp", Sres[0][j_].t[:, 0:128], d["C0"][j_ // 4, j_ % 4], [], [Sres[0][j_]])
                        n8 = self.smr.next()
                        n8v = self.gtmp.next()
                        k.dma("sp", n8v.t[0:8, 0:128], d["n0"].rearrange("r h d -> (r h) d"), [], [n8v])
                        pbn = self.psb()
                        k.op("pe", lambda: nc.tensor.transpose(out=pbn.t[:, 0:8], in_=n8v.t[0:8, 0:128], identity=self.ident_f.t[0:8, 0:8]),
                             [n8v, self.ident_f], [pbn])
                        self.V(lambda: nc.vector.tensor_copy(out=S.t[:, 0, :, 128:129], in_=pbn.t[:, 0:8].unsqueeze(2)), [pbn], Sres[0])
                        k.dma("sp", mcur0.t[:, 0, :], d["m0"].partition_broadcast(128), [], [mcur0])
                    else:
                        self.V(lambda: nc.vector.memset(S.t[:], 0.0), [], [b_ for r_ in Sres for b_ in r_])
                        self.V(lambda: nc.vector.memset(mcur0.t[:], -1e30), [], [mcur0])
                    mcur = None
                    with ExitStack() as ps_:
                        self.wring = Ring([A(ps_, g + "w1_%d" % i, [128, 8 * 512], BF16) for i in range(3)])
                        self.xring = Ring([A(ps_, g + "xt%d" % i, [128, D], F32) for i in range(2)])
                        self.xnb = Ring([A(ps_, g + "xnb%d" % i, [128, D], BF16) for i in range(2)])
                        self.junk = Ring([A(ps_, g + "junk%d" % i, [128, 512], BF16) for i in range(1)])
                        self.evt = Ring([A(ps_, g + "evt%d" % i, [128, 8, 128], F32) for i in range(1 if samp else 2)])
                        if samp:
                            hT = tiled(Buf(zmT.t[:].rearrange("p a b -> p (a b)").rearrange("p (f t) -> p f t", f=8)), 8)
                        else:
                            hT = tiled(A(ps_, g + "hT", [128, 8, 1024], BF16), 8)
                        bufs = dict(qT=qT, kT=kT, vaug=vaug, graw=graw, cqnT=cqnT, ckvT=ckvT, krT=krT, hT=hT)
                        bufs["tokw"] = Ring([A(ps_, g + "tokw%d" % i, [128, 288], F32) for i in range(2)])
                        bufs["tokb"] = Ring([A(ps_, g + "tokb%d" % i, [128, 384], BF16) for i in range(3)])
                        bufs["rot"] = Ring([A(ps_, g + "rot%d" % i, [128, 6, 16], F32) for i in range(2)])
                        bufs["cs"] = Ring([A(ps_, g + "cs%d" % i, [128, 2, 16], F32) for i in range(3)])
                        if samp:
                            zflat = zaT.t[:].rearrange("p a b -> p (a b)")
                            ktok = Buf(zflat[:, 0:4096].rearrange("p (i c) -> p i c", c=512))
                            vo = Buf(zflat[:, 4096:8192].rearrange("p (i c) -> p i c", c=512))
                            wk_r = Ring([A(ps_, g + "wkr%d" % i, [128, 128], BF16) for i in range(3)])
                            bufs["ktok"] = ktok
                            bufs["vo"] = vo
                            mview = mcur0.t[:, 0:1, 4:8]
                            mb = [mcur0]
                            for tb in (3, 2):
                                self.proj_pass(g, gi, x_own, tb, False, bufs)
                                tiles = list(range(tb * 8 + 7, tb * 8 - 1, -1))
                                self.gates_phase(graw, gsc, tb * 8, tb * 8 + 8)
                                mview = self.gates_scan(gsc, tiles, 1, mview, mb, 1)
                                mb = [gsc["mall"]]
                                self.gates_bulk(gsc, tb * 8, tb * 8 + 8)
                                for c in tiles:
                                    for h in range(4):
                                        self.mlstm_state_only(gsc, Sres[0][4 + h], ktok, vo, c, c - tb * 8, h, wk_r)
                            mcur = mview
                        for tb in range(n_own // 8):
                            self.proj_pass(g, gi, x_own, tb, True, bufs)
                        k.barrier()
                        if self.ck(g + "proj"):
                            return
                    with ExitStack() as sw:
                        hacc = A(sw, g + "hacc", [128, n_own, 512], F32)
                        self.gates_phase(graw, gsc, 0, n_own)
                        if samp:
                            self.gates_scan(gsc, list(range(16)), 0, mcur0.t[:, 0:1, 0:4], [mcur0], 1)
                            self.gates_scan(gsc, list(range(15, -1, -1)), 1, mcur, [gsc["mall"]], 1)
                        else:
                            self.gates_scan(gsc, [0, 1], 0, mcur0.t[:, :, 0:4], [mcur0], 4)
                            self.gates_scan(gsc, [1, 0], 1, mcur0.t[:, :, 4:8], [mcur0], 4)
                        self.gates_bulk(gsc, 0, n_own)
                        if self.ck(g + "gates"):
                            k.barrier()
                            return
                        self.mlstm_sweeps(g, gsc, Sres, qT, kT, vaug, hacc, zmT, sw, nseq, n_own)
                        if not samp:
                            for s_ in range(4):
                                for j_ in range(8):
                                    k.dma("sp", o["o_C"][s_, j_], Sres[s_][j_].t[:, 0:128], [Sres[s_][j_]], [])
                            pbn = self.psb()
                            for s_ in range(4):
                                k.op("pe", lambda s_=s_: nc.tensor.transpose(out=pbn.t[0:8, s_ * 128:(s_ + 1) * 128], in_=S.t[:, s_, :, 128],
                                                                             identity=self.ident_f.t[:]), Sres[s_] + [self.ident_f], [pbn])
                            for hf in range(2):
                                n_out = self.gtmp.next()
                                self.V(lambda: nc.vector.tensor_copy(out=n_out.t[0:8, 0:256], in_=pbn.t[0:8, hf * 256:(hf + 1) * 256]), [pbn], [n_out])
                                for s2 in range(2):
                                    k.dma("sp", o["o_n"][hf * 2 + s2], n_out.t[0:8, s2 * 128:(s2 + 1) * 128], [n_out], [])
                            mall = gsc["mall"]
                            for s_ in range(4):
                                k.dma("sp", o["o_m"][s_:s_ + 1, 0:4], mall.t[0:1, s_ * 2 + 1, 0:4], [mall], [])
                                k.dma("sp", o["o_m"][s_:s_ + 1, 4:8], mall.t[0:1, s_ * 2 + 0, 4:8], [mall], [])
                        k.barrier()
                        if self.ck(g + "sweeps"):
                            return
                with ExitStack() as at:
                    self.attention(g, at, cqnT, ckvT, krT, zaT, n_own, nkey)
                    k.barrier()
                    if self.ck(g + "attn"):
                        return
            self.join_mlp(g, gi, x_own, zmT, zaT, n_own)
            k.barrier()

    def proj_pass(self, g, gi, x_dram, tb, own, B):
        nc, k, d, o = self.nc, self.k, self.d, self.o
        samp = (g == "s")
        hT = B["hT"]
        t0 = tb * 8
        w = d["w_in"]
        reqs = []
        if own:
            reqs += [(w[:, C_Q:C_Q + 512], 8, 512)]
        reqs += [(w[:, C_K:C_K + 512], 8, 512), (w[:, C_V:C_V + 512], 8, 512), (w[:, C_G:C_G + 16], 8, 16)]
        if own:
            reqs += [(w[:, C_CQ:C_CQ + 384], 8, 384)]
        reqs += [(w[:, C_CKV:C_CKV + 288], 8, 288)]
        self.wstream(reqs)
        self.wget(-1)
        xts = []

        def issue(i):
            b = self.xring.next()
            r0 = (t0 + i) * 128
            k.dma("sp", b.t[:], x_dram[r0:r0 + 128, :], [], [b])
            xts.append(b)
        issue(0)
        issue(1)
        for i in range(8):
            self.norm_transpose(xts[i].t[:], xts[i], gi, 0, hT, i * 128)
            if i + 2 < 8:
                issue(i + 2)
        wi = 0
        if self.ck("pp1"):
            return

        def featmajor(W, ws, dst, scale):
            for h in range(4):
                for half in range(2):
                    pb = self.psb()
                    self.mm_group(pb.t[:, :], pb, [(W[:, kc, h * 128:(h + 1) * 128], hT.t[:, kc, half * 512:(half + 1) * 512])
                                                   for kc in range(8)], [ws] + hT.tiles[half * 4:half * 4 + 4])
                    c0 = tb * 1024 + half * 512
                    self.copy(self.evac_engine(), dst.t[:, h, c0:c0 + 512], pb.t[:, :], [pb], [dst], scale=scale)

        def tokmajor(W, ws, ncols, fn, defer=False):
            prev = None
            for i in range(8):
                pb = self.psb()
                self.mm_group(pb.t[:, 0:ncols], pb, [(hT.t[:, kc, i * 128:(i + 1) * 128], W[:, kc, :]) for kc in range(8)], [ws, hT.tiles[i]])
                if defer:
                    post = fn(i, pb)
                    if prev is not None:
                        prev()
                    prev = post
                else:
                    fn(i, pb)
            if prev is not None:
                prev()

        if own:
            W, ws = self.wget(wi); wi += 1
            featmajor(W, ws, B["qT"], 128.0 ** -0.5)
            if self.ck("pp2"):
                return
            W, ws = self.wget(wi); wi += 1
            featmajor(W, ws, B["kT"], None)
        else:
            W, ws = self.wget(wi); wi += 1
            tokmajor(W, ws, 512, lambda i, pb: self.copy(self.evac_engine(), B["ktok"].t[:, i, :], pb.t[:, :], [pb], [B["ktok"]]))
        if self.ck("pp3"):
            return
        W, ws = self.wget(wi); wi += 1
        if own:
            tokmajor(W, ws, 512, lambda i, pb: self.copy(self.evac_engine(), B["vaug"].t[:, t0 + i, :, 0:128],
                                                         pb.t[:, :].rearrange("p (h e) -> p h e", h=4), [pb], [B["vaug"]]))
        else:
            tokmajor(W, ws, 512, lambda i, pb: self.copy(self.evac_engine(), B["vo"].t[:, i, :], pb.t[:, :], [pb], [B["vo"]]))
        W, ws = self.wget(wi); wi += 1
        tokmajor(W, ws, 16, lambda i, pb: self.copy("dve", B["graw"].t[:, t0 + i, :], pb.t[:, 0:16], [pb], [B["graw"]]))
        if self.ck("pp5"):
            return
        if own:
            W, ws = self.wget(wi); wi += 1

            def cq_fn(i, pb):
                ss = self.smr.next()
                junk = self.junk.next()
                k.op("act", lambda: nc.scalar.activation(out=junk.t[:, 0:384], in_=pb.t[:, 0:384], func=AF.Square,
                                                         accum_out=ss.t[:, 0:1]), [pb], [junk, ss])
                rstd = self.rstd_from_ss(ss, 0, 384, [])
                tb_ = B["tokb"].next()
                self.V(lambda: nc.vector.scalar_tensor_tensor(out=tb_.t[:, 0:384], in0=pb.t[:, 0:384], scalar=rstd,
                                                              in1=self.qn_row.t[:], op0=ALU.mult, op1=ALU.mult),
                       [pb, ss, self.qn_row], [tb_])

                def post():
                    p2 = self.psb()
                    pv = p2.t[:].bitcast(BF16).rearrange("p (f t) -> p f t", t=128)
                    for kc in range(3):
                        k.op("pe", lambda kc=kc: nc.tensor.transpose(out=pv[:, kc, :], in_=tb_.t[:, kc * 128:(kc + 1) * 128],
                                                                     identity=self.ident_b.t[:]), [tb_, self.ident_b], [p2], sig=(kc == 2))
                    c0 = (t0 + i) * 128
                    self.copy(self.evac_engine(), B["cqnT"].t[:, :, c0:c0 + 128], pv[:, 0:3, :], [p2], [B["cqnT"]])
                return post
            tokmajor(W, ws, 384, cq_fn, defer=True)
        if self.ck("pp6"):
            return
        W, ws = self.wget(wi); wi += 1

        def ckv_fn(i, pb):
            ti = t0 + i
            ss = self.smr.next()
            junk = self.junk.next()
            k.op("act", lambda: nc.scalar.activation(out=junk.t[:, 0:256], in_=pb.t[:, 0:256], func=AF.Square,
                                                     accum_out=ss.t[:, 0:1]), [pb], [junk, ss])
            rstd = self.rstd_from_ss(ss, 0, 256, [])
            tw = B["tokw"].next()
            tb_ = B["tokb"].next()
            self.V(lambda: nc.vector.scalar_tensor_tensor(out=tw.t[:, 0:256], in0=pb.t[:, 0:256], scalar=rstd,
                                                          in1=self.kvn_row.t[:], op0=ALU.mult, op1=ALU.mult),
                   [pb, ss, self.kvn_row], [tw])
            self.V(lambda: nc.vector.tensor_copy(out=tw.t[:, 256:288], in_=pb.t[:, 256:288]), [pb], [tw])
            self.V(lambda: nc.vector.tensor_copy(out=tb_.t[:, 0:256], in_=tw.t[:, 0:256]), [tw], [tb_])
            if not samp:
                k.dma("sp", o["o_ckv"][ti * 128:(ti + 1) * 128, :], tw.t[:, 0:256], [tw], [])
                k.dma("sp", o["o_kr"][ti * 128:(ti + 1) * 128, :], tw.t[:, 256:288], [tw], [])
                self.V(lambda: nc.vector.tensor_copy(out=tb_.t[:, 256:288], in_=tw.t[:, 256:288]), [tw], [tb_])
            else:
                cs = B["cs"].next()
                rot = B["rot"].next()
                k.dma("sp", cs.t[:, 0, :], d["kcos"][ti * 128:(ti + 1) * 128, :], [], [cs])
                k.dma("sp", cs.t[:, 1, :], d["ksin"][ti * 128:(ti + 1) * 128, :], [], [cs])
                kr2 = tw.t[:, 256:288].rearrange("p (i two) -> p i two", two=2)
                x1, x2 = kr2[:, :, 0], kr2[:, :, 1]
                ob = tb_.t[:, 256:288].rearrange("p (i two) -> p i two", two=2)
                cos_, sin_ = cs.t[:, 0, :], cs.t[:, 1, :]
                self.V(lambda: nc.vector.tensor_tensor(out=rot.t[:, 0, :], in0=x1, in1=cos_, op=ALU.mult), [tw, cs], [rot])
                self.V(lambda: nc.vector.tensor_tensor(out=rot.t[:, 1, :], in0=x2, in1=sin_, op=ALU.mult), [tw, cs], [rot])
                self.V(lambda: nc.vector.tensor_tensor(out=rot.t[:, 2, :], in0=x1, in1=sin_, op=ALU.mult), [tw, cs], [rot])
                self.V(lambda: nc.vector.tensor_tensor(out=rot.t[:, 3, :], in0=x2, in1=cos_, op=ALU.mult), [tw, cs], [rot])
                self.V(lambda: nc.vector.tensor_tensor(out=ob[:, :, 0], in0=rot.t[:, 0, :], in1=rot.t[:, 1, :], op=ALU.subtract), [rot], [tb_])
                self.V(lambda: nc.vector.tensor_tensor(out=ob[:, :, 1], in0=rot.t[:, 2, :], in1=rot.t[:, 3, :], op=ALU.add), [rot], [tb_])

            def post():
                p2 = self.psb()
                pv = p2.t[:].bitcast(BF16).rearrange("p (f t) -> p f t", t=128)
                for kc in range(2):
                    k.op("pe", lambda kc=kc: nc.tensor.transpose(out=pv[:, kc, :], in_=tb_.t[:, kc * 128:(kc + 1) * 128],
                                                                 identity=self.ident_b.t[:]), [tb_, self.ident_b], [p2], sig=False)
                k.op("pe", lambda: nc.tensor.transpose(out=pv[0:32, 2, :], in_=tb_.t[:, 256:288], identity=self.ident_b.t[:]),
                     [tb_, self.ident_b], [p2])
                c0 = ti * 128
                eng = self.evac_engine()
                self.copy(eng, B["ckvT"].t[:, :, c0:c0 + 128], pv[:, 0:2, :], [p2], [B["ckvT"]])
                self.copy(eng, B["krT"].t[0:32, c0:c0 + 128], pv[0:32, 2, :], [p2], [B["krT"]])
            return post
        tokmajor(W, ws, 288, ckv_fn, defer=True)

    def gates_phase(self, graw, gsc, lo, hi):
        nc, k = self.nc, self.k
        n = hi - lo
        G = graw.t[:, lo:hi, :]
        self.V(lambda: nc.vector.tensor_tensor(out=G, in0=G, in1=self.gb_row.t[:].unsqueeze(1).broadcast_to([128, n, 16]), op=ALU.add),
               [graw, self.gb_row], [graw])
        G5 = graw.t[:, lo:hi, :].rearrange("p n (r i h) -> p n r i h", r=2, i=2)
        gf, gi_ = G5[:, :, :, 1, :], G5[:, :, :, 0, :]
        t1b, t2b = self.gtmp.next(), self.gtmp.next()
        t1 = t1b.t[:, 0:n * 8].rearrange("p (n r h) -> p n r h", r=2, h=4)
        t2 = t2b.t[:, 0:n * 8].rearrange("p (n r h) -> p n r h", r=2, h=4)
        k.op("act", lambda: nc.scalar.activation(out=t1, in_=gf, func=AF.Abs), [graw], [t1b])
        k.op("act", lambda: nc.scalar.activation(out=t1, in_=t1, func=AF.Exp, scale=-1.0), [t1b], [t1b])
        k.op("act", lambda: nc.scalar.activation(out=t1, in_=t1, func=AF.Ln, scale=1.0, bias=1.0), [t1b], [t1b])
        self.V(lambda: nc.vector.tensor_single_scalar(out=t2, in_=gf, scalar=0.0, op=ALU.min), [graw], [t2b])
        lf, b, bL, a, amax = gsc["lf"], gsc["b"], gsc["bL"], gsc["a"], gsc["amax"]
        lfv = lf.t[:, lo:hi, :].rearrange("p n (r h) -> p n r h", r=2)
        self.V(lambda: nc.vector.tensor_tensor(out=lfv, in0=t2, in1=t1, op=ALU.subtract), [t1b, t2b], [lf])
        pb = self.psb()
        for r, tri in ((0, self.trif), (1, self.trib)):
            k.op("pe", lambda r=r, tri=tri: nc.tensor.matmul(
                pb.t[:, r * n * 4:(r + 1) * n * 4].rearrange("p (n h) -> p n h", h=4), lhsT=tri.t[:],
                rhs=lf.t[:, lo:hi, r * 4:(r + 1) * 4], start=True, stop=True), [tri, lf], [pb])
            self.V(lambda r=r: nc.vector.tensor_copy(out=b.t[:, lo:hi, r * 4:(r + 1) * 4],
                                                     in_=pb.t[:, r * n * 4:(r + 1) * n * 4].rearrange("p (n h) -> p n h", h=4)), [pb], [b])
        pb2 = self.psb()
        k.op("pe", lambda: nc.tensor.matmul(pb2.t[:, 0:n * 8], lhsT=self.ones_f.t[:], rhs=lf.t[:, lo:hi, :].rearrange("p n j -> p (n j)"),
                                            start=True, stop=True), [self.ones_f, lf], [pb2])
        self.V(lambda: nc.vector.tensor_copy(out=bL.t[:, lo:hi, :].rearrange("p n j -> p (n j)"), in_=pb2.t[:, 0:n * 8]), [pb2], [bL])
        av = a.t[:, lo:hi, :].rearrange("p n (r h) -> p n r h", r=2)
        bv = b.t[:, lo:hi, :].rearrange("p n (r h) -> p n r h", r=2)
        self.V(lambda: nc.vector.tensor_tensor(out=av, in0=gi_, in1=bv, op=ALU.subtract), [graw, b], [a])
        for blo in range(lo, hi, 16):
            nb = min(16, hi - blo)
            cols = nb * 8
            p3 = self.psb()
            k.op("pe", lambda: nc.tensor.transpose(out=p3.t[0:cols, 0:128], in_=a.t[:, blo:blo + nb, :].rearrange("p n j -> p (n j)"),
                                                   identity=self.ident_f.t[:]), [a, self.ident_f], [p3])
            sm = self.smr.next()
            self.V(lambda: nc.vector.tensor_reduce(out=sm.t[0:cols, 0:1], in_=p3.t[0:cols, 0:128], axis=AX.X, op=ALU.max), [p3], [sm])
            dg = self.gtmp.next()
            self.V(lambda: nc.vector.tensor_scalar(out=dg.t[0:cols, 0:cols], in0=self.ident_f.t[0:cols, 0:cols], scalar1=sm.t[0:cols, 0:1],
                                                   scalar2=None, op0=ALU.mult), [self.ident_f, sm], [dg])
            p4 = self.psb()
            k.op("pe", lambda: nc.tensor.matmul(p4.t[:, 0:cols], lhsT=self.ones_f.t[0:cols, :], rhs=dg.t[0:cols, 0:cols],
                                                start=True, stop=True), [self.ones_f, dg], [p4])
            self.V(lambda: nc.vector.tensor_copy(out=amax.t[:, blo:blo + nb, :].rearrange("p n j -> p (n j)"), in_=p4.t[:, 0:cols]),
                   [p4], [amax])

    def gates_scan(self, gsc, order, dr, m_ap, m_bufs, nseq):
        nc = self.nc

        def view(nm, c):
            t = gsc[nm].t
            if nseq == 1:
                return t[:, c:c + 1, dr * 4:dr * 4 + 4]
            return t[:, :, dr * 4:dr * 4 + 4].rearrange("p (s c) j -> p s c j", c=2)[:, :, c, :]
        cur, cb = m_ap, list(m_bufs)
        for c in order:
            ref, dm, mall = view("ref", c), view("dm", c), view("mall", c)
            self.V(lambda: nc.vector.tensor_tensor(out=ref, in0=cur, in1=view("amax", c), op=ALU.max), cb + [gsc["amax"]], [gsc["ref"]])
            self.V(lambda: nc.vector.tensor_tensor(out=dm, in0=cur, in1=ref, op=ALU.subtract), cb + [gsc["ref"]], [gsc["dm"]])
            self.V(lambda: nc.vector.tensor_tensor(out=mall, in0=view("bL", c), in1=ref, op=ALU.add), [gsc["bL"], gsc["ref"]], [gsc["mall"]])
            cur, cb = mall, [gsc["mall"]]
        return cur

    def gates_bulk(self, gsc, lo, hi):
        nc, k = self.nc, self.k
        n = hi - lo
        sl = lambda nm: gsc[nm].t[:, lo:hi, :]
        self.V(lambda: nc.vector.tensor_single_scalar(out=sl("dm"), in_=sl("dm"), scalar=-100.0, op=ALU.max), [gsc["dm"]], [gsc["dm"]])
        k.op("act", lambda: nc.scalar.activation(out=sl("dec"), in_=sl("dm"), func=AF.Exp), [gsc["dm"]], [gsc["dec"]])
        t1b, t2b = self.gtmp.next(), self.gtmp.next()
        t1 = t1b.t[:, 0:n * 8].rearrange("p (n j) -> p n j", j=8)
        t2 = t2b.t[:, 0:n * 8].rearrange("p (n j) -> p n j", j=8)
        self.V(lambda: nc.vector.tensor_tensor(out=t1, in0=sl("a"), in1=sl("ref"), op=ALU.subtract), [gsc["a"], gsc["ref"]], [t1b])
        k.op("act", lambda: nc.scalar.activation(out=sl("u"), in_=t1, func=AF.Exp), [t1b], [gsc["u"]])
        self.V(lambda: nc.vector.tensor_tensor(out=t2, in0=sl("b"), in1=sl("ref"), op=ALU.add), [gsc["b"], gsc["ref"]], [t2b])
        self.V(lambda: nc.vector.tensor_single_scalar(out=t2, in_=t2, scalar=-80.0, op=ALU.max), [t2b], [t2b])
        k.op("act", lambda: nc.scalar.activation(out=sl("E"), in_=t2, func=AF.Exp, scale=-1.0), [t2b], [gsc["E"]])

    def mlstm_state_only(self, gsc, Sb_, ktok, vo, c, ci, h, wk_r):
        nc, k = self.nc, self.k
        j = 4 + h
        u = gsc["u"].t[:, c, j:j + 1]
        dec = gsc["dec"].t[:, c, j:j + 1]
        wk = wk_r.next()
        k.op("act", lambda: nc.scalar.activation(out=wk.t[:], in_=ktok.t[:, ci, h * 128:(h + 1) * 128], func=AF.Copy, scale=u),
             [ktok, gsc["u"]], [wk])
        pb = self.psb()
        k.op("pe", lambda: nc.tensor.matmul(pb.t[:, 0:128], lhsT=wk.t[:], rhs=vo.t[:, ci, h * 128:(h + 1) * 128], start=True, stop=True),
             [wk, vo], [pb], sig=False)
        k.op("pe", lambda: nc.tensor.matmul(pb.t[:, 128:129], lhsT=wk.t[:], rhs=self.ones_b.t[:, 0:1], start=True, stop=True),
             [wk, self.ones_b], [pb])
        self.V(lambda: nc.vector.scalar_tensor_tensor(out=Sb_.t[:], in0=Sb_.t[:], scalar=dec, in1=pb.t[:, 0:129], op0=ALU.mult, op1=ALU.add),
               [Sb_, gsc["dec"], pb], [Sb_])

    def mlstm_sweeps(self, g, gsc, Sres, qT, kT, vaug, hacc, zmT, es, nseq, n_own):
        nc, k = self.nc, self.k
        A = self.alloc
        Sbf = Ring([A(es, g + "Sbf%d" % i, [128, 129], BF16) for i in range(8)])
        qkr = Ring([A(es, g + "qk%d" % i, [128, 128], BF16) for i in range(8)])
        wkr = Ring([A(es, g + "wk%d" % i, [128, 128], BF16) for i in range(8)])
        hnr = Ring([A(es, g + "hn%d" % i, [128, 512], BF16) for i in range(2)])
        jk = A(es, g + "jk", [128, 128], BF16)
        npc = n_own // nseq
        visits = {}
        masks = (self.trif, self.trib)

        def stage_a(ch):
            s_, c, dr, h, pb = ch["s"], ch["c"], ch["dr"], ch["h"], ch["pb"]
            j = dr * 4 + h
            Sb_ = Sres[s_][j]
            cols = slice(c * 128, (c + 1) * 128)
            dec = gsc["dec"].t[:, c, j:j + 1]
            sb = Sbf.next()
            ch["sb"] = sb
            k.op("act", lambda: nc.scalar.activation(out=sb.t[:], in_=Sb_.t[:], func=AF.Copy, scale=dec), [Sb_, gsc["dec"]], [sb])
            pA = pb.t[:, 0:128]
            pC = pb.t[:, 260:324].bitcast(BF16)
            k.op("pe", lambda: nc.tensor.matmul(pA, lhsT=kT.t[:, h, cols], rhs=qT.t[:, h, cols], start=True, stop=True),
                 [kT, qT], [pb], sig=False)
            k.op("pe", lambda: nc.tensor.transpose(out=pC, in_=kT.t[:, h, cols], identity=self.ident_b.t[:]), [kT, self.ident_b], [pb])

        def stage_b(ch):
            s_, c, dr, h, pb = ch["s"], ch["c"], ch["dr"], ch["h"], ch["pb"]
            j = dr * 4 + h
            u = gsc["u"].t[:, c, j:j + 1]
            pA = pb.t[:, 0:128]
            pC = pb.t[:, 260:324].bitcast(BF16)
            qk = qkr.next()
            ch["qk"] = qk
            self.V(lambda: nc.vector.scalar_tensor_tensor(out=qk.t[:], in0=pA, scalar=u, in1=masks[dr].t[:], op0=ALU.mult, op1=ALU.mult),
                   [pb, gsc["u"], masks[dr]], [qk])
            wk = wkr.next()
            ch["wk"] = wk
            k.op("act", lambda: nc.scalar.activation(out=wk.t[:], in_=pC, func=AF.Copy, scale=u), [pb, gsc["u"]], [wk])

        def stage_c(ch):
            s_, c, dr, h, pb = ch["s"], ch["c"], ch["dr"], ch["h"], ch["pb"]
            cols = slice(c * 128, (c + 1) * 128)
            pB = pb.t[:, 128:257]
            pD = pb.t[:, 328:457]
            sb, qk, wk = ch["sb"], ch["qk"], ch["wk"]
            k.op("pe", lambda: nc.tensor.matmul(pB, lhsT=qT.t[:, h, cols], rhs=sb.t[:], start=True, stop=False), [qT, sb], [pb], sig=False)
            k.op("pe", lambda: nc.tensor.matmul(pB, lhsT=qk.t[:], rhs=vaug.t[:, c, h, :], start=False, stop=True), [qk, vaug], [pb], sig=False)
            k.op("pe", lambda: nc.tensor.matmul(pD, lhsT=wk.t[:], rhs=vaug.t[:, c, h, :], start=True, stop=True), [wk, vaug], [pb])
            dcol = self.psum[0].t[:, 460 + ch["ci"]:461 + ch["ci"]]
            k.op("pe", lambda: nc.tensor.matmul(dcol, lhsT=qT.t[:, h, cols], rhs=sb.t[:, 128:129], start=True, stop=False),
                 [qT, sb], [self.psum[0]], sig=False)
            k.op("pe", lambda: nc.tensor.matmul(dcol, lhsT=qk.t[:], rhs=vaug.t[:, c, h, 128:129], start=False, stop=True),
                 [qk, vaug], [self.psum[0]])

        def stage_d0(chains):
            cb_, cf_ = chains[0]["c"], chains[1]["c"]
            sm = self.smr.next()
            den8 = self.psum[0].t[:, 460:468]
            self.V(lambda: nc.vector.tensor_scalar(out=sm.t[:, 8:16], in0=den8, scalar1=-1.0, scalar2=None, op0=ALU.mult), [self.psum[0]], [sm])
            self.V(lambda: nc.vector.tensor_tensor(out=sm.t[:, 0:8], in0=den8, in1=sm.t[:, 8:16], op=ALU.max), [self.psum[0], sm], [sm])
            d2 = sm.t[:, 0:8].rearrange("p (h two) -> p h two", two=2)
            self.V(lambda: nc.vector.tensor_tensor(out=d2[:, :, 0], in0=d2[:, :, 0], in1=gsc["E"].t[:, cb_, 4:8], op=ALU.max), [sm, gsc["E"]], [sm])
            self.V(lambda: nc.vector.tensor_tensor(out=d2[:, :, 1], in0=d2[:, :, 1], in1=gsc["E"].t[:, cf_, 0:4], op=ALU.max), [sm, gsc["E"]], [sm])
            self.V(lambda: nc.vector.reciprocal(out=sm.t[:, 0:8], in_=sm.t[:, 0:8]), [sm], [sm])
            for ch in chains:
                ch["rd"] = sm

        def stage_d(ch):
            s_, c, dr, h, pb = ch["s"], ch["c"], ch["dr"], ch["h"], ch["pb"]
            j = dr * 4 + h
            Sb_ = Sres[s_][j]
            E = gsc["E"].t[:, c, j:j + 1]
            pB = pb.t[:, 128:257]
            pD = pb.t[:, 328:457]
            sm = ch["rd"]
            rd = sm.t[:, ch["ci"]:ch["ci"] + 1]
            hv = hacc.t[:, c, h * 128:(h + 1) * 128]
            if visits.get((c, h), 0) == 0:
                self.V(lambda: nc.vector.tensor_scalar(out=hv, in0=pB[:, 0:128], scalar1=rd, scalar2=None, op0=ALU.mult), [pb, sm], [hacc])
            else:
                self.V(lambda: nc.vector.scalar_tensor_tensor(out=hv, in0=pB[:, 0:128], scalar=rd, in1=hv, op0=ALU.mult, op1=ALU.add),
                       [pb, sm, hacc], [hacc])
            visits[(c, h)] = visits.get((c, h), 0) + 1
            dec = gsc["dec"].t[:, c, j:j + 1]
            self.V(lambda: nc.vector.scalar_tensor_tensor(out=Sb_.t[:], in0=Sb_.t[:], scalar=dec, in1=pD, op0=ALU.mult, op1=ALU.add),
                   [Sb_, gsc["dec"], pb], [Sb_])

        def finalize(c):
            ss = self.smr.next()
            for h in range(4):
                k.op("act", lambda h=h: nc.scalar.activation(out=jk.t[:], in_=hacc.t[:, c, h * 128:(h + 1) * 128], func=AF.Square,
                                                             accum_out=ss.t[:, h:h + 1]), [hacc], [jk, ss])
            k.op("act", lambda: nc.scalar.activation(out=ss.t[:, 4:8], in_=ss.t[:, 0:4], func=AF.Ln, scale=1.0 / 128, bias=EPS), [ss], [ss])
            k.op("act", lambda: nc.scalar.activation(out=ss.t[:, 8:12], in_=ss.t[:, 4:8], func=AF.Exp, scale=-0.5), [ss], [ss])
            hn = hnr.next()
            self.V(lambda: nc.vector.tensor_tensor(out=hn.t[:].rearrange("p (h e) -> p h e", h=4),
                                                   in0=hacc.t[:, c, :].rearrange("p (h e) -> p h e", h=4),
                                                   in1=ss.t[:, 8:12].unsqueeze(2).broadcast_to([128, 4, 128]), op=ALU.mult), [hacc, ss], [hn])
            pb = self.psb()
            pv = pb.t[:].bitcast(BF16).rearrange("p (f t) -> p f t", t=128)
            for h in range(4):
                k.op("pe", lambda h=h: nc.tensor.transpose(out=pv[:, h, :], in_=hn.t[:, h * 128:(h + 1) * 128], identity=self.ident_b.t[:]),
                     [hn, self.ident_b], [pb], sig=(h == 3))
            self.V(lambda: nc.vector.tensor_tensor(out=zmT.t[:, :, c * 128:(c + 1) * 128], in0=pv[:, 0:4, :],
                                                   in1=self.hnT.t[:].unsqueeze(2).broadcast_to([128, 4, 128]), op=ALU.mult),
                   [pb, self.hnT], [zmT])

        for jj in range(npc):
            for s_ in range(nseq):
                cb = s_ * npc + (npc - 1 - jj)
                cf = s_ * npc + jj
                chains = []
                for h in range(4):
                    chains.append(dict(s=s_, c=cb, dr=1, h=h, pb=self.psum[2 * h], ci=2 * h))
                    chains.append(dict(s=s_, c=cf, dr=0, h=h, pb=self.psum[2 * h + 1], ci=2 * h + 1))
                for st in (stage_a, stage_b, stage_c):
                    for ch in chains:
                        st(ch)
                stage_d0(chains)
                for ch in chains:
                    stage_d(ch)
                for c in sorted({cb, cf}):
                    if all(visits.get((c, h), 0) == 2 for h in range(4)):
                        finalize(c)

    def attention(self, g, es, cqnT, ckvT, krT, zaT, n_own, nkey):
        nc, k, d = self.nc, self.k, self.d
        A = self.alloc
        samp = (g == "s")
        Tq = n_own * 128
        Sk = nkey * 128
        w_uqc = A(es, g + "w_uqc", [128, 3, 8, 128], BF16)
        w_ukp = A(es, g + "w_ukp", [128, 2, 8, 128], BF16)
        w_uv = A(es, g + "w_uv", [128, 2, 512], BF16)
        esel = A(es, g + "esel", [32, 128], BF16)
        self.V(lambda: nc.vector.memset(w_ukp.t[:], 0.0), [], [w_ukp])
        for kc in range(3):
            rs = slice(kc * 128, (kc + 1) * 128)
            k.dma("pool", w_uqc.t[:, kc, :, 0:96], d["w_uq"][rs, :].rearrange("p (h n) -> p h n", h=8), [], [w_uqc])
            k.dma("pool", w_uqc.t[:, kc, :, 96:128], d["w_uq_sw"][rs], [], [w_uqc])
        for kc in range(2):
            rs = slice(kc * 128, (kc + 1) * 128)
            k.dma("pool", w_ukp.t[:, kc, :, 0:64], d["w_uk"][rs], [], [w_ukp])
        k.dma("pool", w_uv.t[:], d["w_uv"].rearrange("(k p) n -> p k n", p=128), [], [w_uv])
        k.dma("pool", esel.t[:], d["esel"], [], [esel])
        V_all = A(es, g + "Vall", [128, nkey, 8 * 65 + 64], BF16)
        KhT = Ring([A(es, g + "KhT%d" % i, [128, Sk], BF16) for i in range(2)])
        qhT = Ring([A(es, g + "qhT%d" % i, [128, Tq], BF16) for i in range(2)])
        sqb = A(es, g + "sqb", [128, Sk], BF16)
        sqq = A(es, g + "sqq", [128, Tq], BF16)
        PT = Ring([A(es, g + "PT%d" % i, [128, 512], BF16) for i in range(4)])
        tmpO = Ring([A(es, g + "tmpO%d" % i, [128, 512], F32) for i in range(2)])
        stg = Ring([A(es, g + "stg%d" % i, [64, 512], BF16) for i in range(2)])
        mx = A(es, g + "mx", [1, 32], F32)
        negM = Ring([A(es, g + "negM%d" % i, [128, 1], F32) for i in range(2)])
        psS = Ring(self.psum[0:4])
        psO = Ring(self.psum[4:6])
        psX = Ring(self.psum[6:8])
        self.V(lambda: nc.vector.memset(V_all.t[:], 0.0), [], [V_all])
        self.V(lambda: nc.vector.memset(V_all.t[:, :, 0:520].rearrange("p k (h e) -> p k h e", e=65)[:, :, :, 64:65], 1.0), [], [V_all])
        if samp:
            qtab = A(es, g + "qtab", [128, Tq], F32)
            k.dma("sp", qtab.t[64:128, :], d["qtab"], [], [qtab])
            self.V(lambda: nc.vector.tensor_scalar(out=qtab.t[64:128, :], in0=qtab.t[64:128, :], scalar1=SCQ, scalar2=None, op0=ALU.mult),
                   [qtab], [qtab])
            cw = Ring([A(es, g + "cw%d" % i, [128, 288], F32) for i in range(2)])
            cb_ = Ring([A(es, g + "cb%d" % i, [128, 288], BF16) for i in range(2)])
            for i in range(4):
                w_, b_ = cw.next(), cb_.next()
                k.dma("sp", w_.t[:, 0:256], d["cckv"][i * 128:(i + 1) * 128, :], [], [w_])
                k.dma("sp", w_.t[:, 256:288], d["ckr"][i * 128:(i + 1) * 128, :], [], [w_])
                self.V(lambda: nc.vector.tensor_copy(out=b_.t[:], in_=w_.t[:]), [w_], [b_])
                p2 = psX.next()
                pv = p2.t[:].bitcast(BF16).rearrange("p (f t) -> p f t", t=128)
                for kc in range(2):
                    k.op("pe", lambda kc=kc: nc.tensor.transpose(out=pv[:, kc, :], in_=b_.t[:, kc * 128:(kc + 1) * 128],
                                                                 identity=self.ident_b.t[:]), [b_, self.ident_b], [p2], sig=False)
                k.op("pe", lambda: nc.tensor.transpose(out=pv[0:32, 2, :], in_=b_.t[:, 256:288], identity=self.ident_b.t[:]),
                     [b_, self.ident_b], [p2])
                c0 = (32 + i) * 128
                self.copy("dve", ckvT.t[:, :, c0:c0 + 128], pv[:, 0:2, :], [p2], [ckvT])
                self.copy("dve", krT.t[0:32, c0:c0 + 128], pv[0:32, 2, :], [p2], [krT])
        else:
            qcol = A(es, g + "qcol", [128, 1], F32)
            self.V(lambda: nc.vector.memset(qcol.t[:], 0.0), [], [qcol])
            self.V(lambda: nc.vector.memset(qcol.t[64:96, :], SCQ), [], [qcol])
        for kt in range(nkey):
            pb = psX.next()
            self.mm_group(pb.t[:, :], pb, [(ckvT.t[:, kc, kt * 128:(kt + 1) * 128], w_uv.t[:, kc, :]) for kc in range(2)], [ckvT, w_uv])
            self.copy("dve" if kt % 2 else "act", V_all.t[:, kt, 0:520].rearrange("p (h e) -> p h e", e=65)[:, :, 0:64],
                      pb.t[:, :].rearrange("p (h e) -> p h e", h=8), [pb], [V_all])
        if samp:
            units = [(qb * 512, 512, list(range(nkey))) for qb in range(Tq // 512)]
        else:
            units = [(s_ * 256, 256, [2 * s_, 2 * s_ + 1]) for s_ in range(4)]
        heads = {}

        def build(h):
            kh, qh = KhT.next(), qhT.next()
            heads[h] = dict(kh=kh, qh=qh)
            return [lambda c0=c0: build_k(h, c0) for c0 in range(0, Sk, 512)] + [lambda c0=c0: build_q(h, c0) for c0 in range(0, Tq, 512)]

        def build_k(h, c0):
            kh = heads[h]["kh"]
            if True:
                pb = psX.next()
                self.mm_group(pb.t[:, :], pb, [(esel.t[:, :], krT.t[0:32, c0:c0 + 512])] +
                              [(w_ukp.t[:, kc, h, :], ckvT.t[:, kc, c0:c0 + 512]) for kc in range(2)], [esel, krT, w_ukp, ckvT])
                self.copy("dve", kh.t[:, c0:c0 + 512], pb.t[:, :], [pb], [kh])

        def build_q(h, c0):
            qh = heads[h]["qh"]
            if True:
                pb = psX.next()
                self.mm_group(pb.t[:, :], pb, [(w_uqc.t[:, kc, h, :], cqnT.t[:, kc, c0:c0 + 512]) for kc in range(3)], [w_uqc, cqnT])
                self.V(lambda: nc.vector.tensor_scalar(out=qh.t[0:64, c0:c0 + 512], in0=pb.t[0:64, :], scalar1=SCQ, scalar2=None, op0=ALU.mult),
                       [pb], [qh])
                if samp:
                    self.V(lambda: nc.vector.tensor_tensor(out=qh.t[64:128, c0:c0 + 512], in0=pb.t[64:128, :], in1=qtab.t[64:128, c0:c0 + 512],
                                                           op=ALU.mult), [pb, qtab], [qh])
                else:
                    self.V(lambda: nc.vector.tensor_scalar(out=qh.t[64:128, c0:c0 + 512], in0=pb.t[64:128, :], scalar1=qcol.t[64:128, 0:1],
                                                           scalar2=None, op0=ALU.mult), [pb, qcol], [qh])

        def bound(h):
            kh, qh = heads[h]["kh"], heads[h]["qh"]
            nq_blk = Tq // 512
            nk_blk = Sk // 512
            pieces = []

            def sq_piece(src, n, off):
                sq = sqq if src is qh else sqb
                if samp:
                    k.op("pool", lambda: nc.gpsimd.tensor_tensor(out=sq.t[:, 0:n], in0=src.t[:, 0:n], in1=src.t[:, 0:n], op=ALU.mult), [src], [sq])
                else:
                    k.op("act", lambda: nc.scalar.activation(out=sq.t[:, 0:n], in_=src.t[:, 0:n], func=AF.Square), [src], [sq])

            def ones_piece(sq, c0, col):
                pb = psX.next()
                k.op("pe", lambda: nc.tensor.matmul(pb.t[0:1, :], lhsT=self.ones_b.t[:, 0:1], rhs=sq.t[:, c0:c0 + 512],
                                                    start=True, stop=True), [self.ones_b, sq], [pb])
                self.V(lambda: nc.vector.tensor_reduce(out=mx.t[0:1, col:col + 1], in_=pb.t[0:1, :], axis=AX.X, op=ALU.max), [pb], [mx])
            pieces.append(lambda: (sq_piece(qh, Tq, 0), sq_piece(kh, Sk, 0)))
            col = 0
            for sq, n in ((sqq, Tq), (sqb, Sk)):
                for c0 in range(0, n, 512):
                    pieces.append(lambda sq=sq, c0=c0, col=col: ones_piece(sq, c0, col))
                    col += 1
            pieces.append(lambda: bound_final(h, nq_blk, nk_blk))
            return pieces

        def bound_final(h, nq_blk, nk_blk):
            self.V(lambda: nc.vector.tensor_reduce(out=mx.t[0:1, 24:25], in_=mx.t[0:1, 0:nq_blk], axis=AX.X, op=ALU.max), [mx], [mx])
            self.V(lambda: nc.vector.tensor_reduce(out=mx.t[0:1, 25:26], in_=mx.t[0:1, nq_blk:nq_blk + nk_blk], axis=AX.X, op=ALU.max), [mx], [mx])
            self.V(lambda: nc.vector.tensor_tensor(out=mx.t[0:1, 26:27], in0=mx.t[0:1, 24:25], in1=mx.t[0:1, 25:26], op=ALU.mult), [mx], [mx])
            k.op("act", lambda: nc.scalar.activation(out=mx.t[0:1, 27:28], in_=mx.t[0:1, 26:27], func=AF.Ln, bias=1e-30), [mx], [mx])
            k.op("act", lambda: nc.scalar.activation(out=mx.t[0:1, 28:29], in_=mx.t[0:1, 27:28], func=AF.Exp, scale=0.5), [mx], [mx])
            pb = psX.next()
            k.op("pe", lambda: nc.tensor.matmul(pb.t[:, 0:1], lhsT=self.ones_f.t[0:1, :], rhs=mx.t[0:1, 28:29], start=True, stop=True),
                 [self.ones_f, mx], [pb])
            nm = negM.next()
            heads[h]["nm"] = nm
            self.V(lambda: nc.vector.tensor_scalar(out=nm.t[:], in0=pb.t[:, 0:1], scalar1=-1.02, scalar2=None, op0=ALU.mult), [pb], [nm])

        def main_unit(h, unit):
            kh, qh, nm = heads[h]["kh"], heads[h]["qh"], heads[h]["nm"]
            (q0, nq, kts) = unit
            pO = psO.next()
            nk = len(kts)

            def smat(ki):
                pS = psS.next()
                kt = kts[ki]
                k.op("pe", lambda: nc.tensor.matmul(pS.t[:, 0:nq], lhsT=kh.t[:, kt * 128:(kt + 1) * 128], rhs=qh.t[:, q0:q0 + nq],
                                                    start=True, stop=True), [kh, qh], [pS])
                pt = PT.next()
                k.op("act", lambda: nc.scalar.activation(out=pt.t[:, 0:nq], in_=pS.t[:, 0:nq], func=AF.Exp, bias=nm.t[:, 0:1], scale=1.0),
                     [pS, nm], [pt])
                return pt

            def pv(ki, pt):
                kt = kts[ki]
                k.op("pe", lambda: nc.tensor.matmul(pO.t[:, 0:nq], lhsT=V_all.t[:, kt, h * 65:h * 65 + 128], rhs=pt.t[:, 0:nq],
                                                    start=(ki == 0), stop=(ki == nk - 1)), [V_all, pt], [pO], sig=(ki == nk - 1))
            LA = 3
            pts = [smat(i_) for i_ in range(min(LA, nk))]
            for ki in range(nk):
                if ki + LA < nk:
                    pts.append(smat(ki + LA))
                pv(ki, pts[ki])
                if ki == min(5, nk - 1) and pending:
                    pending.pop(0)()
                if ki % 4 == 1 and ki >= 5 and work:
                    work.pop(0)()
            to = tmpO.next()
            if samp:
                self.V(lambda: nc.vector.tensor_copy(out=to.t[0:65, 0:nq], in_=pO.t[0:65, 0:nq]), [pO], [to])
                self.V(lambda: nc.vector.reciprocal(out=to.t[64:65, 0:nq], in_=to.t[64:65, 0:nq]), [to], [to])
            else:
                k.op("act", lambda: nc.scalar.copy(out=to.t[0:65, 0:nq], in_=pO.t[0:65, 0:nq]), [pO], [to])
                k.op("act", lambda: nc.scalar.activation(out=to.t[64:65, 0:nq], in_=to.t[64:65, 0:nq], func=AF.Ln), [to], [to])
                k.op("act", lambda: nc.scalar.activation(out=to.t[64:65, 0:nq], in_=to.t[64:65, 0:nq], func=AF.Exp, scale=-1.0), [to], [to])
            pending.append(lambda: fin(h, q0, nq, to))

        def fin(h, q0, nq, to):
            pR = psX.next()
            k.op("pe", lambda: nc.tensor.matmul(pR.t[0:64, 0:nq], lhsT=self.ones_f.t[64:65, 0:64], rhs=to.t[64:65, 0:nq],
                                                start=True, stop=True), [self.ones_f, to], [pR])
            if h % 2 == 0:
                self.V(lambda: nc.vector.tensor_tensor(out=zaT.t[0:64, h // 2, q0:q0 + nq], in0=to.t[0:64, 0:nq], in1=pR.t[0:64, 0:nq],
                                                       op=ALU.mult), [to, pR], [zaT])
            else:
                sg = stg.next()
                self.V(lambda: nc.vector.tensor_tensor(out=sg.t[0:64, 0:nq], in0=to.t[0:64, 0:nq], in1=pR.t[0:64, 0:nq], op=ALU.mult),
                       [to, pR], [sg])
                k.dma("sp", zaT.t[64:128, h // 2, q0:q0 + nq], sg.t[0:64, 0:nq], [sg], [zaT])

        pending = []
        work = []
        for p_ in build(0) + bound(0):
            p_()
        for h in range(8):
            if h + 1 < 8:
                work += build(h + 1)
                work += bound(h + 1)
            for ui, unit in enumerate(units):
                main_unit(h, unit)
                if not samp:
                    for _ in range(3):
                        if work:
                            work.pop(0)()
            while work:
                work.pop(0)()
        while pending:
            pending.pop(0)()

    def join_mlp(self, g, gi, x_dram, zmT, zaT, n_own):
        nc, k, d, o = self.nc, self.k, self.d, self.o
        A = self.alloc
        samp = (g == "s")
        y_dram = o["ys"] if samp else o["yp"]
        w = d["w_in"]
        with ExitStack() as js:
            if samp:
                G1, G2 = self.Gs
                glist = []
            else:
                G1 = A(js, g + "G1", [128, D], F32)
                G2 = A(js, g + "G2", [128, D], F32)
                glist = [(1, G1, G2), (0, self.Gs[0], self.Gs[1])]
            x1 = tiled(A(js, g + "x1", [128, 8, D], F32), 8)
            self.wring = Ring([A(js, g + "w5_%d" % i, [128, 8 * 512], BF16) for i in range(4)])
            self.xnb = Ring([A(js, g + "xnb5%d" % i, [128, D], BF16) for i in range(2)])
            self.junk = Ring([A(js, g + "junk5%d" % i, [128, 512], BF16) for i in range(1)])
            self.evt = Ring([A(js, g + "evt5%d" % i, [128, 8, 128], F32) for i in range(1)])
            tmpf = Ring([A(js, g + "tmpf%d" % i, [128, 512], F32) for i in range(3 if samp else 5)])
            tmpb = Ring([A(js, g + "tmpb%d" % i, [128, 512], BF16) for i in range(2)])

            def load_x(tb):
                for i in range(8):
                    r0 = (tb * 8 + i) * 128
                    k.dma("sp", x1.t[:, i, :], x_dram[r0:r0 + 128, :], [], [x1.tiles[i]])
            load_x(0)
            if glist:
                reps = {gi_: A(js, g + "rep%d" % gi_, [128, 8, 128], BF16) for (gi_, _, _) in glist}

            def g_begin():
                for (gi_, _, _) in glist:
                    for kc in range(8):
                        self.V(lambda kc=kc, gi_=gi_: nc.vector.tensor_scalar(out=reps[gi_].t[:, kc, :], in0=self.ones_b.t[:],
                                                                               scalar1=self.sT.t[:, kc, gi_:gi_ + 1], scalar2=None, op0=ALU.mult),
                               [self.ones_b, self.sT], [reps[gi_]])
                self.wstream([(d["w_ada"][:, mi * D + j * 512: mi * D + (j + 1) * 512], 8, 512) for mi in (2, 5) for j in range(2)])
                self.wget(-1)

            def g_finish():
                wi = 0
                for idx, (mi, post) in enumerate(((2, "post1"), (5, "post2"))):
                    for j in range(2):
                        W, ws = self.wget(wi); wi += 1
                        t1, t2 = tmpf.next(), tmpf.next()
                        k.dma("sp", t1.t[:], d["b_ada"][mi * D + j * 512: mi * D + (j + 1) * 512].partition_broadcast(128), [], [t1])
                        k.dma("sp", t2.t[:], d[post][j * 512:(j + 1) * 512].partition_broadcast(128), [], [t2])
                        for (gi_, G1_, G2_) in glist:
                            G = (G1_, G2_)[idx]
                            pb = self.psb()
                            self.mm_group(pb.t[:, :], pb, [(reps[gi_].t[:, kc, :], W[:, kc, :]) for kc in range(8)], [reps[gi_], ws])
                            t3 = tmpf.next()
                            self.V(lambda: nc.vector.tensor_tensor(out=t3.t[:], in0=pb.t[:, :], in1=t1.t[:], op=ALU.add), [pb, t1], [t3])
                            self.V(lambda: nc.vector.tensor_tensor(out=G.t[:, j * 512:(j + 1) * 512], in0=t3.t[:], in1=t2.t[:], op=ALU.mult), [t3, t2], [G])
            for tb in range(n_own // 8):
                with ExitStack() as s1:
                    hT = tiled(A(s1, g + "hT5", [128, 8, 1024], BF16), 8)
                    Wm = A(s1, g + "Wm", [128, 4, D], BF16)
                    Wa = A(s1, g + "Wa", [128, 4, D], BF16)
                    mT = A(s1, g + "mT", [128, 8, 1024], BF16)
                    tmpj = Ring(tmpf.b + [A(s1, g + "tmpj%d" % i, [128, 512], F32) for i in range(3)])
                    if tb > 0:
                        load_x(tb)
                    k.dma("pool", Wm.t[:], d["w_mlstm_o"].rearrange("(k p) n -> p k n", p=128), [], [Wm])
                    k.dma("pool", Wa.t[:], d["w_mla_o"].rearrange("(k p) n -> p k n", p=128), [], [Wa])
                    reqs = [(w[:, C_O:C_O + 512], 8, 512)]
                    for fg in range(2):
                        reqs += [(w[:, C_MG + fg * 512:C_MG + (fg + 1) * 512], 8, 512),
                                 (w[:, C_MG + D + fg * 512:C_MG + D + (fg + 1) * 512], 8, 512)]
                    reqs += [(d["w_out"][:, 0:512], 8, 512), (d["w_out"][:, 512:1024], 8, 512)]
                    if tb == 0 and glist:
                        g_begin()
                        for i in range(8):
                            self.norm_transpose(x1.t[:, i, :], x1.tiles[i], gi, 0, hT, i * 128)
                        g_finish()
                        self.wstream(reqs)
                        self.wget(-1)
                    else:
                        self.wstream(reqs)
                        self.wget(-1)
                        for i in range(8):
                            self.norm_transpose(x1.t[:, i, :], x1.tiles[i], gi, 0, hT, i * 128)
                    W, ws = self.wget(0)
                    for h in range(4):
                        for half in range(2):
                            pb = self.psb()
                            self.mm_group(pb.t[:, :], pb, [(W[:, kc, h * 128:(h + 1) * 128], hT.t[:, kc, half * 512:(half + 1) * 512])
                                                           for kc in range(8)], [ws] + hT.tiles[half * 4:half * 4 + 4])
                            e_ = tmpj.next()
                            k.op("act", lambda: nc.scalar.activation(out=e_.t[:], in_=pb.t[:, :], func=AF.Exp, scale=-1.0), [pb], [e_])
                            k.op("act", lambda: nc.scalar.activation(out=e_.t[:], in_=e_.t[:], func=AF.Ln, bias=1.0), [e_], [e_])
                            k.op("act", lambda: nc.scalar.activation(out=e_.t[:], in_=e_.t[:], func=AF.Exp, scale=-1.0), [e_], [e_])
                            c0 = tb * 1024 + half * 512
                            self.V(lambda: nc.vector.tensor_tensor(out=zmT.t[:, h, c0:c0 + 512], in0=zmT.t[:, h, c0:c0 + 512], in1=e_.t[:],
                                                                   op=ALU.mult), [zmT, e_], [zmT])
                    for fg in range(2):
                        Wga, wsa = self.wget(1 + 2 * fg)
                        Wgb, wsb = self.wget(2 + 2 * fg)
                        for c4 in range(4):
                            fc = fg * 4 + c4
                            for half in range(2):
                                c0 = tb * 1024 + half * 512
                                hc = slice(half * 512, (half + 1) * 512)
                                pM, pA, pGa, pGb = self.psb(), self.psb(), self.psb(), self.psb()
                                self.mm_group(pM.t[:, :], pM, [(Wm.t[:, kc, fc * 128:(fc + 1) * 128], zmT.t[:, kc, c0:c0 + 512]) for kc in range(4)], [Wm, zmT])
                                self.mm_group(pA.t[:, :], pA, [(Wa.t[:, kc, fc * 128:(fc + 1) * 128], zaT.t[:, kc, c0:c0 + 512]) for kc in range(4)], [Wa, zaT])
                                self.mm_group(pGa.t[:, :], pGa, [(Wga[:, kc, c4 * 128:(c4 + 1) * 128], hT.t[:, kc, hc]) for kc in range(8)], [wsa] + hT.tiles[half * 4:half * 4 + 4])
                                self.mm_group(pGb.t[:, :], pGb, [(Wgb[:, kc, c4 * 128:(c4 + 1) * 128], hT.t[:, kc, hc]) for kc in range(8)], [wsb] + hT.tiles[half * 4:half * 4 + 4])
                                ea, eb = tmpj.next(), tmpj.next()
                                k.op("act", lambda: nc.scalar.activation(out=ea.t[:], in_=pGa.t[:, :], func=AF.Exp, scale=-1.0), [pGa], [ea])
                                k.op("act", lambda: nc.scalar.activation(out=eb.t[:], in_=pGb.t[:, :], func=AF.Exp, scale=-1.0), [pGb], [eb])
                                k.op("act", lambda: nc.scalar.activation(out=ea.t[:], in_=ea.t[:], func=AF.Ln, bias=1.0), [ea], [ea])
                                k.op("act", lambda: nc.scalar.activation(out=ea.t[:], in_=ea.t[:], func=AF.Exp, scale=-1.0), [ea], [ea])
                                k.op("act", lambda: nc.scalar.activation(out=eb.t[:], in_=eb.t[:], func=AF.Ln, bias=1.0), [eb], [eb])
                                k.op("act", lambda: nc.scalar.activation(out=eb.t[:], in_=eb.t[:], func=AF.Exp, scale=-1.0), [eb], [eb])
                                self.V(lambda: nc.vector.tensor_tensor(out=ea.t[:], in0=pM.t[:, :], in1=ea.t[:], op=ALU.mult), [pM, ea], [ea])
                                self.V(lambda: nc.vector.tensor_tensor(out=eb.t[:], in0=pA.t[:, :], in1=eb.t[:], op=ALU.mult), [pA, eb], [eb])
                                self.V(lambda: nc.vector.tensor_tensor(out=mT.t[:, fc, hc], in0=ea.t[:], in1=eb.t[:], op=ALU.add), [ea, eb], [mT])
                    W0, ws0 = self.wget(5)
                    W1, ws1 = self.wget(6)
                    for i in range(8):
                        p0, p1 = self.psb(), self.psb()
                        self.mm_group(p0.t[:, :], p0, [(mT.t[:, kc, i * 128:(i + 1) * 128], W0[:, kc, :]) for kc in range(8)], [mT, ws0])
                        self.mm_group(p1.t[:, :], p1, [(mT.t[:, kc, i * 128:(i + 1) * 128], W1[:, kc, :]) for kc in range(8)], [mT, ws1])
                        self.post_residual(x1, i, [p0.t[:, :], p1.t[:, :]], [p0, p1], G1, tmpf)
                with ExitStack() as s2:
                    h2T = tiled(A(s2, g + "h2T", [128, 8, 1024], BF16), 8)
                    ff0b = [Buf(None) for _ in range(8)]
                    aT = A(s2, g + "aT", [128, 32, 1024], BF16)
                    self.wstream([(d["w_mlp1"][:, t * 512:(t + 1) * 512], 8, 512) for t in range(8)] +
                                 [(d["w_mlp2"][gq * 1024:(gq + 1) * 1024, j * 512:(j + 1) * 512], 8, 512) for j in range(2) for gq in range(4)])
                    self.wget(-1)
                    for i in range(8):
                        self.norm_transpose(x1.t[:, i, :], x1.tiles[i], gi, 1, h2T, i * 128)
                    def mlp1_blk(t, W, ws, half):
                        hc = slice(half * 512, (half + 1) * 512)
                        for c4 in range(4):
                            pb = self.psb()
                            self.mm_group(pb.t[:, :], pb, [(W[:, kc, c4 * 128:(c4 + 1) * 128], h2T.t[:, kc, hc]) for kc in range(8)],
                                          [ws] + h2T.tiles[half * 4:half * 4 + 4])
                            dst = aT.t[:, t * 4 + c4, hc]
                            r_ = tmpb.next()
                            k.op("act", lambda: nc.scalar.activation(out=r_.t[:], in_=pb.t[:, :], func=AF.Relu), [pb], [r_])
                            self.V(lambda: nc.vector.tensor_tensor(out=dst, in0=r_.t[:], in1=r_.t[:], op=ALU.mult), [r_], [aT])
                    w01 = [self.wget(0), self.wget(1)]
                    for half in range(2):
                        for t in range(2):
                            mlp1_blk(t, w01[t][0], w01[t][1], half)
                    for t in range(2, 8):
                        W, ws = self.wget(t)
                        for half in range(2):
                            mlp1_blk(t, W, ws, half)
                    ff0 = h2T.t[:].rearrange("p a b -> p (a b)").bitcast(F32).rearrange("p (i c) -> p i c", c=512)
                    for j in range(2):
                        for gq in range(4):
                            W, ws = self.wget(8 + j * 4 + gq)
                            for i in range(8):
                                pb = self.psum[i]
                                for kc in range(8):
                                    first = (gq == 0 and kc == 0)
                                    last = (gq == 3 and kc == 7)
                                    k.op("pe", lambda kc=kc, first=first, last=last: nc.tensor.matmul(
                                        pb.t[:, :], lhsT=aT.t[:, gq * 8 + kc, i * 128:(i + 1) * 128], rhs=W[:, kc, :], start=first, stop=last),
                                        [aT, ws], [pb], sig=(kc == 7))
                        if j == 0:
                            for i in range(8):
                                self.copy("act" if i % 2 else "dve", ff0[:, i, :], self.psum[i].t[:, :], [self.psum[i]], [ff0b[i]] + h2T.tiles)
                        else:
                            for i in range(8):
                                self.post_residual(x1, i, [ff0[:, i, :], self.psum[i].t[:, :]], [ff0b[i], self.psum[i]], G2, tmpf)
                                r0 = (tb * 8 + i) * 128
                                k.dma("sp", y_dram[r0:r0 + 128, :], x1.t[:, i, :], [x1.tiles[i]], [])
                    k.barrier()

    def post_residual(self, x1, i, halves, hbufs, G, tmpf):
        nc, k = self.nc, self.k
        ss = self.smr.next()
        junk = self.junk.next()
        for j in range(2):
            k.op("act", lambda j=j: nc.scalar.activation(out=junk.t[:, 0:512], in_=halves[j], func=AF.Square,
                                                         accum_out=ss.t[:, 4 + j:5 + j]), [hbufs[j]], [junk, ss])
        self.V(lambda: nc.vector.tensor_tensor(out=ss.t[:, 0:1], in0=ss.t[:, 4:5], in1=ss.t[:, 5:6], op=ALU.add), [ss], [ss])
        rstd = self.rstd_from_ss(ss, 0, D, [])
        for j in range(2):
            t_ = tmpf.next()
            self.V(lambda j=j: nc.vector.scalar_tensor_tensor(out=t_.t[:], in0=halves[j], scalar=rstd, in1=G.t[:, j * 512:(j + 1) * 512],
                                                              op0=ALU.mult, op1=ALU.mult), [hbufs[j], ss, G], [t_])
            self.V(lambda j=j: nc.vector.tensor_tensor(out=x1.t[:, i, j * 512:(j + 1) * 512], in0=x1.t[:, i, j * 512:(j + 1) * 512],
                                                       in1=t_.t[:], op=ALU.add), [x1.tiles[i], t_], [x1.tiles[i]])


_CACHE = {}


def _rope_tables(rev):
    t = np.arange(NST)
    pos = (NST - 1 - t) if rev else t
    row = (pos // 64).astype(np.float32)
    col = (pos % 64).astype(np.float32)
    inv = (10000.0 ** (-np.arange(0, 16, 2, dtype=np.float32) / 16)).astype(np.float32)
    ang = np.concatenate([row[:, None] * inv, col[:, None] * inv], axis=-1).astype(np.float32)
    cos, sin = np.cos(ang).astype(np.float32), np.sin(ang).astype(np.float32)
    qtab = np.zeros((64, NSO), np.float32)
    qtab[0:32:2] = cos[:NSO].T
    qtab[1:32:2] = cos[:NSO].T
    qtab[32:64:2] = -sin[:NSO].T
    qtab[33:64:2] = sin[:NSO].T
    return cos, sin, qtab


def _consts():
    ident = np.eye(128, dtype=np.float32)
    s_, t_ = np.meshgrid(np.arange(128), np.arange(128), indexing="ij")
    trif = (s_ <= t_).astype(np.float32)
    trib = (s_ >= t_).astype(np.float32)
    esel = np.zeros((32, 128), np.float32)
    esel[np.arange(32), 64 + np.arange(32)] = 1.0
    esel[np.arange(32), 96 + np.arange(32)] = 1.0
    return ident, trif, trib, esel


def make_in_maps(inp):
    f = lambda a: np.ascontiguousarray(np.asarray(a, dtype=np.float32))
    ident, trif, trib, esel = _consts()
    w_in = f(inp["w_in"][0])
    w_in_sw = w_in.copy()
    gsw = w_in[:, C_G:C_G + 16].reshape(D, 2, 8)[:, ::-1, :].reshape(D, 16)
    w_in_sw[:, C_G:C_G + 16] = gsw
    gate_b = f(inp["mlstm_gate_b"][0]).reshape(2, 8)
    w_uq = f(inp["w_uq"][0]).reshape(384, 8, 96)
    sw = w_uq[:, :, 64:96].reshape(384, 8, 16, 2)[:, :, :, ::-1].reshape(384, 8, 32)
    w_ukv = f(inp["w_ukv"][0])
    vecs = np.concatenate([f(inp["b_ada"][0]).reshape(48, 128), f(inp["norm_pre1"][0]).reshape(8, 128),
                           f(inp["norm_pre2"][0]).reshape(8, 128), f(inp["mlstm_head_norm"][0]).reshape(4, 128)], axis=0)
    common = dict(
        w_ada=f(inp["w_ada"][0]), vecs=f(vecs), b_ada=f(inp["b_ada"][0]), post1=f(inp["norm_post1"][0]), post2=f(inp["norm_post2"][0]),
        q_norm=f(inp["mla_q_norm"][0]), kv_norm=f(inp["mla_kv_norm"][0]),
        w_uq=f(w_uq.reshape(384, 768)), w_uq_sw=f(sw), w_uk=f(w_ukv[:, :, 0:64]), w_uv=f(w_ukv[:, :, 64:128].reshape(256, 512)),
        w_mla_o=f(inp["w_mla_o"][0]), w_mlstm_o=f(inp["w_mlstm_o"][0]), w_out=f(inp["w_out"][0]),
        w_mlp1=f(inp["w_mlp1"][0]), w_mlp2=f(inp["w_mlp2"][0]), ident=ident, trif=trif, trib=trib, esel=esel,
    )
    tabs = {False: _rope_tables(False), True: _rope_tables(True)}
    maps = []
    for r in range(8):
        b, rev = r // 2, (r % 2 == 1)
        xs = np.asarray(inp["x_sample"][b], np.float32)
        xp = np.asarray(inp["x_prompt"][4 * r:4 * r + 4], np.float32)
        C0 = np.asarray(inp["state_mlstm_C"][b, 0], np.float32)
        n0 = np.asarray(inp["state_mlstm_n"][b, 0], np.float32)
        m0 = np.asarray(inp["state_mlstm_m"][b, 0], np.float32)
        if rev:
            xs, xp, C0, n0, m0 = xs[::-1], xp[:, ::-1], C0[::-1], n0[::-1], m0[::-1]
        cos, sin, qtab = tabs[rev]
        m = dict(common)
        m.update(
            xs=f(xs), xp=f(xp.reshape(NTP, D)), cckv=f(inp["cache_mla_ckv"][b, 0]), ckr=f(inp["cache_mla_krope"][b, 0]),
            C0=f(C0), n0=f(n0), m0=f(m0.reshape(8)), cvec=f(np.stack([np.asarray(inp["c"][b]), np.asarray(inp["c_ctx"])])),
            w_in=(w_in_sw if rev else w_in), gate_b=f((gate_b[::-1] if rev else gate_b).reshape(16)),
            qtab=f(qtab), kcos=f(cos), ksin=f(sin),
        )
        maps.append(m)
    return maps


def assemble(results):
    y_p = np.zeros((32, 256, D), np.float32)
    y_s = np.zeros((4, 4096, D), np.float32)
    ckv = np.zeros((32, 1, 256, 256), np.float32)
    kr = np.zeros((32, 1, 256, 32), np.float32)
    Cn = np.zeros((32, 1, 2, 4, 128, 128), np.float32)
    nn = np.zeros((32, 1, 2, 4, 128), np.float32)
    mm = np.zeros((32, 1, 2, 4), np.float32)
    for r, res in enumerate(results):
        b, rev = r // 2, (r % 2 == 1)
        yp = np.asarray(res["yp"]).reshape(4, 256, D)
        ock = np.asarray(res["o_ckv"]).reshape(4, 256, 256)
        okr = np.asarray(res["o_kr"]).reshape(4, 256, 32)
        oC = np.asarray(res["o_C"]).reshape(4, 2, 4, 128, 128)
        on = np.asarray(res["o_n"]).reshape(4, 2, 4, 128)
        om = np.asarray(res["o_m"]).reshape(4, 2, 4)
        ys = np.asarray(res["ys"])
        if rev:
            yp, ock, okr, oC, on, om, ys = yp[:, ::-1], ock[:, ::-1], okr[:, ::-1], oC[:, ::-1], on[:, ::-1], om[:, ::-1], ys[::-1]
            y_s[b, 2048:] = ys
        else:
            y_s[b, :2048] = ys
        sl = slice(4 * r, 4 * r + 4)
        y_p[sl], ckv[sl, 0], kr[sl, 0], Cn[sl, 0], nn[sl, 0], mm[sl, 0] = yp, ock, okr, oC, on, om
    return (y_p, y_s, ckv, kr, Cn, nn, mm)


def kernel(**inputs):
    if "nc" not in _CACHE:
        _CACHE["nc"] = Prog().build()
    maps = make_in_maps(inputs)
    res = run_bass_kernel_spmd(_CACHE["nc"], maps, core_ids=list(range(8)))
    return assemble(res.results)
```
